# Optimizing a Trainium2 kernel written in Bass

```python
import math
import jax, jax.numpy as jnp
from jax import lax
import numpy as np

D_MODEL = 1024
BATCH = 8
SEQ = 2048
DEPTH = 2
DEC_BATCH = 128
DEC_SEQ = 4
PAST_LEN = 16384
PAGE_SIZE = 128

S5_WIDTH = D_MODEL // 4
S5_GROUP = 16
S5_N_GROUPS = S5_WIDTH // S5_GROUP
S5_STATE = 64
LRU_WIDTH = D_MODEL // 4
LRU_BLOCKS = 8
LRU_BLOCK = LRU_WIDTH // LRU_BLOCKS
LRU_C = 8.0
CONV_WIDTH = 4
GDN_WIDTH = D_MODEL // 2
GDN_HEAD_DIM = 128
GDN_HEADS = GDN_WIDTH // GDN_HEAD_DIM
GDN_CHUNK = 64
MIX_WIDTH = S5_WIDTH + LRU_WIDTH + GDN_WIDTH
IN_SPLITS = (S5_WIDTH, LRU_WIDTH, LRU_WIDTH, 3 * GDN_WIDTH, GDN_HEADS, GDN_HEADS, GDN_WIDTH)
IN_COLS = S5_WIDTH + 2 * LRU_WIDTH + 4 * GDN_WIDTH + 2 * GDN_HEADS
FFN_HIDDEN = ((8 * D_MODEL // 3 + 255) // 256) * 256
NORM_EPS = 1e-6

kernel_name = "hybrid_s5_rglru_gdn_decode_step"


def rms_norm(x, w):
    xf = x.astype(jnp.float32)
    y = xf * lax.rsqrt(jnp.mean(xf * xf, axis=-1, keepdims=True) + NORM_EPS)
    return (y * w.astype(jnp.float32)).astype(x.dtype)


def l2_normalize(x):
    xf = x.astype(jnp.float32)
    return xf * lax.rsqrt(jnp.sum(xf * xf, axis=-1, keepdims=True) + NORM_EPS)


def causal_conv(x, buf, w):
    L = x.shape[1]
    xp = jnp.concatenate([buf.astype(x.dtype), x], axis=1)
    y = xp[:, 0:L] * w[0]
    for j in range(1, w.shape[0]):
        y = y + xp[:, j:j + L] * w[j]
    return y, xp[:, L:]


def _cmul(ar, ai, br, bi):
    return ar * br - ai * bi, ar * bi + ai * br


def _s5_combine(e1, e2):
    a1r, a1i, b1r, b1i = e1
    a2r, a2i, b2r, b2i = e2
    ar, ai = _cmul(a2r, a2i, a1r, a1i)
    br, bi = _cmul(a2r, a2i, b1r, b1i)
    return ar, ai, br + b2r, bi + b2i


def _lru_combine(e1, e2):
    a1, b1 = e1
    a2, b2 = e2
    return a1 * a2, a2 * b1 + b2


def s5_mixer(u, h0_re, h0_im, lam_re, lam_im, log_dt, b_re, b_im, c_re, c_im, d, w_glu, b_glu):
    f32 = jnp.float32
    Bsz, L, _ = u.shape
    uf = u.astype(f32)
    ug = uf.reshape(Bsz, L, S5_N_GROUPS, S5_GROUP)
    lr = jnp.minimum(lam_re.astype(f32), -1e-4)
    li = lam_im.astype(f32)
    dt = jnp.exp(log_dt.astype(f32))[:, None]
    mag = jnp.exp(lr * dt)
    ar, ai = mag * jnp.cos(li * dt), mag * jnp.sin(li * dt)
    den = lr * lr + li * li
    fr = ((ar - 1.0) * lr + ai * li) / den
    fi = (ai * lr - (ar - 1.0) * li) / den
    bbr = fr[..., None] * b_re - fi[..., None] * b_im
    bbi = fr[..., None] * b_im + fi[..., None] * b_re
    bu_r = jnp.einsum("blgh,gph->blgp", ug, bbr)
    bu_i = jnp.einsum("blgh,gph->blgp", ug, bbi)
    ir, ii = _cmul(ar, ai, h0_re.astype(f32), h0_im.astype(f32))
    bu_r = bu_r.at[:, 0].add(ir)
    bu_i = bu_i.at[:, 0].add(ii)
    a_r = jnp.broadcast_to(ar, bu_r.shape)
    a_i = jnp.broadcast_to(ai, bu_i.shape)
    _, _, xr, xi = lax.associative_scan(_s5_combine, (a_r, a_i, bu_r, bu_i), axis=1)
    y = jnp.einsum("blgp,ghp->blgh", xr, c_re) - jnp.einsum("blgp,ghp->blgh", xi, c_im)
    y = y.reshape(Bsz, L, S5_WIDTH) + d * uf
    y = jax.nn.gelu(y)
    y = y * jax.nn.sigmoid(y @ w_glu + b_glu)
    return y, xr[:, -1], xi[:, -1]


def lru_mixer(xb, gate, pos, h0, conv_buf, conv_w, conv_b, w_a, b_a, w_x, b_x, lam):
    f32 = jnp.float32
    Bsz, L, _ = xb.shape
    xc, new_buf = causal_conv(xb, conv_buf, conv_w)
    xc = (xc + conv_b).astype(f32)
    xh = xc.reshape(Bsz, L, LRU_BLOCKS, LRU_BLOCK)
    r = jax.nn.sigmoid(jnp.einsum("blhi,hij->blhj", xh, w_a).reshape(Bsz, L, LRU_WIDTH) + b_a)
    gi = jax.nn.sigmoid(jnp.einsum("blhi,hij->blhj", xh, w_x).reshape(Bsz, L, LRU_WIDTH) + b_x)
    log_a = -LRU_C * r * jax.nn.softplus(-lam.astype(f32))
    a = jnp.exp(log_a)
    mult = jnp.sqrt(-jnp.expm1(2.0 * log_a))
    mult = jnp.where((pos == 0)[None, :, None], 1.0, mult)
    b = mult * gi * xc
    b = b.at[:, 0].add(a[:, 0] * h0.astype(f32))
    _, h = lax.associative_scan(_lru_combine, (a, b), axis=1)
    y = h * jax.nn.gelu(gate.astype(f32))
    return y, h[:, -1], new_buf


def _pad_seq(t, pad):
    cfg = [(0, 0)] * t.ndim
    cfg[1] = (0, pad)
    return jnp.pad(t, cfg)


def gated_delta_chunked(q, k, v, g, beta, S0):
    f32 = jnp.float32
    Bsz, L, H, dk = q.shape
    dv = v.shape[-1]
    C = min(GDN_CHUNK, L)
    n = -(-L // C)
    pad = n * C - L
    q = q.astype(f32) * (dk ** -0.5)
    k, v, g, beta = k.astype(f32), v.astype(f32), g.astype(f32), beta.astype(f32)
    if pad:
        q, k, v, g, beta = (_pad_seq(t, pad) for t in (q, k, v, g, beta))

    def chunks(t):
        return jnp.moveaxis(t.reshape((Bsz, n, C, H) + t.shape[3:]), (1, 3), (0, 2))

    qc, kc, vc, gc, bc = (chunks(t) for t in (q, k, v, g, beta))
    gcum = jnp.cumsum(gc, axis=-1)
    idx = jnp.arange(C)
    incl = idx[:, None] >= idx[None, :]
    strict = idx[:, None] > idx[None, :]
    decay = jnp.exp(jnp.where(incl, gcum[..., :, None] - gcum[..., None, :], -jnp.inf))
    kb = kc * bc[..., None]
    vb = vc * bc[..., None]
    M = jnp.where(strict, jnp.einsum("nbhik,nbhjk->nbhij", kb, kc) * decay, 0.0)
    eye = jnp.eye(C, dtype=f32)
    T = lax.linalg.triangular_solve(eye + M, jnp.broadcast_to(eye, M.shape),
                                    left_side=True, lower=True, unit_diagonal=True)
    w_v = T @ vb
    w_k = T @ (kb * jnp.exp(gcum)[..., None])
    q_dec = qc * jnp.exp(gcum)[..., None]
    attn = jnp.einsum("nbhik,nbhjk->nbhij", qc, kc) * decay
    g_last = gcum[..., -1]
    k_dec = kc * jnp.exp(g_last[..., None] - gcum)[..., None]

    def step(S, xs):
        w_v_i, w_k_i, q_dec_i, attn_i, k_dec_i, g_last_i = xs
        v_new = w_v_i - jnp.einsum("bhck,bhkv->bhcv", w_k_i, S)
        o = jnp.einsum("bhck,bhkv->bhcv", q_dec_i, S) + jnp.einsum("bhij,bhjv->bhiv", attn_i, v_new)
        S = S * jnp.exp(g_last_i)[..., None, None] + jnp.einsum("bhck,bhcv->bhkv", k_dec_i, v_new)
        return S, o

    S, o = lax.scan(step, S0.astype(f32), (w_v, w_k, q_dec, attn, k_dec, g_last))
    o = jnp.moveaxis(o, (0, 2), (1, 3)).reshape(Bsz, n * C, H, dv)[:, :L]
    return o, S


def layer(x, pos, st, p):
    s5_re, s5_im, lru_h, lru_conv, gdn_S, gdn_conv = st
    Bsz, L, _ = x.shape
    h = rms_norm(x, p["norm_mix_pre"])
    proj = h @ p["w_in"]
    cuts = [int(c) for c in np.cumsum(IN_SPLITS)[:-1]]
    u_s5, x_lru, g_lru, qkv, a_gdn, b_gdn, z_gdn = jnp.split(proj, cuts, axis=-1)
    y_s5, n_s5_re, n_s5_im = s5_mixer(u_s5, s5_re, s5_im, p["s5_lambda_re"], p["s5_lambda_im"],
                                      p["s5_log_dt"], p["s5_b_re"], p["s5_b_im"], p["s5_c_re"],
                                      p["s5_c_im"], p["s5_d"], p["s5_w_glu"], p["s5_b_glu"])
    y_lru, n_lru_h, n_lru_conv = lru_mixer(x_lru, g_lru, pos, lru_h, lru_conv, p["lru_conv_w"],
                                           p["lru_conv_b"], p["lru_w_a"], p["lru_b_a"],
                                           p["lru_w_x"], p["lru_b_x"], p["lru_lambda"])
    qkv_c, n_gdn_conv = causal_conv(qkv, gdn_conv, p["gdn_conv_w"])
    qkv_c = jax.nn.silu(qkv_c.astype(jnp.float32))
    q, k, v = jnp.split(qkv_c, 3, axis=-1)
    q = l2_normalize(q.reshape(Bsz, L, GDN_HEADS, GDN_HEAD_DIM))
    k = l2_normalize(k.reshape(Bsz, L, GDN_HEADS, GDN_HEAD_DIM))
    v = v.reshape(Bsz, L, GDN_HEADS, GDN_HEAD_DIM)
    g = -jnp.exp(p["gdn_a_log"].astype(jnp.float32)) * jax.nn.softplus(a_gdn.astype(jnp.float32) + p["gdn_dt_bias"])
    beta = jax.nn.sigmoid(b_gdn.astype(jnp.float32))
    o, n_gdn_S = gated_delta_chunked(q, k, v, g, beta, gdn_S)
    z = z_gdn.astype(jnp.float32).reshape(Bsz, L, GDN_HEADS, GDN_HEAD_DIM)
    y_gdn = (rms_norm(o, p["gdn_norm_w"]) * jax.nn.silu(z)).reshape(Bsz, L, GDN_WIDTH)
    mix = jnp.concatenate([y_s5, y_lru, y_gdn], axis=-1).astype(x.dtype) @ p["w_out"]
    x = x + rms_norm(mix, p["norm_mix_post"]).astype(x.dtype)
    h = rms_norm(x, p["norm_ffn_pre"])
    gt, up = jnp.split(h @ p["ffn_w_gate_up"], 2, axis=-1)
    f = (jax.nn.silu(gt) * up) @ p["ffn_w_down"]
    x = x + rms_norm(f, p["norm_ffn_post"]).astype(x.dtype)
    return x, (n_s5_re, n_s5_im, n_lru_h, n_lru_conv, n_gdn_S, n_gdn_conv)


def setup_inputs(seed: int = 0) -> dict:
    key = jax.random.key(seed)
    ks = jax.random.split(key, 40)
    f32 = jnp.float32

    def nrm(k, shape, s):
        return s * jax.random.normal(k, shape, f32)

    G, P, H = S5_N_GROUPS, S5_STATE, S5_GROUP
    gdn_dt = jnp.exp(jax.random.uniform(ks[31], (DEPTH, GDN_HEADS), f32, minval=math.log(1e-3), maxval=math.log(1e-1)))
    lru_u = jax.random.uniform(ks[36], (DEPTH, LRU_WIDTH), f32, minval=0.9, maxval=0.999)
    lru_a0 = lru_u ** (1.0 / LRU_C)
    return {
        "x_prompt": nrm(ks[0], (BATCH, SEQ, D_MODEL), 1.0),
        "x_sample": nrm(ks[1], (DEC_BATCH, DEC_SEQ, D_MODEL), 1.0),
        "state_s5_re": nrm(ks[2], (DEPTH, DEC_BATCH, G, P), 0.3),
        "state_s5_im": nrm(ks[3], (DEPTH, DEC_BATCH, G, P), 0.3),
        "state_lru_h": nrm(ks[4], (DEPTH, DEC_BATCH, LRU_WIDTH), 0.3),
        "state_lru_conv": nrm(ks[5], (DEPTH, DEC_BATCH, CONV_WIDTH - 1, LRU_WIDTH), 1.0),
        "state_gdn_S": nrm(ks[6], (DEPTH, DEC_BATCH, GDN_HEADS, GDN_HEAD_DIM, GDN_HEAD_DIM), 0.05),
        "state_gdn_conv": nrm(ks[7], (DEPTH, DEC_BATCH, CONV_WIDTH - 1, 3 * GDN_WIDTH), 1.0),
        "norm_mix_pre": 1.0 + nrm(ks[8], (DEPTH, D_MODEL), 0.05),
        "norm_mix_post": 1.0 + nrm(ks[9], (DEPTH, D_MODEL), 0.05),
        "norm_ffn_pre": 1.0 + nrm(ks[10], (DEPTH, D_MODEL), 0.05),
        "norm_ffn_post": 1.0 + nrm(ks[11], (DEPTH, D_MODEL), 0.05),
        "w_in": nrm(ks[12], (DEPTH, D_MODEL, IN_COLS), D_MODEL ** -0.5),
        "s5_lambda_re": -0.5 + nrm(ks[13], (DEPTH, G, P), 0.01),
        "s5_lambda_im": jnp.pi * jnp.arange(P, dtype=f32) + nrm(ks[14], (DEPTH, G, P), 0.01),
        "s5_log_dt": jax.random.uniform(ks[15], (DEPTH, G), f32, minval=math.log(1e-3), maxval=math.log(1e-1)),
        "s5_b_re": nrm(ks[16], (DEPTH, G, P, H), (2 * H) ** -0.5),
        "s5_b_im": nrm(ks[17], (DEPTH, G, P, H), (2 * H) ** -0.5),
        "s5_c_re": nrm(ks[18], (DEPTH, G, H, P), P ** -0.5),
        "s5_c_im": nrm(ks[19], (DEPTH, G, H, P), P ** -0.5),
        "s5_d": nrm(ks[20], (DEPTH, S5_WIDTH), 1.0),
        "s5_w_glu": nrm(ks[21], (DEPTH, S5_WIDTH, S5_WIDTH), S5_WIDTH ** -0.5),
        "s5_b_glu": nrm(ks[22], (DEPTH, S5_WIDTH), 0.01),
        "lru_conv_w": nrm(ks[23], (DEPTH, CONV_WIDTH, LRU_WIDTH), CONV_WIDTH ** -0.5),
        "lru_conv_b": nrm(ks[24], (DEPTH, LRU_WIDTH), 0.01),
        "lru_w_a": nrm(ks[25], (DEPTH, LRU_BLOCKS, LRU_BLOCK, LRU_BLOCK), LRU_BLOCK ** -0.5),
        "lru_b_a": nrm(ks[26], (DEPTH, LRU_WIDTH), 0.01),
        "lru_w_x": nrm(ks[27], (DEPTH, LRU_BLOCKS, LRU_BLOCK, LRU_BLOCK), LRU_BLOCK ** -0.5),
        "lru_b_x": nrm(ks[28], (DEPTH, LRU_WIDTH), 0.01),
        "lru_lambda": jnp.log(lru_a0) - jnp.log1p(-lru_a0),
        "gdn_conv_w": nrm(ks[29], (DEPTH, CONV_WIDTH, 3 * GDN_WIDTH), CONV_WIDTH ** -0.5),
        "gdn_a_log": jnp.log(jax.random.uniform(ks[30], (DEPTH, GDN_HEADS), f32, minval=1.0, maxval=16.0)),
        "gdn_dt_bias": gdn_dt + jnp.log(-jnp.expm1(-gdn_dt)),
        "gdn_norm_w": 1.0 + nrm(ks[32], (DEPTH, GDN_HEAD_DIM), 0.05),
        "w_out": nrm(ks[33], (DEPTH, MIX_WIDTH, D_MODEL), MIX_WIDTH ** -0.5),
        "ffn_w_gate_up": nrm(ks[34], (DEPTH, D_MODEL, 2 * FFN_HIDDEN), D_MODEL ** -0.5),
        "ffn_w_down": nrm(ks[35], (DEPTH, FFN_HIDDEN, D_MODEL), FFN_HIDDEN ** -0.5),
    }


def reference(x_prompt, x_sample, state_s5_re, state_s5_im, state_lru_h, state_lru_conv,
              state_gdn_S, state_gdn_conv, norm_mix_pre, norm_mix_post, norm_ffn_pre,
              norm_ffn_post, w_in, s5_lambda_re, s5_lambda_im, s5_log_dt, s5_b_re, s5_b_im,
              s5_c_re, s5_c_im, s5_d, s5_w_glu, s5_b_glu, lru_conv_w, lru_conv_b, lru_w_a,
              lru_b_a, lru_w_x, lru_b_x, lru_lambda, gdn_conv_w, gdn_a_log, gdn_dt_bias,
              gdn_norm_w, w_out, ffn_w_gate_up, ffn_w_down):
    f32 = jnp.float32
    bp = x_prompt.shape[0]
    pos_prompt = jnp.arange(x_prompt.shape[1], dtype=jnp.int32)
    pos_sample = PAST_LEN + jnp.arange(x_sample.shape[1], dtype=jnp.int32)
    y_p, y_s = x_prompt, x_sample
    new_p = [[] for _ in range(6)]
    new_s = [[] for _ in range(6)]
    for l in range(DEPTH):
        p = {
            "norm_mix_pre": norm_mix_pre[l], "norm_mix_post": norm_mix_post[l],
            "norm_ffn_pre": norm_ffn_pre[l], "norm_ffn_post": norm_ffn_post[l],
            "w_in": w_in[l], "s5_lambda_re": s5_lambda_re[l], "s5_lambda_im": s5_lambda_im[l],
            "s5_log_dt": s5_log_dt[l], "s5_b_re": s5_b_re[l], "s5_b_im": s5_b_im[l],
            "s5_c_re": s5_c_re[l], "s5_c_im": s5_c_im[l], "s5_d": s5_d[l],
            "s5_w_glu": s5_w_glu[l], "s5_b_glu": s5_b_glu[l],
            "lru_conv_w": lru_conv_w[l], "lru_conv_b": lru_conv_b[l], "lru_w_a": lru_w_a[l],
            "lru_b_a": lru_b_a[l], "lru_w_x": lru_w_x[l], "lru_b_x": lru_b_x[l],
            "lru_lambda": lru_lambda[l], "gdn_conv_w": gdn_conv_w[l], "gdn_a_log": gdn_a_log[l],
            "gdn_dt_bias": gdn_dt_bias[l], "gdn_norm_w": gdn_norm_w[l], "w_out": w_out[l],
            "ffn_w_gate_up": ffn_w_gate_up[l], "ffn_w_down": ffn_w_down[l],
        }
        st_p = (jnp.zeros((bp, S5_N_GROUPS, S5_STATE), f32),
                jnp.zeros((bp, S5_N_GROUPS, S5_STATE), f32),
                jnp.zeros((bp, LRU_WIDTH), f32),
                jnp.zeros((bp, CONV_WIDTH - 1, LRU_WIDTH), x_prompt.dtype),
                jnp.zeros((bp, GDN_HEADS, GDN_HEAD_DIM, GDN_HEAD_DIM), f32),
                jnp.zeros((bp, CONV_WIDTH - 1, 3 * GDN_WIDTH), x_prompt.dtype))
        st_s = (state_s5_re[l], state_s5_im[l], state_lru_h[l], state_lru_conv[l],
                state_gdn_S[l], state_gdn_conv[l])
        y_p, sp = layer(y_p, pos_prompt, st_p, p)
        y_s, ss = layer(y_s, pos_sample, st_s, p)
        for i in range(6):
            new_p[i].append(sp[i])
            new_s[i].append(ss[i])
    p_s5_re, p_s5_im, p_lru_h, p_lru_conv, p_gdn_S, p_gdn_conv = (jnp.stack(t, axis=0) for t in new_p)
    s_s5_re, s_s5_im, s_lru_h, s_lru_conv, s_gdn_S, s_gdn_conv = (jnp.stack(t, axis=0) for t in new_s)
    return (y_p, y_s, p_s5_re, p_s5_im, p_lru_h, p_lru_conv, p_gdn_S, p_gdn_conv,
            s_s5_re, s_s5_im, s_lru_h, s_lru_conv, s_gdn_S, s_gdn_conv)
```

```python
import math
import numpy as np
from contextlib import ExitStack
import concourse.bass as bass
import concourse.mybir as mybir
from concourse.bass_utils import run_bass_kernel_spmd

F32 = mybir.dt.float32
BF16 = mybir.dt.bfloat16
AF = mybir.ActivationFunctionType
ALU = mybir.AluOpType

ENG = ("pe", "act", "dve", "pool", "sp")
NDMA_SEM = 40
PI = math.pi


class Tok:
    __slots__ = ("w", "r")

    def __init__(self):
        self.w = None
        self.r = {}


def toks(n):
    return [Tok() for _ in range(n)]


class Prog:
    def __init__(self, same_engine_sync=True):
        self.nc = bass.Bass("TRN2", target_bir_lowering=False)
        self.es = ExitStack()
        self.ops = {e: [] for e in ENG}
        self.cnt = {e: 0 for e in ENG}
        self.seen = {e: {} for e in ENG}
        self.same_engine_sync = same_engine_sync
        self.esem = {e: self.es.enter_context(self.nc.semaphore("s_" + e)) for e in ENG[:4]}
        self.dsem = [self.es.enter_context(self.nc.semaphore("d%d" % i)) for i in range(NDMA_SEM)]
        self.dval = [0] * NDMA_SEM
        self.dnext = {False: 0, True: 0}
        self.nsb = 0
        self.capture = None
        self.opn = {e: {} for e in ENG}

    def sb(self, shape, dtype=F32):
        self.nsb += 1
        return self.es.enter_context(self.nc.sbuf_tensor("sb%d" % self.nsb, list(shape), dtype))

    def ps(self, shape, dtype=F32):
        self.nsb += 1
        return self.es.enter_context(self.nc.psum_tensor("ps%d" % self.nsb, list(shape), dtype))

    def dram(self, name, shape, dtype=F32, kind="ExternalInput"):
        return self.nc.dram_tensor(name, list(shape), dtype, kind=kind).ap()

    def _deps(self, reads, writes):
        deps = {}
        for t in reads:
            if t.w is not None and deps.get(t.w[0], 0) < t.w[1]:
                deps[t.w[0]] = t.w[1]
        for t in writes:
            if t.w is not None and deps.get(t.w[0], 0) < t.w[1]:
                deps[t.w[0]] = t.w[1]
            for k, v in t.r.items():
                if deps.get(k, 0) < v:
                    deps[k] = v
        return deps

    def _waits(self, eng, deps):
        waits = []
        for k, v in deps.items():
            if k == eng and (eng == "pe" or not self.same_engine_sync):
                continue
            if self.seen[eng].get(k, 0) >= v:
                continue
            self.seen[eng][k] = v
            waits.append((k, v))
        return waits

    def _update(self, ev, reads, writes):
        k, v = ev
        for t in reads:
            if t.r.get(k, 0) < v:
                t.r[k] = v
        for t in writes:
            t.w = ev
            t.r = {}

    def op(self, eng, fn, reads=(), writes=(), n=0):
        if self.capture is not None:
            self.capture.append((0, eng, fn, list(reads), list(writes), n))
            return
        deps = self._deps(reads, writes)
        if ELIDE_BIG and eng != "pe" and eng in deps and n >= BIG_N:
            big = True
            for t in list(reads) + list(writes):
                if t.w is not None and t.w[0] == eng and self.opn[eng].get(t.w[1], 0) < BIG_N:
                    big = False
            for t in writes:
                if eng in t.r and self.opn[eng].get(t.r[eng], 0) < BIG_N:
                    big = False
            if big:
                del deps[eng]
        waits = self._waits(eng, deps)
        self.cnt[eng] += 1
        self.opn[eng][self.cnt[eng]] = n
        ev = (eng, self.cnt[eng])
        self.ops[eng].append((waits, fn, ev))
        self._update(ev, reads, writes)

    def dma(self, eng, fn, reads=(), writes=()):
        if self.capture is not None:
            self.capture.append((1, eng, fn, list(reads), list(writes), 0))
            return
        deps = self._deps(reads, writes)
        half = NDMA_SEM // 2
        base = half if eng == "pool" else 0
        i = base + self.dnext[eng == "pool"]
        self.dnext[eng == "pool"] = (self.dnext[eng == "pool"] + 1) % half
        key = "d%d" % i
        if self.dval[i] > 0 and deps.get(key, 0) < self.dval[i]:
            deps[key] = self.dval[i]
        waits = self._waits(eng, deps)
        self.dval[i] += 16
        ev = (key, self.dval[i])
        self.ops[eng].append((waits, fn, ev))
        self._update(ev, reads, writes)

    def replay(self, a, b=(), ratio=1.0):
        ia = ib = 0
        acc = 0.0
        while ia < len(a) or ib < len(b):
            if ia < len(a) and (ib >= len(b) or acc < ratio):
                rec = a[ia]
                ia += 1
                acc += 1.0
            else:
                rec = b[ib]
                ib += 1
                acc -= ratio
            if rec[0]:
                self.dma(*rec[1:5])
            else:
                self.op(*rec[1:6])

    def barrier(self):
        for e in ENG:
            deps = {k: self.cnt[k] for k in ENG[:4] if k != e and self.cnt[k] > 0}
            if e != "sp":
                for i in range(NDMA_SEM):
                    if self.dval[i] > 0:
                        deps["d%d" % i] = self.dval[i]
            waits = self._waits(e, deps)
            if waits:
                self.ops[e].append((waits, None, None))

    def _sem(self, k):
        return self.esem[k] if k in self.esem else self.dsem[int(k[1:])]

    def build(self):
        nc = self.nc
        with nc.Block() as block:
            def run(e, name):
                for waits, fn, ev in self.ops[name]:
                    for k, v in waits:
                        e.wait_ge(self._sem(k), v)
                    if fn is None:
                        continue
                    inst = fn(e)
                    k, v = ev
                    inst.then_inc(self._sem(k), 1 if k in self.esem else 16)
                if name == "sp":
                    for i in range(NDMA_SEM):
                        if self.dval[i] > 0:
                            e.wait_ge(self.dsem[i], self.dval[i])
                    for k2 in ENG[:4]:
                        if self.cnt[k2] > 0:
                            e.wait_ge(self.esem[k2], self.cnt[k2])

            @block.tensor
            def _(e):
                run(e, "pe")

            @block.scalar
            def _(e):
                run(e, "act")

            @block.vector
            def _(e):
                run(e, "dve")

            @block.gpsimd
            def _(e):
                run(e, "pool")

            @block.sync
            def _(e):
                run(e, "sp")
        self.es.close()
        return nc


D_MODEL = 1024
SEQ = 2048
NT = 4
TN = 512
NSB = 16
SL = 4
SN = NSB * SL
IN_COLS = 2824
FFN_H = 2816
SEG = 128
EPS = 1e-6
PPL = 128
NLEV = [1, 2, 4, 8, 16, 32, 64]
SAME_ENGINE_SYNC = True
ELIDE_BIG = False
BIG_N = 256
ARENA_F = 12864
ARENA_B = 16896

CF_ID, CF_TRI, CF_ONES, CF_MSTR, CF_MINCT = 0, 128, 256, 384, 512
CF_TRIS, CF_SEGS, CF_MSTRS, CF_MINCTS, CF_SIND = 640, 704, 768, 832, 896
NCF = 912
CB_ID, CB_ONES, CB_NML, CB_NMU, CB_SEQM = 0, 128, 256, 256 + 7 * 128, 256 + 14 * 128
NCB = CB_SEQM + 1024


class Grp:
    def __init__(self, kind, idx):
        self.kind = kind
        self.idx = idx
        if kind == "p":
            self.N, self.nb, self.L, self.P, self.nblk = TN, 1, TN, 128, 4
        else:
            self.N, self.nb, self.L, self.P, self.nblk = SN, NSB, SL, 64, 1


class Kern:
    def __init__(self, dbg=False, nlayers=2, groups=None):
        self.dbgon = dbg
        self.P = Prog(same_engine_sync=SAME_ENGINE_SYNC)
        self.nl = nlayers
        self.groups = groups
        self.dbg_outs = []
        self.build()

    @staticmethod
    def nel(ap):
        return int(np.prod(ap.shape[1:]))

    def tt(self, eng, out, a, b, op, r, w):
        n = min(self.nel(out), self.nel(a), self.nel(b))
        self.P.op(eng, lambda e: e.tensor_tensor(out=out, in0=a, in1=b, op=op), r, w, n=n)

    def ts(self, eng, out, a, s1, op0, r, w, s2=None, op1=None):
        if op1 is None:
            self.P.op(eng, lambda e: e.tensor_scalar(out=out, in0=a, scalar1=s1, scalar2=None, op0=op0), r, w, n=self.nel(out))
        else:
            self.P.op(eng, lambda e: e.tensor_scalar(out=out, in0=a, scalar1=s1, scalar2=s2, op0=op0, op1=op1), r, w, n=self.nel(out))

    def stt(self, out, a, s, b, op0, op1, r, w):
        n = min(self.nel(out), self.nel(a), self.nel(b))
        self.P.op("dve", lambda e: e.scalar_tensor_tensor(out=out, in0=a, scalar=s, in1=b, op0=op0, op1=op1), r, w, n=n)

    def act(self, out, in_, func, r, w, bias=None, scale=None):
        kw = {}
        if bias is not None:
            kw["bias"] = bias
        if scale is not None:
            kw["scale"] = scale
        self.P.op("act", lambda e: e.activation(out=out, in_=in_, func=func, **kw), r, w, n=min(self.nel(out), self.nel(in_)))

    def cp(self, eng, out, in_, r, w):
        if eng == "act":
            self.P.op("act", lambda e: e.activation(out=out, in_=in_, func=AF.Copy), r, w, n=min(self.nel(out), self.nel(in_)))
        else:
            self.P.op(eng, lambda e: e.tensor_copy(out=out, in_=in_), r, w, n=min(self.nel(out), self.nel(in_)))

    def mm(self, out, lhsT, rhs, r, w, start=True, stop=True):
        self.P.op("pe", lambda e: e.matmul(out, lhsT=lhsT, rhs=rhs, start=start, stop=stop), r, w)

    def tr(self, out, in_, ident, r, w):
        self.P.op("pe", lambda e: e.transpose(out=out, in_=in_, identity=ident), r, w)

    def recip(self, out, in_, r, w):
        self.P.op("dve", lambda e: e.reciprocal(out=out, in_=in_), r, w)

    def memset(self, eng, ap, val, w):
        self.P.op(eng, lambda e: e.memset(ap, val), [], w)

    def ld(self, out, in_, w, eng="sp", r=()):
        self.P.dma(eng, lambda e: e.dma_start(out=out, in_=in_), r, w)

    def st(self, out, in_, r, eng="sp"):
        self.P.dma(eng, lambda e: e.dma_start(out=out, in_=in_), r, [])

    def dbg(self, name, ap, r, shape, bf=False):
        if not self.dbgon:
            return
        d = self.P.dram("dbg_" + name, shape, kind="ExternalOutput")
        self.dbg_outs.append("dbg_" + name)
        self.st(d, ap, r, eng=("pool" if bf else "sp"))

    def psum(self):
        st = self.ps_set
        i = st[self.ps_cur[id(st)] % len(st)]
        self.ps_cur[id(st)] += 1
        return self.PS[i], self.tPS[i]

    def capture(self, fn, psset):
        P = self.P
        old = self.ps_set
        self.ps_set = psset
        P.capture = []
        fn()
        ops = P.capture
        P.capture = None
        self.ps_set = old
        return ops

    def AFB_(self, *shape):
        n = int(np.prod(shape))
        v = self.AF_((n + 1) // 2).bitcast(BF16)[:, 0:n]
        return self._shape(v, shape)

    def arena_reset(self):
        self.P.barrier()
        self.af = 0
        self.ab = 0

    def AF_(self, *shape):
        n = int(np.prod(shape))
        self.af = (self.af + 15) // 16 * 16
        assert self.af + n <= ARENA_F, ("arena f32 overflow", self.af, n)
        v = self.ARF[:, self.af:self.af + n]
        self.af += n
        return self._shape(v, shape)

    def AB_(self, *shape):
        n = int(np.prod(shape))
        self.ab = (self.ab + 31) // 32 * 32
        assert self.ab + n <= ARENA_B, ("arena bf16 overflow", self.ab, n)
        v = self.ARB[:, self.ab:self.ab + n]
        self.ab += n
        return self._shape(v, shape)

    @staticmethod
    def _shape(v, shape):
        if len(shape) == 1:
            return v
        if len(shape) == 2:
            return v.rearrange("p (a b) -> p a b", a=shape[0])
        return v.rearrange("p (a b c) -> p a b c", a=shape[0], b=shape[1])

    def wplan(self):
        self.wA = []
        self.wB = []
        for g in self.groups:
            for l in range(self.nl):
                wi = self.w_in[l]
                for c0 in (0, 256, 512, 768, 1024, 1280, 1536, 1792, 2048, 2312, 2568):
                    self.wA.append((wi[:, c0:c0 + 256], 256))
                for c0 in range(0, D_MODEL, 256):
                    self.wA.append((self.w_out[l][:, c0:c0 + 256], 256))
                for c0 in range(0, FFN_H, 256):
                    self.wA.append((self.w_gu[l][:, c0:c0 + 256], 256))
                    self.wA.append((self.w_gu[l][:, FFN_H + c0:FFN_H + c0 + 256], 256))
                for c0 in range(0, D_MODEL, 128):
                    self.wB.append(self.w_dn[l][:, c0:c0 + 128])
        self.wAi = 0
        self.wAe = 0
        self.wBi = 0
        self.wBe = 0

    def _emitA(self, upto):
        while self.wAe <= upto and self.wAe < len(self.wA):
            j = self.wAe
            ap, w = self.wA[j]
            buf = self.WA[j % len(self.WA)]
            self.ld(buf[:, :, 0:w], ap.rearrange("(kc p) n -> p kc n", p=128), [self.tWA[j % len(self.WA)]], eng="pool")
            self.wAe += 1

    def getA(self):
        i = self.wAi
        self._emitA(i)
        self.wAi += 1
        return self.WA[i % len(self.WA)], self.tWA[i % len(self.WA)]

    def prefetchA(self):
        self._emitA(self.wAi + len(self.WA) - 1)

    def prefetchB(self):
        self._emitB(self.wBi + len(self.WB) - 1)

    def _emitB(self, upto):
        while self.wBe <= upto and self.wBe < len(self.wB):
            j = self.wBe
            buf = self.WB[j % len(self.WB)]
            self.ld(buf[:], self.wB[j].rearrange("(kc p) n -> p kc n", p=128), [self.tWB[j % len(self.WB)]], eng="pool")
            self.wBe += 1

    def getB(self):
        i = self.wBi
        self._emitB(i)
        self.wBi += 1
        return self.WB[i % len(self.WB)], self.tWB[i % len(self.WB)]

    def build(self):
        P = self.P
        nl = self.nl
        if self.groups is None:
            self.groups = [Grp("s", 0)] + [Grp("p", i) for i in range(NT)]
        self.xp = P.dram("xp", [D_MODEL, SEQ])
        self.xs = P.dram("xs", [D_MODEL, SN])
        self.i_s5 = P.dram("i_s5", [2, 2, 128, 8, NSB])
        self.i_lruh = P.dram("i_lruh", [2, 128, 2, NSB])
        self.i_lruconv = P.dram("i_lruconv", [2, 128, 2, NSB, 3])
        self.i_gdnconv = P.dram("i_gdnconv", [2, 128, 12, NSB, 3])
        self.i_gdnS = P.dram("i_gdnS", [2, NSB, 4, 128, 128])
        self.w_in = P.dram("w_in", [2, D_MODEL, IN_COLS])
        self.w_out = P.dram("w_out", [2, D_MODEL, D_MODEL])
        self.w_gu = P.dram("w_gu", [2, D_MODEL, 2 * FFN_H])
        self.w_dn = P.dram("w_dn", [2, FFN_H, D_MODEL])
        self.d_pp = P.dram("pp", [128, 2 * PPL])
        self.d_rp = P.dram("rp", [2, 3, 1024])
        self.d_gr = P.dram("gr", [2, 2, 4])
        self.d_cf = P.dram("cf", [128, NCF])
        self.d_cb = P.dram("cb", [128, NCB])
        self.d_bbd = P.dram("bbd", [2, 2, 8, 128, 128])
        self.d_cbd = P.dram("cbd", [2, 2, 8, 128, 128])
        self.d_wglu = P.dram("wglu", [2, 256, 256])
        self.d_wax = P.dram("wax", [2, 2, 2, 128, 128])
        o = lambda n, s: P.dram(n, s, kind="ExternalOutput")
        self.yp = o("yp", [D_MODEL, SEQ])
        self.ys = o("ys", [D_MODEL, SN])
        self.o_s5 = o("o_s5", [2, 2, 128, 8])
        self.o_lruh = o("o_lruh", [2, 128, 2])
        self.o_lruconv = o("o_lruconv", [2, 128, 2, 3])
        self.o_gdnconv = o("o_gdnconv", [2, 128, 12, 3])
        self.o_gdnS = o("o_gdnS", [2, 4, 128, 128])
        self.os_s5 = o("os_s5", [2, 2, 128, 8, NSB])
        self.os_lruh = o("os_lruh", [2, 128, 2, NSB])
        self.os_lruconv = o("os_lruconv", [2, 128, 2, NSB, 3])
        self.os_gdnconv = o("os_gdnconv", [2, 128, 12, NSB, 3])
        self.os_gdnS = o("os_gdnS", [2, NSB, 4, 128, 128])

        self.X = P.sb([128, 8, TN]); self.tX = toks(8)
        self.HB = P.sb([128, 8, TN], BF16); self.tHB = toks(8)
        self.MIXB = P.sb([128, 8, TN], BF16); self.tMIX = toks(8)
        self.ARF = P.sb([128, ARENA_F]); self.ARB = P.sb([128, ARENA_B], BF16)
        self.CF = P.sb([128, NCF]); self.tCF = Tok()
        self.CB = P.sb([128, NCB], BF16); self.tCB = Tok()
        self.PP = P.sb([128, 2 * PPL]); self.tPP = Tok()
        self.EPS = P.sb([128, 16]); self.tC = Tok()
        self.WA = [P.sb([128, 8, 256], BF16) for _ in range(5)]; self.tWA = toks(5)
        self.WB = [P.sb([128, 22, 128], BF16) for _ in range(3)]; self.tWB = toks(3)
        self.PS = [P.ps([128, 512]) for _ in range(6)]; self.tPS = toks(6)
        self.PSN = P.ps([128, 512]); self.tPSN = Tok()
        self.PSB = P.ps([128, 2, 512], BF16); self.tPSB = [Tok()] * 2
        self.PS_ALL = [0, 1, 2, 3, 4, 5]; self.PS_F = [0, 1, 2]; self.PS_C = [3, 4, 5]
        self.ps_set = self.PS_ALL
        self.ps_cur = {id(self.PS_ALL): 0, id(self.PS_F): 0, id(self.PS_C): 0}
        self.psbn = 0
        L2 = range(2)
        self.TC = [P.sb([128, 8, SEG]) for _ in L2]; self.TS = [P.sb([128, 8, SEG]) for _ in L2]
        self.S5C = [P.sb([128, 8, 16]) for _ in L2]
        self.BRE = [P.sb([128, 8, 128], BF16) for _ in L2]; self.BIM = [P.sb([128, 8, 128], BF16) for _ in L2]
        self.CRE = [P.sb([128, 8, 128], BF16) for _ in L2]; self.NCIM = [P.sb([128, 8, 128], BF16) for _ in L2]
        self.WGLU = [P.sb([128, 2, 256], BF16) for _ in L2]
        self.WAX = [P.sb([128, 2, 2, 128], BF16) for _ in L2]
        self.WAB = [P.sb([128, 8, 8], BF16) for _ in L2]
        self.NTSL = [P.sb([128, 16]) for _ in L2]
        self.LC = [P.sb([128, 16]) for _ in L2]
        self.GR = [P.sb([128, 2, 16]) for _ in L2]
        self.tL = toks(2)
        self.S5I = [P.sb([128, 2, 16]) for _ in L2]; self.tS5I = [toks(8) for _ in L2]
        self.HL = [P.sb([128, 16]) for _ in L2]; self.tHL = toks(2); self.tHLc = [toks(2) for _ in L2]
        self.HXL = [P.sb([128, 2, 8]) for _ in L2]; self.tHXL = toks(2)
        self.HQKV = [P.sb([128, 12, 4]) for _ in L2]; self.tHQ = toks(2)
        self.SF = [P.sb([128, 4, 128]) for _ in L2]; self.SBF = [P.sb([128, 4, 128], BF16) for _ in L2]
        self.tS = [toks(4) for _ in L2]

        self.wplan()
        self.setup()
        self.dbg("s5c0", self.S5C[0][:, :, 0:8], [self.tL[0]], [128, 8, 8])
        self.dbg("tc0", self.TC[0][:], [self.tL[0]], [128, 8, SEG])
        self.dbg("ts0", self.TS[0][:], [self.tL[0]], [128, 8, SEG])
        self.dbg("bre0", self.BRE[0][:], [self.tL[0]], [128, 8, 128], bf=True)
        self.dbg("bim0", self.BIM[0][:], [self.tL[0]], [128, 8, 128], bf=True)
        self.dbg("ncim0", self.NCIM[0][:], [self.tL[0]], [128, 8, 128], bf=True)
        self.XN = None
        for gi, g in enumerate(self.groups):
            self.next_g = self.groups[gi + 1] if gi + 1 < len(self.groups) else None
            for l in range(nl):
                self.layer(g, l)
        self.nc = P.build()

    def cf(self, off, P_, n):
        return self.CF[0:P_, off:off + n]

    def pp(self, l, off, n=1):
        return self.PP[:, l * PPL + off:l * PPL + off + n]

    def sincos(self, th, shape, sn, cs, r, w):
        k = self.AF_(*shape); rr = self.AF_(*shape); tk = Tok()
        for phase, dst in ((0.0, sn), (PI / 2, cs)):
            self.ts("dve", k, th, PI - phase, ALU.is_ge, r, [tk])
            for m in range(1, 7):
                self.stt(k, th, (2 * m + 1) * PI - phase, k, ALU.is_ge, ALU.add, r + [tk], [tk])
            self.stt(rr, k, -2 * PI, th, ALU.mult, ALU.add, r + [tk], [tk])
            if phase:
                self.ts("dve", rr, rr, phase, ALU.add, [tk], [tk])
            self.act(dst, rr, AF.Sin, [tk], w + [tk])

    def setup(self):
        P = self.P
        self.af = 0
        self.ab = 0
        self.ld(self.CF[:], self.d_cf, [self.tCF])
        self.ld(self.PP[:], self.d_pp, [self.tPP])
        self.ld(self.CB[:], self.d_cb, [self.tCB], eng="pool")
        self.memset("dve", self.EPS[:], EPS, [self.tC])
        for l in range(self.nl):
            tl = self.tL[l]
            self.memset("dve", self.HL[l][:], 0.0, self.tHLc[l])
            self.memset("dve", self.HXL[l][:], 0.0, [self.tHXL[l]])
            self.memset("dve", self.HQKV[l][:], 0.0, [self.tHQ[l]])
            self.memset("dve", self.S5I[l][:], 0.0, self.tS5I[l])
            self.memset("dve", self.SF[l][:], 0.0, self.tS[l])
            self.memset("dve", self.SBF[l][:], 0.0, self.tS[l])
            self.ld(self.CRE[l][:], self.d_cbd[l, 0].rearrange("s r c -> r s c"), [tl], eng="pool")
            self.ld(self.WGLU[l][:], self.d_wglu[l].rearrange("(kc p) n -> p kc n", p=128), [tl], eng="pool")
            self.ld(self.WAX[l][:], self.d_wax[l].rearrange("a c r k -> r a c k"), [tl], eng="pool")
            self.ld(self.WAB[l][:], self.w_in[l][:, 2304:2312].rearrange("(kc p) n -> p kc n", p=128), [tl], eng="pool")
            self.ld(self.GR[l][:, :, 0:4], self.d_gr[l:l + 1].rearrange("o a h -> o (a h)").partition_broadcast(128)
                    .rearrange("p o (a h) -> p (o a) h", a=2), [tl])
            self.arena_reset()
            r0 = [self.tPP]
            t = Tok()
            lre, lim, ldt = self.pp(l, 32, 8), self.pp(l, 40, 8), self.pp(l, 48, 8)
            lr = self.AF_(8); dt = self.AF_(8); th = self.AF_(8); sn = self.AF_(8); cs = self.AF_(8); tmp = self.AF_(8)
            C = self.S5C[l]
            self.ts("dve", lr, lre, -1e-4, ALU.min, r0, [t])
            self.act(dt, ldt, AF.Exp, r0, [t])
            self.tt("dve", th, lim, dt, ALU.mult, r0 + [t], [t])
            self.tt("dve", tmp, lr, dt, ALU.mult, [t], [t])
            self.act(C[:, 0, 0:8], tmp, AF.Exp, [t], [t, tl])
            self.sincos(th, (8,), sn, cs, [t], [t])
            self.tt("dve", C[:, 3, 0:8], C[:, 0, 0:8], cs, ALU.mult, [t], [t, tl])
            self.tt("dve", C[:, 4, 0:8], C[:, 0, 0:8], sn, ALU.mult, [t], [t, tl])
            self.ts("dve", C[:, 6, 0:8], C[:, 4, 0:8], -1.0, ALU.mult, [t], [t, tl])
            TCt, TSt = self.TC[l], self.TS[l]
            er = self.AF_(8); ei = self.AF_(8); e2 = self.AF_(8); e3 = self.AF_(8)
            t1 = self.AF_(8, SEG // 2); t2 = self.AF_(8, SEG // 2)
            self.memset("dve", TCt[:, :, 0:1], 1.0, [t, tl])
            self.memset("dve", TSt[:, :, 0:1], 0.0, [t, tl])
            self.cp("dve", er, cs, [t], [t])
            self.cp("dve", ei, sn, [t], [t])
            m = 1
            while m < SEG:
                erb = er.unsqueeze(2).to_broadcast([128, 8, m]); eib = ei.unsqueeze(2).to_broadcast([128, 8, m])
                self.tt("dve", t1[:, :, 0:m], TCt[:, :, 0:m], erb, ALU.mult, [t], [t])
                self.tt("dve", t2[:, :, 0:m], TSt[:, :, 0:m], eib, ALU.mult, [t], [t])
                self.tt("dve", TCt[:, :, m:2 * m], t1[:, :, 0:m], t2[:, :, 0:m], ALU.subtract, [t], [t, tl])
                self.tt("dve", t1[:, :, 0:m], TCt[:, :, 0:m], eib, ALU.mult, [t], [t])
                self.tt("dve", t2[:, :, 0:m], TSt[:, :, 0:m], erb, ALU.mult, [t], [t])
                self.tt("dve", TSt[:, :, m:2 * m], t1[:, :, 0:m], t2[:, :, 0:m], ALU.add, [t], [t, tl])
                self.tt("dve", e2, er, er, ALU.mult, [t], [t])
                self.tt("dve", e3, ei, ei, ALU.mult, [t], [t])
                self.stt(ei, er, 2.0, ei, ALU.mult, ALU.mult, [t], [t])
                self.tt("dve", er, e2, e3, ALU.subtract, [t], [t])
                m *= 2
            self.cp("dve", C[:, 1, 0:8], er, [t], [t, tl])
            self.cp("dve", C[:, 2, 0:8], ei, [t], [t, tl])
            self.ts("dve", C[:, 5, 0:8], ei, -1.0, ALU.mult, [t], [t, tl])
            self.ts("dve", self.NTSL[l][:, 0:8], TSt[:, :, SEG - 1], -1.0, ALU.mult, [t], [t, tl])
            li = lim
            am1 = self.AF_(8); n1 = self.AF_(8); n2 = self.AF_(8); dn = self.AF_(8); fr = self.AF_(8); fi = self.AF_(8)
            self.ts("dve", am1, C[:, 3, 0:8], -1.0, ALU.add, [t], [t])
            self.tt("dve", n1, am1, lr, ALU.mult, [t], [t])
            self.tt("dve", n2, C[:, 4, 0:8], li, ALU.mult, [t, self.tPP], [t])
            self.tt("dve", n1, n1, n2, ALU.add, [t], [t])
            self.tt("dve", n2, C[:, 4, 0:8], lr, ALU.mult, [t], [t])
            self.tt("dve", dn, am1, li, ALU.mult, [t, self.tPP], [t])
            self.tt("dve", n2, n2, dn, ALU.subtract, [t], [t])
            self.tt("dve", dn, lr, lr, ALU.mult, [t], [t])
            self.tt("dve", am1, li, li, ALU.mult, [t, self.tPP], [t])
            self.tt("dve", dn, dn, am1, ALU.add, [t], [t])
            self.recip(dn, dn, [t], [t])
            self.tt("dve", fr, n1, dn, ALU.mult, [t], [t])
            self.tt("dve", fi, n2, dn, ALU.mult, [t], [t])
            tdg = toks(2)
            DG = [self.AF_(8, 128), self.AF_(8, 128)]
            idf = self.cf(CF_ID, 128, 128); onesf = self.cf(CF_ONES, 128, 128)
            rows = []
            for qi, col in enumerate((fr, fi)):
                banks = []
                for half in range(2):
                    ps, tps = self.psum()
                    for j in range(4):
                        sc = half * 4 + j
                        self.ts("dve", DG[qi][:, sc, :], idf, col[:, sc:sc + 1], ALU.mult, [self.tCF, t], [tdg[qi]])
                        self.mm(ps[:, j * 128:(j + 1) * 128], onesf, DG[qi][:, sc, :], [self.tCF, tdg[qi]], [tps])
                    banks.append((ps, tps))
                rows.append(banks)
            bre = self.AF_(8, 128); bim = self.AF_(8, 128); u1 = self.AF_(8, 128); u2 = self.AF_(8, 128)
            self.ld(bre, self.d_bbd[l, 0].rearrange("s r c -> r s c"), [t])
            self.ld(bim, self.d_bbd[l, 1].rearrange("s r c -> r s c"), [t])
            for half in range(2):
                hs = slice(half * 4, half * 4 + 4)
                frp, tfr = rows[0][half]; fip, tfi = rows[1][half]
                frv = frp[:, :].rearrange("p (s c) -> p s c", s=4); fiv = fip[:, :].rearrange("p (s c) -> p s c", s=4)
                self.tt("dve", u1[:, hs], frv, bre[:, hs], ALU.mult, [tfr, t], [t])
                self.tt("dve", u2[:, hs], fiv, bim[:, hs], ALU.mult, [tfi, t], [t])
                self.tt("dve", self.BRE[l][:, hs], u1[:, hs], u2[:, hs], ALU.subtract, [t], [t, tl])
                self.tt("dve", u1[:, hs], frv, bim[:, hs], ALU.mult, [tfr, t], [t])
                self.tt("dve", u2[:, hs], fiv, bre[:, hs], ALU.mult, [tfi, t], [t])
                self.tt("dve", self.BIM[l][:, hs], u1[:, hs], u2[:, hs], ALU.add, [t], [t, tl])
            self.ld(bre, self.d_cbd[l, 1].rearrange("s r c -> r s c"), [t])
            self.ts("dve", self.NCIM[l][:], bre, -1.0, ALU.mult, [t], [t, tl])
            a3 = u1[:, 0]
            e1 = a3[:, 0:2]
            self.act(e1, self.pp(l, 125, 2), AF.Exp, [self.tPP], [t], scale=-1.0)
            self.act(e1, e1, AF.Ln, [t], [t], bias=1.0)
            self.ts("dve", self.LC[l][:, 0:2], e1, -8.0, ALU.mult, [t], [t, tl])
            self.act(self.GR[l][:, 0, 0:4], self.GR[l][:, 0, 0:4], AF.Exp, [tl], [tl])
            self.ts("dve", self.GR[l][:, 0, 0:4], self.GR[l][:, 0, 0:4], -1.0, ALU.mult, [tl], [tl])
        self.arena_reset()

    def norm_ssq(self, srcs, N, r_list):
        ps, tps = self.psum()
        n = len(srcs)
        sc = 1.0 / math.sqrt(128.0 * n)
        for c, (src, rt) in enumerate(zip(srcs, r_list)):
            sq = self.SQ[c % 2]; tsq = self.tSQ[c % 2]
            self.act(sq[:, 0:N], src, AF.Square, rt, [tsq], scale=sc)
            self.mm(ps[:, 0:N], self.CB[:, CB_ONES:CB_ONES + 128], sq[:, 0:N], [tsq, self.tCB], [tps],
                    start=(c == 0), stop=(c == n - 1))
        return ps, tps

    def rstd_from(self, ps, tps, N, out, tout):
        self.act(out, ps[:, 0:N], AF.Ln, [tps, self.tC], [tout], bias=self.EPS[:, 0:1])
        self.act(out, out, AF.Exp, [tout], [tout], scale=-0.5)

    def prenorm(self, g, l, kind):
        N = g.N
        ps, tps = self.norm_ssq([self.X[:, c, 0:N] for c in range(8)], N, [[self.tX[c]] for c in range(8)])
        rs = self.RSTD; trs = self.tRSTD
        self.rstd_from(ps, tps, N, rs[:, 0:N], trs)
        for c in range(8):
            self.stt(self.HB[:, c, 0:N], self.X[:, c, 0:N], self.pp(l, kind * 8 + c), rs[:, 0:N], ALU.mult, ALU.mult,
                     [self.tX[c], trs, self.tPP], [self.tHB[c]])

    def postnorm(self, g, l, kind, MO, tMO, ps, tps):
        N = g.N
        rs = self.RSTD; trs = self.tRSTD
        self.rstd_from(ps, tps, N, rs[:, 0:N], trs)
        for c in range(8):
            self.tt("dve", MO[:, c, :], MO[:, c, :], rs[:, 0:N], ALU.mult, [tMO[c], trs], [tMO[c]])
            self.stt(self.X[:, c, 0:N], MO[:, c, :], self.pp(l, kind * 8 + c), self.X[:, c, 0:N], ALU.mult, ALU.add,
                     [tMO[c], self.tPP], [self.tX[c]])

    def proj_chunk(self, wbuf, twb, col, N, rhs_buf, rhs_toks, nk=8):
        ps, tps = self.psum()
        for k in range(nk):
            self.mm(ps[:, 0:N], wbuf[:, k, col:col + 128], rhs_buf[:, k, 0:N], [twb, rhs_toks[k]], [tps],
                    start=(k == 0), stop=(k == nk - 1))
        return ps, tps

    def layer(self, g, l):
        N = g.N
        tl = self.tL[l]
        if l == 0 and self.XN is not None:
            for c in range(8):
                self.cp(("act", "dve", "pool")[c % 3], self.X[:, c, 0:N], self.XN[:, c, 0:N], [self.tXN], [self.tX[c]])
            self.XN = None
        elif l == 0:
            src = self.xs if g.kind == "s" else self.xp[:, g.idx * TN:(g.idx + 1) * TN]
            sv = src.rearrange("(c p) t -> p c t", p=128)
            for c in range(8):
                self.ld(self.X[:, c, 0:N], sv[:, c, :], [self.tX[c]])
        self.arena_reset()
        self.SQ = [self.AB_(TN), self.AB_(TN)]; self.tSQ = toks(2)
        self.RSTD = self.AF_(TN); self.tRSTD = Tok()
        self.prenorm(g, l, 0)
        U = self.AF_(2, N); UB = self.AB_(2, N); tU = toks(2)
        XLs = self.AF_(2, g.nb, 3 + g.L); tXL = toks(2)
        GL = self.AF_(2, N); tGL = toks(2)
        wb, twb = self.getA()
        for c in range(2):
            ps, tps = self.proj_chunk(wb, twb, c * 128, N, self.HB, self.tHB)
            self.cp("act", U[:, c, :], ps[:, 0:N], [tps], [tU[c]])
            self.cp("dve", UB[:, c, :], ps[:, 0:N], [tps], [tU[c]])
        self.prefetchA()
        wb, twb = self.getA()
        for c in range(2):
            ps, tps = self.proj_chunk(wb, twb, c * 128, N, self.HB, self.tHB)
            self.cp("act", XLs[:, c, :, 3:3 + g.L], ps[:, 0:N].rearrange("p (b t) -> p b t", b=g.nb), [tps], [tXL[c]])
        self.prefetchA()
        wb, twb = self.getA()
        for c in range(2):
            ps, tps = self.proj_chunk(wb, twb, c * 128, N, self.HB, self.tHB)
            self.cp("act", GL[:, c, :], ps[:, 0:N], [tps], [tGL[c]])
        self.prefetchA()
        markF, markB = self.af, self.ab
        if g.kind == "p":
            self.s5_prompt(g, l, U, UB, tU)
        else:
            self.s5_sample(g, l, U, UB, tU)
        self.P.barrier()
        self.af, self.ab = markF, markB
        self.GT = [self.AF_(N), self.AF_(N)]; self.tGT = Tok()
        self.lru(g, l, XLs, tXL, GL, tGL)
        self.arena_reset()
        self.SQ = [self.AB_(TN), self.AB_(TN)]; self.tSQ = toks(2)
        self.gdn(g, l)
        if l == 0 and g.idx == 0:
            self.dbg("mix_%s" % g.kind, self.MIXB[:, :, 0:N], self.tMIX, [128, 8, N], bf=True)
        self.arena_reset()
        self.SQ = [self.AB_(TN), self.AB_(TN)]; self.tSQ = toks(2)
        self.RSTD = self.AF_(TN); self.tRSTD = Tok()
        MO = self.AF_(8, N); tMO = toks(8)
        if l == self.nl - 1 and self.next_g is not None:
            ng = self.next_g
            self.XN = self.AF_(8, TN); self.tXN = Tok()
            nsrc = self.xs if ng.kind == "s" else self.xp[:, ng.idx * TN:(ng.idx + 1) * TN]
            self.ld(self.XN[:, :, 0:ng.N], nsrc.rearrange("(c p) t -> p c t", p=128), [self.tXN])
        nps, tnps = self.PSN, self.tPSN
        sc = 1.0 / math.sqrt(1024.0)
        for blk in range(4):
            wb, twb = self.getA()
            for c in range(2):
                oc = blk * 2 + c
                ps, tps = self.proj_chunk(wb, twb, c * 128, N, self.MIXB, self.tMIX)
                self.cp("act", MO[:, oc, :], ps[:, 0:N], [tps], [tMO[oc]])
                sq = self.SQ[oc % 2]; tsq = self.tSQ[oc % 2]
                self.act(sq[:, 0:N], ps[:, 0:N], AF.Square, [tps], [tsq], scale=sc)
                self.mm(nps[:, 0:N], self.CB[:, CB_ONES:CB_ONES + 128], sq[:, 0:N], [tsq, self.tCB], [tnps],
                        start=(oc == 0), stop=(oc == 7))
            self.prefetchA()
        if l == 0:
            self.dbg("mo_%s%d" % (g.kind, g.idx), MO, tMO, [128, 8, N])
        self.postnorm(g, l, 1, MO, tMO, nps, tnps)
        if l == 0 and g.idx == 0:
            self.dbg("x1_%s" % g.kind, self.X[:, :, 0:N], self.tX, [128, 8, N])
        self.prenorm(g, l, 2)
        if l == 0 and g.idx == 0:
            self.dbg("hb2_%s" % g.kind, self.HB[:, :, 0:N], self.tHB, [128, 8, N], bf=True)
        self.prefetchB()
        HID = self.AB_(22, N); tH = toks(22)
        SG = [self.AF_(N), self.AF_(N)]; tSG = toks(2)
        for c0 in range(0, FFN_H, 256):
            w = 256
            wg, twg = self.getA()
            wu, twu = self.getA()
            for c in range(w // 128):
                hc = c0 // 128 + c
                pg, tpg = self.proj_chunk(wg, twg, c * 128, N, self.HB, self.tHB)
                pu, tpu = self.proj_chunk(wu, twu, c * 128, N, self.HB, self.tHB)
                sg = SG[hc % 2]; tsg = tSG[hc % 2]
                self.act(sg, pg[:, 0:N], AF.Silu, [tpg], [tsg])
                self.tt("dve", HID[:, hc, :], pu[:, 0:N], sg, ALU.mult, [tpu, tsg], [tH[hc]])
            self.prefetchA()
        FO = MO; tFO = tMO
        nps, tnps = self.PSN, self.tPSN
        for blk in range(8):
            wb, twb = self.getB()
            for c in range(1):
                oc = blk
                ps, tps = self.proj_chunk(wb, twb, 0, N, HID, tH, nk=22)
                self.prefetchB()
                self.cp("act", FO[:, oc, :], ps[:, 0:N], [tps], [tFO[oc]])
                sq = self.SQ[oc % 2]; tsq = self.tSQ[oc % 2]
                self.act(sq[:, 0:N], ps[:, 0:N], AF.Square, [tps], [tsq], scale=sc)
                self.mm(nps[:, 0:N], self.CB[:, CB_ONES:CB_ONES + 128], sq[:, 0:N], [tsq, self.tCB], [tnps],
                        start=(oc == 0), stop=(oc == 7))
        if l == 0 and g.idx == 0:
            self.dbg("hid_%s" % g.kind, HID, tH, [128, 22, N], bf=True)
            self.dbg("fo_%s" % g.kind, FO, tFO, [128, 8, N])
        self.postnorm(g, l, 3, FO, tFO, nps, tnps)
        if l == 0 and g.idx == 0:
            self.dbg("x2_%s" % g.kind, self.X[:, :, 0:N], self.tX, [128, 8, N])
        if l == self.nl - 1:
            dst = self.ys if g.kind == "s" else self.yp[:, g.idx * TN:(g.idx + 1) * TN]
            dv = dst.rearrange("(c p) t -> p c t", p=128)
            for c in range(8):
                self.st(dv[:, c, :], self.X[:, c, 0:N], [self.tX[c]])

    def gelu(self, out, x, N, r, w, gt=None, tg=None):
        gt = gt or self.GT
        tg = tg or self.tGT
        t1 = gt[0]; t2 = gt[1]
        self.act(t1[:, 0:N], x, AF.Square, r, [tg])
        self.ts("dve", t1[:, 0:N], t1[:, 0:N], 0.044715, ALU.mult, [tg], [tg], s2=1.0, op1=ALU.add)
        self.tt("dve", t1[:, 0:N], t1[:, 0:N], x, ALU.mult, r + [tg], [tg])
        self.act(t2[:, 0:N], t1[:, 0:N], AF.Sigmoid, [tg], [tg], scale=1.5957691216057308)
        self.tt("dve", out, t2[:, 0:N], x, ALU.mult, r + [tg], w + [tg])

    def s5_out(self, g, l, U, tU, XR, XI, tXR, tXI=None):
        tXI = tXI or tXR
        N = g.N
        tl = self.tL[l]
        self.GT = [self.AF_(N), self.AF_(N)]; self.tGT = Tok()
        Y1 = self.AF_(2, N); Y1B = self.AB_(2, N); tY = toks(2)
        for uc in range(2):
            ps, tps = self.psum()
            for j in range(4):
                sc = uc * 4 + j
                self.mm(ps[:, 0:N], self.CRE[l][:, sc, :], XR[:, sc, :], [tl, tXR[sc]], [tps], start=(j == 0), stop=False)
                self.mm(ps[:, 0:N], self.NCIM[l][:, sc, :], XI[:, sc, :], [tl, tXI[sc]], [tps], start=False, stop=(j == 3))
            self.stt(Y1[:, uc, :], U[:, uc, :], self.pp(l, 56 + uc), ps[:, 0:N], ALU.mult, ALU.add,
                     [tU[uc], tps, self.tPP], [tY[uc]])
            self.gelu(Y1[:, uc, :], Y1[:, uc, :], N, [tY[uc]], [tY[uc]])
            self.cp("act", Y1B[:, uc, :], Y1[:, uc, :], [tY[uc]], [tY[uc]])
        for oc in range(2):
            ps, tps = self.psum()
            for k in range(2):
                self.mm(ps[:, 0:N], self.WGLU[l][:, k, oc * 128:(oc + 1) * 128], Y1B[:, k, :], [tl, tY[k]], [tps],
                        start=(k == 0), stop=(k == 1))
            sg = self.GT[0]
            self.act(sg[:, 0:N], ps[:, 0:N], AF.Sigmoid, [tps, self.tPP], [self.tGT], bias=self.pp(l, 58 + oc))
            self.tt("dve", self.MIXB[:, oc, 0:N], sg[:, 0:N], Y1[:, oc, :], ALU.mult, [self.tGT, tY[oc]], [self.tMIX[oc]])

    def s5_prompt(self, g, l, U, UB, tU):
        N = g.N
        NS = N // SEG
        tl = self.tL[l]
        C = self.S5C[l]
        XR = self.AB_(8, N); XI = self.AB_(8, N); tXR = toks(8); tXI = toks(8)
        sets = [([self.AF_(N) for _ in range(6)], toks(6)) for _ in range(2)]
        tI = self.tS5I[l]
        I = self.S5I[l]
        last = (g.idx == NT - 1)
        v = lambda ap: ap.rearrange("p (s k) -> p s k", s=NS)
        T4 = [self.AF_(16) for _ in range(2)]; tT4 = toks(2)

        def chunk(sc):
            uc = sc // 4
            (PRs, PIs, A, B, C2, D), (tPR, tPI, tA, tB, tC2, tD) = sets[sc % 2]
            t4 = T4[sc % 2]; tt4 = tT4[sc % 2]
            pr, tpr = self.psum()
            pi, tpi = self.psum()
            self.mm(pr[:, 0:N], self.BRE[l][:, sc, :], UB[:, uc, :], [tl, tU[uc]], [tpr])
            self.mm(pi[:, 0:N], self.BIM[l][:, sc, :], UB[:, uc, :], [tl, tU[uc]], [tpi])
            self.cp("act", PRs, pr[:, 0:N], [tpr], [tPR])
            self.cp("act", PIs, pi[:, 0:N], [tpi], [tPI])
            tc_, ts_ = self.TC[l][:, sc, :], self.TS[l][:, sc, :]
            TCb = tc_.unsqueeze(1).to_broadcast([128, NS, SEG]); TSb = ts_.unsqueeze(1).to_broadcast([128, NS, SEG])
            rho = C[:, 0, sc:sc + 1]
            self.tt("dve", v(A), v(PRs), TCb, ALU.mult, [tPR, tl], [tA])
            self.tt("dve", v(B), v(PIs), TSb, ALU.mult, [tPI, tl], [tB])
            self.tt("dve", v(C2), v(PIs), TCb, ALU.mult, [tPI, tl], [tC2])
            self.tt("dve", v(D), v(PRs), TSb, ALU.mult, [tPR, tl], [tD])
            self.tt("dve", A, A, B, ALU.add, [tA, tB], [tA])
            self.tt("pool", C2, C2, D, ALU.subtract, [tC2, tD], [tC2])
            for s_ in range(NS):
                cols = slice(s_ * SEG, (s_ + 1) * SEG)
                self.P.op("dve", lambda e, o=PRs[:, cols], d1=A[:, cols], rho=rho, ii=I[:, 0, sc:sc + 1]: e.tensor_tensor_scan(
                    out=o, data0=rho.to_broadcast([128, SEG]), data1=d1, initial=ii, op0=ALU.mult, op1=ALU.add),
                    [tA, tI[sc], tl], [tPR])
                self.P.op("dve", lambda e, o=PIs[:, cols], d1=C2[:, cols], rho=rho, ii=I[:, 1, sc:sc + 1]: e.tensor_tensor_scan(
                    out=o, data0=rho.to_broadcast([128, SEG]), data1=d1, initial=ii, op0=ALU.mult, op1=ALU.add),
                    [tC2, tI[sc], tl], [tPI])
                e0 = s_ * SEG + SEG - 1
                we_r, we_i = PRs[:, e0:e0 + 1], PIs[:, e0:e0 + 1]
                if last and s_ == NS - 1:
                    cr, ci, nci = tc_[:, SEG - 1:SEG], ts_[:, SEG - 1:SEG], self.NTSL[l][:, sc:sc + 1]
                else:
                    cr, ci, nci = C[:, 1, sc:sc + 1], C[:, 2, sc:sc + 1], C[:, 5, sc:sc + 1]
                self.act(t4[:, 0:1], we_i, AF.Copy, [tPI, tl], [tt4], scale=nci)
                self.act(I[:, 0, sc:sc + 1], we_r, AF.Identity, [tPR, tt4, tl], [tI[sc]], scale=cr, bias=t4[:, 0:1])
                self.act(t4[:, 1:2], we_r, AF.Copy, [tPR, tl], [tt4], scale=ci)
                self.act(I[:, 1, sc:sc + 1], we_i, AF.Identity, [tPI, tt4, tl], [tI[sc]], scale=cr, bias=t4[:, 1:2])
            self.tt("dve", v(B), v(PRs), TCb, ALU.mult, [tPR, tl], [tB])
            self.tt("pool", v(D), v(PIs), TSb, ALU.mult, [tPI, tl], [tD])
            self.tt("dve", XR[:, sc, :], B, D, ALU.subtract, [tB, tD], [tXR[sc]])
            self.tt("pool", v(A), v(PRs), TSb, ALU.mult, [tPR, tl], [tA])
            self.tt("dve", v(C2), v(PIs), TCb, ALU.mult, [tPI, tl], [tC2])
            self.tt("pool", XI[:, sc, :], A, C2, ALU.add, [tA, tC2], [tXI[sc]])

        for sc in range(0, 8, 2):
            o0 = self.capture(lambda: chunk(sc), self.PS_F)
            o1 = self.capture(lambda: chunk(sc + 1), self.PS_C)
            self.P.replay(o0, o1, ratio=1.0)
        if last:
            self.st(self.o_s5[l].rearrange("a p c -> p a c"), I[:, :, 0:8], tI)
        if l == 0 and g.idx == 0:
            self.dbg("xr_p", XR, tXR, [128, 8, N], bf=True)
            self.dbg("xi_p", XI, tXI, [128, 8, N], bf=True)
        self.s5_out(g, l, U, tU, XR, XI, tXR, tXI)

    def s5_sample(self, g, l, U, UB, tU):
        N = g.N
        tl = self.tL[l]
        C = self.S5C[l]
        XR = self.AB_(8, N); XI = self.AB_(8, N); tXR = toks(8)
        BR = self.AF_(8, N); BI = self.AF_(8, N); tB = toks(8)
        ST = self.AF_(2, 8, NSB); tST = Tok()
        T = [self.AF_(8, NSB) for _ in range(4)]; tT = Tok()
        self.ld(ST, self.i_s5[l].rearrange("a p c b -> p a c b"), [tST])
        for sc in range(8):
            uc = sc // 4
            pr, tpr = self.psum()
            pi, tpi = self.psum()
            self.mm(pr[:, 0:N], self.BRE[l][:, sc, :], UB[:, uc, :], [tl, tU[uc]], [tpr])
            self.mm(pi[:, 0:N], self.BIM[l][:, sc, :], UB[:, uc, :], [tl, tU[uc]], [tpi])
            self.cp("act", BR[:, sc, :], pr[:, 0:N], [tpr], [tB[sc]])
            self.cp("act", BI[:, sc, :], pi[:, 0:N], [tpi], [tB[sc]])
        arb = C[:, 3, 0:8].unsqueeze(2).to_broadcast([128, 8, NSB])
        aib = C[:, 4, 0:8].unsqueeze(2).to_broadcast([128, 8, NSB])
        BRv = BR.rearrange("p c (b t) -> p c b t", t=SL); BIv = BI.rearrange("p c (b t) -> p c b t", t=SL)
        XRv = XR.rearrange("p c (b t) -> p c b t", t=SL); XIv = XI.rearrange("p c (b t) -> p c b t", t=SL)
        xr, xi = ST[:, 0], ST[:, 1]
        for t in range(SL):
            self.tt("dve", T[0], xr, arb, ALU.mult, [tST, tl], [tT])
            self.tt("dve", T[1], xi, aib, ALU.mult, [tST, tl], [tT])
            self.tt("dve", T[2], xr, aib, ALU.mult, [tST, tl], [tT])
            self.tt("dve", T[3], xi, arb, ALU.mult, [tST, tl], [tT])
            self.tt("dve", T[0], T[0], T[1], ALU.subtract, [tT], [tT])
            self.tt("dve", T[2], T[2], T[3], ALU.add, [tT], [tT])
            self.tt("dve", xr, T[0], BRv[:, :, :, t], ALU.add, [tT] + tB, [tST])
            self.tt("dve", xi, T[2], BIv[:, :, :, t], ALU.add, [tT] + tB, [tST])
            self.cp("dve", XRv[:, :, :, t], xr, [tST], tXR)
            self.cp("dve", XIv[:, :, :, t], xi, [tST], tXR)
        self.st(self.os_s5[l].rearrange("a p c b -> p a c b"), ST, [tST])
        self.s5_out(g, l, U, tU, XR, XI, tXR)

    def conv(self, out3, stg, wbase, l, c, L, r, w, bias=None):
        if bias is None:
            self.ts("dve", out3, stg[:, :, 0:L], self.pp(l, wbase + c * 4), ALU.mult, r + [self.tPP], w)
        else:
            self.ts("dve", out3, stg[:, :, 0:L], self.pp(l, wbase + c * 4), ALU.mult, r + [self.tPP], w, s2=bias, op1=ALU.add)
        for j in range(1, 4):
            self.stt(out3, stg[:, :, j:j + L], self.pp(l, wbase + c * 4 + j), out3, ALU.mult, ALU.add, r + w + [self.tPP], w)

    def lru(self, g, l, XLs, tXL, GL, tGL):
        N, nb, L = g.N, g.nb, g.L
        tl = self.tL[l]
        XC = self.AF_(2, N); XCB = self.AB_(2, N); tXC = toks(2)
        RG = self.AF_(2, N); GI = self.AF_(2, N); tG = toks(2)
        A = self.AF_(N); M_ = self.AF_(N); B_ = self.AF_(N); H = self.AF_(N); tA = Tok()
        HS = self.AF_(2, NSB); tHS = Tok()
        v3 = lambda ap: ap.rearrange("p (b t) -> p b t", b=nb)
        if g.kind == "s":
            LH = self.AF_(2, NSB, 3); LO = self.AF_(2, NSB, 3); tLH = Tok(); tLO = Tok()
            self.ld(LH, self.i_lruconv[l], [tLH])
        for c in range(2):
            if g.kind == "p":
                self.cp("dve", XLs[:, c, 0, 0:3], self.HXL[l][:, c, 0:3], [self.tHXL[l]], [tXL[c]])
            else:
                self.cp("dve", XLs[:, c, :, 0:3], LH[:, c], [tLH], [tXL[c]])
        if g.kind == "s":
            self.ld(HS, self.i_lruh[l], [tHS])
        for c in range(2):
            self.conv(v3(XC[:, c, :]), XLs[:, c], 60, l, c, L, [tXL[c]], [tXC[c]], bias=self.pp(l, 68 + c))
            self.cp("act", XCB[:, c, :], XC[:, c, :], [tXC[c]], [tXC[c]])
            if g.kind == "p":
                if g.idx == NT - 1:
                    pass
            else:
                self.cp("pool", LO[:, c], XLs[:, c, :, L:L + 3], [tXL[c]], [tLO])
        if g.kind == "s":
            self.st(self.os_lruconv[l], LO, [tLO])
        if g.kind == "p":
            for c in range(2):
                self.cp("dve", self.HXL[l][:, c, 0:3], XLs[:, c, 0, L:L + 3], [tXL[c]], [self.tHXL[l]])
            if g.idx == NT - 1:
                self.st(self.o_lruconv[l], self.HXL[l][:, :, 0:3], [self.tHXL[l]])
        for c in range(2):
            for a in range(2):
                ps, tps = self.psum()
                self.mm(ps[:, 0:N], self.WAX[l][:, a, c, :], XCB[:, c, :], [tl, tXC[c]], [tps])
                dst = RG if a == 0 else GI
                self.act(dst[:, c, :], ps[:, 0:N], AF.Sigmoid, [tps, self.tPP], [tG[c]], bias=self.pp(l, 70 + a * 2 + c))
        sets = [(A, M_, B_, H, self.GT, tA, self.tGT),
                (self.AF_(N), self.AF_(N), self.AF_(N), self.AF_(N), [self.AF_(N), self.AF_(N)], Tok(), Tok())]

        def chunk(c):
            A, M_, B_, H, gt, tA, tg = sets[c]
            self.act(A[:], RG[:, c, :], AF.Exp, [tG[c], tl], [tA], scale=self.LC[l][:, c:c + 1])
            self.act(M_[:], A[:], AF.Square, [tA], [tA])
            self.act(M_[:], M_[:], AF.Sqrt, [tA], [tA], scale=-1.0, bias=1.0)
            if g.kind == "p" and g.idx == 0:
                self.memset("dve", M_[:, 0:1], 1.0, [tA])
            self.tt("dve", B_[:], GI[:, c, :], XC[:, c, :], ALU.mult, [tG[c], tXC[c]], [tA])
            self.tt("dve", B_[:], B_[:], M_[:], ALU.mult, [tA], [tA])
            if g.kind == "p":
                self.P.op("dve", lambda e, c=c: e.tensor_tensor_scan(
                    out=H[:], data0=A[:], data1=B_[:], initial=self.HL[l][:, c:c + 1], op0=ALU.mult, op1=ALU.add),
                    [tA, self.tHLc[l][c]], [tA])
                self.cp("dve", self.HL[l][:, c:c + 1], H[:, N - 1:N], [tA], [self.tHLc[l][c]])
            else:
                Av, Bv, Hv = v3(A[:]), v3(B_[:]), v3(H[:])
                hp = HS[:, c, :]
                for t in range(L):
                    self.tt("dve", Hv[:, :, t], Av[:, :, t], hp, ALU.mult, [tA, tHSc[c]], [tA])
                    self.tt("dve", Hv[:, :, t], Hv[:, :, t], Bv[:, :, t], ALU.add, [tA], [tA])
                    hp = Hv[:, :, t]
                self.cp("dve", HS[:, c, :], Hv[:, :, L - 1], [tA], [tHSc[c]])
            self.gelu(GL[:, c, :], GL[:, c, :], N, [tGL[c]], [tGL[c]], gt=gt, tg=tg)
            self.tt("dve", self.MIXB[:, 2 + c, 0:N], H[:], GL[:, c, :], ALU.mult, [tA, tGL[c]], [self.tMIX[2 + c]])

        tHSc = [tHS, Tok()]
        if g.kind == "s":
            tHSc[1].w = tHS.w
        o0 = self.capture(lambda: chunk(0), self.PS_F)
        o1 = self.capture(lambda: chunk(1), self.PS_C)
        self.P.replay(o0, o1, ratio=1.0)
        tHS = None
        if g.kind == "p":
            if g.idx == NT - 1:
                self.st(self.o_lruh[l], self.HL[l][:, 0:2], self.tHLc[l])
        else:
            self.st(self.os_lruh[l], HS, tHSc)

    def psb(self):
        i = self.psbn
        self.psbn = (self.psbn + 1) % 2
        return self.PSB[:, i, :], self.tPSB[i]

    def gdn(self, g, l):
        N, nb, L, Pn = g.N, g.nb, g.L, g.P
        tl = self.tL[l]
        smp = (g.kind == "s")
        QT = self.AB_(4, N); KT = self.AB_(4, N); VT = self.AB_(4, N); SZ = self.AB_(4, N)
        tQ = toks(4); tK = toks(4); tV = toks(4); tZ = toks(4)
        STG = [self.AF_(nb, 3 + L) for _ in range(2)]; tSTG = toks(2)
        CV = [self.AF_(N) for _ in range(4)]; tCV = toks(4)
        RN = [self.AF_(N) for _ in range(4)]; tRN = toks(4)
        v3 = lambda ap: ap.rearrange("p (b t) -> p b t", b=nb)
        n = 0
        if smp:
            HALL = self.AF_(12, NSB, 3); HOUT = self.AF_(12, NSB, 3); tHALL = Tok(); tHOUT = Tok()
            self.ld(HALL, self.i_gdnconv[l], [tHALL])
        for grp in range(3):
            for h in range(4):
                c = grp * 4 + h
                if h % 2 == 0:
                    if h:
                        self.prefetchA()
                    wb, twb = self.getA()
                ps, tps = self.proj_chunk(wb, twb, (h % 2) * 128, N, self.HB, self.tHB)
                stg = STG[n % 2]; tst = tSTG[n % 2]
                n += 1
                cv = CV[h]; tcv = tCV[h]
                self.cp("act", stg[:, :, 3:3 + L], v3(ps[:, 0:N]), [tps], [tst])
                if smp:
                    self.cp("dve", stg[:, :, 0:3], HALL[:, c], [tHALL], [tst])
                    self.cp("pool", HOUT[:, c], stg[:, :, L:L + 3], [tst], [tHOUT])
                else:
                    self.cp("dve", stg[:, 0, 0:3], self.HQKV[l][:, c, 0:3], [self.tHQ[l]], [tst])
                    self.cp("dve", self.HQKV[l][:, c, 0:3], stg[:, 0, L:L + 3], [tst], [self.tHQ[l]])
                self.conv(v3(cv), stg, 76, l, c, L, [tst], [tcv])
            self.prefetchA()
            if grp == 2:
                for h in range(4):
                    self.act(VT[:, h, :], CV[h], AF.Silu, [tCV[h]], [tV[h]])
                continue
            for h in range(4):
                self.act(CV[h], CV[h], AF.Silu, [tCV[h]], [tCV[h]])
            p2s = []
            for h in range(4):
                sq = self.SQ[h % 2]; tsq = self.tSQ[h % 2]
                self.act(sq[:, 0:N], CV[h], AF.Square, [tCV[h]], [tsq])
                p2, tp2 = self.psum()
                self.mm(p2[:, 0:N], self.CB[:, CB_ONES:CB_ONES + 128], sq[:, 0:N], [tsq, self.tCB], [tp2])
                p2s.append((p2, tp2))
            for h in range(4):
                self.act(RN[h], p2s[h][0][:, 0:N], AF.Ln, [p2s[h][1], self.tC], [tRN[h]], bias=self.EPS[:, 0:1])
            for h in range(4):
                self.act(RN[h], RN[h], AF.Exp, [tRN[h]], [tRN[h]], scale=-0.5)
            dst, td = (QT, tQ) if grp == 0 else (KT, tK)
            for h in range(4):
                self.stt(dst[:, h, :], CV[h], (128.0 ** -0.5) if grp == 0 else 1.0, RN[h], ALU.mult, ALU.mult,
                         [tCV[h], tRN[h]], [td[h]])
        if smp:
            self.st(self.os_gdnconv[l], HOUT, [tHOUT])
        if (not smp) and g.idx == NT - 1:
            self.st(self.o_gdnconv[l], self.HQKV[l][:, :, 0:3], [self.tHQ[l]])
        for h in range(4):
            if h % 2 == 0:
                if h:
                    self.prefetchA()
                wb, twb = self.getA()
            ps, tps = self.proj_chunk(wb, twb, (h % 2) * 128, N, self.HB, self.tHB)
            self.act(SZ[:, h, :], ps[:, 0:N], AF.Silu, [tps], [tZ[h]])
        self.prefetchA()
        nblk = g.nblk
        GTM = self.AF_(nblk, 4); BTM = self.AF_(nblk, 4); tGB = Tok()
        ps, tps = self.psum()
        for blk in range(nblk):
            for k in range(8):
                self.mm(ps[0:Pn, blk * 8:blk * 8 + 8], self.HB[:, k, blk * Pn:(blk + 1) * Pn], self.WAB[l][:, k, :],
                        [self.tHB[k], tl], [tps], start=(k == 0), stop=(k == 7))
        pv = ps[0:Pn, 0:nblk * 8].rearrange("p (b c) -> p b c", c=8)
        G_, B_ = GTM[0:Pn], BTM[0:Pn]
        dtb = self.GR[l][0:Pn, 1, 0:4].unsqueeze(1).to_broadcast([Pn, nblk, 4])
        nea = self.GR[l][0:Pn, 0, 0:4].unsqueeze(1).to_broadcast([Pn, nblk, 4])
        self.tt("dve", G_, pv[:, :, 0:4], dtb, ALU.add, [tps, tl], [tGB])
        self.act(G_, G_, AF.Exp, [tGB], [tGB])
        self.act(G_, G_, AF.Ln, [tGB], [tGB], bias=1.0)
        self.tt("dve", G_, G_, nea, ALU.mult, [tGB, tl], [tGB])
        self.act(B_, pv[:, :, 4:8], AF.Sigmoid, [tps], [tGB])
        f4 = lambda: self.AF_(4, 128)
        b4 = lambda: self.AB_(4, 128)
        nset = 2 if nblk > 1 else 1
        GBc = f4(); D1 = f4(); R1 = f4(); EG = f4(); OT = f4(); RT = f4()
        DECB = b4(); DECT = b4(); Mm = b4(); Nm = b4(); VN = b4(); Ub = b4(); XmL = b4(); XmU = b4()
        Dl = [b4(), b4()]; Du = [b4(), b4()]
        VBs = [f4() for _ in range(nset)]
        SMs = [self.AF_(8, 16) for _ in range(nset)]
        KDs = [b4()] + [self.AFB_(4, 128) for _ in range(nset - 1)]
        ATTs = [b4()] + [self.AFB_(4, 128) for _ in range(nset - 1)]
        QDs = [b4()] + [self.AFB_(4, 128) for _ in range(nset - 1)]
        TTs = [self.AFB_(4, 128) for _ in range(nset)]
        tbs = toks(nset); tTTs = toks(nset)
        tD = Tok(); tM = Tok(); tDl = toks(2); tDu = toks(2); tX = toks(2); tUb = toks(4); tVN = toks(4); tOT = Tok()
        idb = self.CB[0:Pn, CB_ID:CB_ID + Pn]
        idbh = idb.unsqueeze(1).to_broadcast([Pn, 4, Pn])
        if smp:
            TRI, SEGM, MSTR, MINCT = (self.cf(o_, Pn, Pn) for o_ in (CF_TRIS, CF_SEGS, CF_MSTRS, CF_MINCTS))
            levels = [1, 2]
        else:
            TRI, SEGM, MSTR, MINCT = (self.cf(o_, Pn, Pn) for o_ in (CF_TRI, CF_ONES, CF_MSTR, CF_MINCT))
            levels = NLEV
        bh = lambda ap: ap.unsqueeze(1).to_broadcast([Pn, 4, Pn])
        h4 = lambda p_: p_[0:Pn, 0:4 * Pn].rearrange("p (h j) -> p h j", h=4)

        def front(blk):
            par = blk % nset
            VB, SM, KD, ATT, QD, TTf, tb, tTT = VBs[par], SMs[par], KDs[par], ATTs[par], QDs[par], TTs[par], tbs[par], tTTs[par]
            cols = slice(blk * Pn, (blk + 1) * Pn)
            gcol = GTM[0:Pn, blk, :]; bcol = BTM[0:Pn, blk, :]
            gc, gl, kdf, nbeg, egt, gam = (SM[0:Pn, i, 0:4] for i in range(6))
            ps, tps = self.psum()
            self.mm(ps[0:Pn, 0:4], TRI, gcol, [self.tCF, tGB], [tps])
            self.mm(ps[0:Pn, 4:8], SEGM, gcol, [self.tCF, tGB], [tps])
            self.cp("dve", gc, ps[0:Pn, 0:4], [tps], [tb])
            self.cp("dve", gl, ps[0:Pn, 4:8], [tps], [tb])
            self.cp("dve", GBc[0:Pn], gcol.unsqueeze(2).to_broadcast([Pn, 4, 128]), [tGB], [tD])
            pg, tpg = self.psum()
            pgv = pg[:, 0:4 * Pn].rearrange("p (h j) -> p h j", h=4)
            for h in range(4):
                self.mm(pgv[:, h, :], GBc[0:Pn, h, :], TRI, [tD, self.tCF], [tpg])
            d1 = D1[0:Pn, :, 0:Pn]; r1 = R1[0:Pn, :, 0:Pn]
            self.tt("dve", d1, pgv[0:Pn], gc.unsqueeze(2).to_broadcast([Pn, 4, Pn]), ALU.subtract, [tpg, tb], [tD])
            self.ts("dve", d1, d1, -60.0, ALU.max, [tD], [tD], s2=60.0, op1=ALU.min)
            eg = EG[:, :, 0:Pn]
            self.act(eg, pgv, AF.Exp, [tpg], [tD])
            BM = GBc[0:Pn, :, 0:Pn]
            self.tt("pool", BM, bh(MSTR), bcol.unsqueeze(2).to_broadcast([Pn, 4, Pn]), ALU.mult, [tpg, tGB, self.tCF], [tD])
            self.act(r1, d1, AF.Exp, [tD], [tD], scale=-1.0)
            self.act(d1, d1, AF.Exp, [tD], [tD])
            decb = DECB[0:Pn, :, 0:Pn]; dect = DECT[0:Pn, :, 0:Pn]
            self.stt(decb, r1, 1.0, BM, ALU.min, ALU.mult, [tD], [tD])
            self.stt(dect, d1, 1.0, bh(MINCT), ALU.min, ALU.mult, [tD, self.tCF], [tD])
            self.tt("dve", kdf, gl, gc, ALU.subtract, [tb], [tb])
            self.act(kdf, kdf, AF.Exp, [tb], [tb])
            self.act(egt, gc, AF.Exp, [tb], [tb])
            self.stt(nbeg, egt, -1.0, bcol, ALU.mult, ALU.mult, [tb, tGB], [tb])
            if not smp:
                self.act(gam, gl, AF.Exp, [tb], [tb])
            pb, tpb = self.psb()
            pbv = pb[0:Pn, :].rearrange("p (h d) -> p h d", h=4)
            for h in range(4):
                self.tr(pbv[:, h, :], VT[:, h, cols], self.CB[:, CB_ID:CB_ID + 128], [tV[h], self.tCB], [tpb])
            self.tt("dve", VB[0:Pn], pbv, bcol.unsqueeze(2).to_broadcast([Pn, 4, 128]), ALU.mult, [tpb, tGB], [tb])
            pb, tpb = self.psb()
            pbv = pb[0:Pn, :].rearrange("p (h d) -> p h d", h=4)
            for h in range(4):
                self.tr(pbv[:, h, :], KT[:, h, cols], self.CB[:, CB_ID:CB_ID + 128], [tK[h], self.tCB], [tpb])
            self.tt("dve", KD[0:Pn], pbv, kdf.unsqueeze(2).to_broadcast([Pn, 4, 128]), ALU.mult, [tpb, tb], [tb])
            pk, tpk = self.psum()
            pkv = h4(pk)
            for h in range(4):
                self.mm(pkv[:, h, :], KT[:, h, cols], KT[:, h, cols], [tK[h]], [tpk])
            mm_ = Mm[0:Pn, :, 0:Pn]; nm_ = Nm[0:Pn, :, 0:Pn]
            self.tt("dve", mm_, pkv, decb, ALU.mult, [tpk, tD], [tM])
            pb, tpb = self.psb()
            pbv = pb[0:Pn, 0:4 * Pn].rearrange("p (h d) -> p h d", h=4)
            for h in range(4):
                self.tr(pbv[:, h, :], mm_[:, h, :], idb, [tM, self.tCB], [tpb])
            self.cp("act", nm_, pbv, [tpb], [tM])
            pq, tpq = self.psum()
            pqv = h4(pq)
            for h in range(4):
                self.mm(pqv[:, h, :], KT[:, h, cols], QT[:, h, cols], [tK[h], tQ[h]], [tpq])
            att = ATT[0:Pn, :, 0:Pn]
            self.tt("dve", att, pqv, dect, ALU.mult, [tpq, tD], [tb])
            qd = QD[:, :, 0:Pn]
            self.tt("pool", qd, QT[:, :, cols], eg, ALU.mult, tQ + [tD], [tb])
            cur = 0
            dl = Dl[0][0:Pn, :, 0:Pn]; du = Du[0][0:Pn, :, 0:Pn]
            nml = lambda i: bh(self.CB[0:Pn, CB_NML + i * 128:CB_NML + i * 128 + Pn])
            nmu = lambda i: bh(self.CB[0:Pn, CB_NMU + i * 128:CB_NMU + i * 128 + Pn])
            self.tt("pool", dl, mm_, nml(0), ALU.mult, [tM, self.tCB], [tDl[0]])
            self.tt("pool", dl, dl, idbh, ALU.add, [tDl[0], self.tCB], [tDl[0]])
            self.tt("dve", du, nm_, nmu(0), ALU.mult, [tM, self.tCB], [tDu[0]])
            self.tt("dve", du, du, idbh, ALU.add, [tDu[0], self.tCB], [tDu[0]])
            for li in range(1, len(levels)):
                lastlev = (li == len(levels) - 1)
                dl = Dl[cur][0:Pn, :, 0:Pn]; du = Du[cur][0:Pn, :, 0:Pn]
                dl2 = Dl[1 - cur][0:Pn, :, 0:Pn]
                du2 = TTf[0:Pn, :, 0:Pn] if lastlev else Du[1 - cur][0:Pn, :, 0:Pn]
                tdl, tdu, tdl2 = tDl[cur], tDu[cur], tDl[1 - cur]
                tdu2 = tTT if lastlev else tDu[1 - cur]
                xl = XmL[0:Pn, :, 0:Pn]; xu = XmU[0:Pn, :, 0:Pn]
                p1, tp1 = self.psum()
                p1v = h4(p1)
                for h in range(4):
                    self.mm(p1v[:, h, :], mm_[:, h, :], du[:, h, :], [tM, tdu], [tp1])
                self.tt("dve", xu, p1v, nmu(li), ALU.mult, [tp1, self.tCB], [tX[1]])
                if not lastlev:
                    p2, tp2 = self.psum()
                    p2v = h4(p2)
                    for h in range(4):
                        self.mm(p2v[:, h, :], nm_[:, h, :], dl[:, h, :], [tM, tdl], [tp2])
                    self.tt("dve", xl, p2v, nml(li), ALU.mult, [tp2, self.tCB], [tX[0]])
                p3, tp3 = self.psum()
                p3v = h4(p3)
                for h in range(4):
                    self.mm(p3v[:, h, :], dl[:, h, :], xu[:, h, :], [tdl, tX[1]], [tp3], start=True, stop=False)
                    self.mm(p3v[:, h, :], idb, du[:, h, :], [self.tCB, tdu], [tp3], start=False, stop=True)
                self.cp("act", du2, p3v, [tp3], [tdu2])
                if not lastlev:
                    p4, tp4 = self.psum()
                    p4v = h4(p4)
                    for h in range(4):
                        self.mm(p4v[:, h, :], du[:, h, :], xl[:, h, :], [tdu, tX[0]], [tp4], start=True, stop=False)
                        self.mm(p4v[:, h, :], idb, dl[:, h, :], [self.tCB, tdl], [tp4], start=False, stop=True)
                    self.cp("act", dl2, p4v, [tp4], [tdl2])
                cur = 1 - cur

        def chain(blk):
            par = blk % nset
            VB, SM, KD, ATT, QD, TTf, tb, tTT = VBs[par], SMs[par], KDs[par], ATTs[par], QDs[par], TTs[par], tbs[par], tTTs[par]
            cols = slice(blk * Pn, (blk + 1) * Pn)
            gcol = GTM[0:Pn, blk, :]
            gc, gl, kdf, nbeg, egt, gam = (SM[0:Pn, i, 0:4] for i in range(6))
            att = ATT[0:Pn, :, 0:Pn]; qd = QD[:, :, 0:Pn]
            TT_ = TTf[0:Pn, :, 0:Pn]
            if smp:
                self.gdn_state_sample(g, l, KT, tK, KD, tb, TT_, tTT, VB, nbeg, qd, att, gcol, OT, tOT, Ub, tUb, VN, tVN, tGB)
            else:
                S_f, S_b, tS = self.SF[l], self.SBF[l], self.tS[l]
                v4 = lambda p_: p_[:, :].rearrange("p (h j) -> p h j", h=4)
                pks, tpks = self.psum()
                for h in range(4):
                    self.mm(v4(pks)[:, h, :], KT[:, h, cols], S_b[:, h, :], [tK[h], tS[h]], [tpks])
                for h in range(4):
                    self.stt(Ub[:, h, :], v4(pks)[:, h, :], nbeg[:, h:h + 1], VB[:, h, :], ALU.mult, ALU.add, [tpks, tb], [tUb[h]])
                pvn, tpvn = self.psum()
                for h in range(4):
                    self.mm(v4(pvn)[:, h, :], TT_[:, h, :], Ub[:, h, :], [tTT, tUb[h]], [tpvn])
                self.cp("act", VN[:, :, :], v4(pvn), [tpvn], tVN)
                pot, tpot = self.psum()
                for h in range(4):
                    self.mm(v4(pot)[:, h, :], S_b[:, h, :], qd[:, h, :], [tS[h], tb], [tpot], start=True, stop=False)
                    self.mm(v4(pot)[:, h, :], VN[:, h, :], att[:, h, :], [tVN[h], tb], [tpot], start=False, stop=True)
                self.cp("act", OT[:, :, 0:Pn], v4(pot), [tpot], [tOT])
                pss, tpss = self.psum()
                for h in range(4):
                    self.mm(v4(pss)[:, h, :], KD[:, h, :], VN[:, h, :], [tb, tVN[h]], [tpss])
                for h in range(4):
                    self.stt(S_f[:, h, :], S_f[:, h, :], gam[:, h:h + 1], v4(pss)[:, h, :], ALU.mult, ALU.add, [tb, tpss], [tS[h]])
                self.cp("pool", S_b[:, :, :], S_f[:, :, :], tS, tS)
            ot = OT[:, :, 0:Pn]
            sq = self.SQ[0]; tsq = self.tSQ[0]
            sqv = sq[:, 0:4 * Pn].rearrange("p (h j) -> p h j", h=4)
            self.act(sqv, ot, AF.Square, [tOT], [tsq], scale=1.0 / math.sqrt(128.0))
            pn, tpn = self.psum()
            self.mm(pn[:, 0:4 * Pn], self.CB[:, CB_ONES:CB_ONES + 128], sq[:, 0:4 * Pn], [tsq, self.tCB], [tpn])
            rt = RT[:, :, 0:Pn]
            self.act(rt, pn[:, 0:4 * Pn].rearrange("p (h j) -> p h j", h=4), AF.Ln, [tpn, self.tC], [tOT], bias=self.EPS[:, 0:1])
            self.act(rt, rt, AF.Exp, [tOT], [tOT], scale=-0.5)
            self.stt(ot, ot, self.pp(l, 124), rt, ALU.mult, ALU.mult, [tOT, self.tPP], [tOT])
            self.tt("dve", self.MIXB[:, 4:8, cols], ot, SZ[:, :, cols], ALU.mult, [tOT] + tZ, self.tMIX[4:8])

        if nblk == 1:
            front(0)
            chain(0)
        else:
            self.P.replay(self.capture(lambda: front(0), self.PS_F))
            for blk in range(nblk):
                c_ops = self.capture(lambda: chain(blk), self.PS_C)
                f_ops = self.capture(lambda: front(blk + 1), self.PS_F) if blk + 1 < nblk else []
                self.P.replay(f_ops, c_ops, ratio=(len(f_ops) / max(1, len(c_ops))))
        if (not smp) and g.idx == NT - 1:
            self.st(self.o_gdnS[l].rearrange("h d v -> d h v"), self.SF[l][:], self.tS[l])

    def gdn_state_sample(self, g, l, KT, tK, KD, tb, TT_, tTT, VB, nbeg, qd, att, gcol, OT, tOT, Ub, tUb, VN, tVN, tGB):
        Pn = 64
        S0 = [self.AF_(NSB, 128) for _ in range(2)]; tS0 = toks(2)
        S0B = [self.AB_(NSB, 128) for _ in range(2)]
        SN_ = [self.AF_(NSB, 128) for _ in range(1)]; tSN = toks(1)
        KTm = self.AB_(NSB, Pn); KDm = self.AB_(NSB, 128); tKm = Tok()
        GSI = self.AF_(4, NSB); GAM = self.AF_(4, NSB); tGm = Tok()
        OTB = self.AF_(4, Pn)
        sind = self.cf(CF_SIND, Pn, NSB)
        self.tt("dve", GSI[0:Pn], gcol.unsqueeze(2).to_broadcast([Pn, 4, NSB]), sind.unsqueeze(1).to_broadcast([Pn, 4, NSB]),
                ALU.mult, [self.tCF, tGB], [tGm])
        ps, tps = self.psum()
        self.mm(ps[:, 0:4 * NSB], self.cf(CF_ONES, Pn, 128), GSI[0:Pn].rearrange("p h b -> p (h b)"), [tGm, self.tCF], [tps])
        self.act(GAM.rearrange("p h b -> p (h b)"), ps[:, 0:4 * NSB], AF.Exp, [tps], [tGm])
        seqm = self.CB[:, CB_SEQM:CB_SEQM + 1024].rearrange("p (b i) -> p b i", b=NSB)
        for h in range(4):
            s0 = S0[h % 2]; s0b = S0B[h % 2]; ts0 = tS0[h % 2]; sn = SN_[0]; tsn = tSN[0]
            self.ld(s0, self.i_gdnS[l][:, h].rearrange("b d v -> d b v"), [ts0])
            self.cp("act", s0b, s0, [ts0], [ts0])
            self.tt("dve", KTm, KT[:, h, 0:Pn].unsqueeze(1).to_broadcast([128, NSB, Pn]), seqm, ALU.mult, [tK[h], self.tCB], [tKm])
            self.tt("dve", KDm[0:Pn], KD[0:Pn, h, :].unsqueeze(1).to_broadcast([Pn, NSB, 128]),
                    sind.unsqueeze(2).to_broadcast([Pn, NSB, 128]), ALU.mult, [tb, self.tCF], [tKm])
            pks, tpks = self.psum()
            for b in range(NSB):
                self.mm(pks[0:Pn, 0:128], KTm[:, b, :], s0b[:, b, :], [tKm, ts0], [tpks], start=(b == 0), stop=(b == NSB - 1))
            self.stt(Ub[0:Pn, h, :], pks[0:Pn, 0:128], nbeg[:, h:h + 1], VB[0:Pn, h, :], ALU.mult, ALU.add, [tpks, tb], [tUb[h]])
            pvn, tpvn = self.psum()
            self.mm(pvn[0:Pn, 0:128], TT_[:, h, :], Ub[0:Pn, h, :], [tTT, tUb[h]], [tpvn])
            self.cp("act", VN[0:Pn, h, :], pvn[0:Pn, 0:128], [tpvn], [tVN[h]])
            pota, tpota = self.psum()
            self.mm(pota[:, 0:Pn], VN[0:Pn, h, :], att[:, h, :], [tVN[h], tb], [tpota])
            potb, tpotb = self.psum()
            for b in range(NSB):
                self.mm(potb[:, b * SL:(b + 1) * SL], s0b[:, b, :], qd[:, h, b * SL:(b + 1) * SL], [ts0, tb], [tpotb])
            self.cp("act", OTB[:, h, :], potb[:, 0:Pn], [tpotb], [tOT])
            self.tt("dve", OT[:, h, 0:Pn], pota[:, 0:Pn], OTB[:, h, :], ALU.add, [tpota, tOT], [tOT])
            for b4 in range(NSB // 4):
                pss, tpss = self.psum()
                pssv = pss[:, :].rearrange("p (j v) -> p j v", j=4)
                for j in range(4):
                    b = b4 * 4 + j
                    self.mm(pssv[:, j, :], KDm[0:Pn, b, :], VN[0:Pn, h, :], [tKm, tVN[h]], [tpss])
                for j in range(4):
                    b = b4 * 4 + j
                    self.stt(sn[:, b, :], s0[:, b, :], GAM[:, h, b:b + 1], pssv[:, j, :], ALU.mult, ALU.add, [ts0, tGm, tpss], [tsn])
            self.st(self.os_gdnS[l][:, h].rearrange("b d v -> d b v"), sn, [tsn])


def _consts():
    cf = np.zeros((128, NCF), np.float32)
    i = np.arange(128)
    cf[:, CF_ID:CF_ID + 128] = np.eye(128)
    cf[:, CF_TRI:CF_TRI + 128] = (i[:, None] <= i[None, :])
    cf[:, CF_ONES:CF_ONES + 128] = 1.0
    cf[:, CF_MSTR:CF_MSTR + 128] = (i[:, None] > i[None, :])
    cf[:, CF_MINCT:CF_MINCT + 128] = (i[None, :] >= i[:, None])
    j = np.arange(64)
    same = (j[:, None] // SL) == (j[None, :] // SL)
    cf[:64, CF_TRIS:CF_TRIS + 64] = same & (j[:, None] <= j[None, :])
    cf[:64, CF_SEGS:CF_SEGS + 64] = same
    cf[:64, CF_MSTRS:CF_MSTRS + 64] = same & (j[:, None] > j[None, :])
    cf[:64, CF_MINCTS:CF_MINCTS + 64] = same & (j[None, :] >= j[:, None])
    cf[:64, CF_SIND:CF_SIND + NSB] = (j[:, None] // SL) == np.arange(NSB)[None, :]
    cb = np.zeros((128, NCB), np.float32)
    cb[:, CB_ID:CB_ID + 128] = np.eye(128)
    cb[:, CB_ONES:CB_ONES + 128] = 1.0
    for li, s in enumerate(NLEV):
        I = i[:, None] // s
        J = i[None, :] // s
        ml = ((I % 2 == 1) & (J == I - 1)).astype(np.float32)
        cb[:, CB_NML + li * 128:CB_NML + (li + 1) * 128] = -ml
        cb[:, CB_NMU + li * 128:CB_NMU + (li + 1) * 128] = -ml.T
    seqm = (np.arange(NSB)[:, None] == (j[None, :] // SL)).astype(np.float32).reshape(1, NSB * 64)
    cb[:, CB_SEQM:CB_SEQM + 1024] = seqm
    return cf, cb


def _chunkcol(v, n):
    return np.moveaxis(v.reshape(v.shape[:-1] + (n, 128)), -1, 0)


_NC_CACHE = {}


def _get_nc(dbg=False):
    key = ("full", dbg)
    if key not in _NC_CACHE:
        _NC_CACHE[key] = Kern(dbg=dbg)
    return _NC_CACHE[key]


def _prep_shared(inp):
    f = lambda k: np.ascontiguousarray(np.asarray(inp[k], dtype=np.float32))
    pp = np.zeros((128, 2 * PPL), np.float32)
    rp = np.zeros((2, 3, 1024), np.float32)
    gr = np.zeros((2, 2, 4), np.float32)
    bbd = np.zeros((2, 2, 8, 128, 128), np.float32)
    cbd = np.zeros((2, 2, 8, 128, 128), np.float32)
    wax = np.zeros((2, 2, 2, 128, 128), np.float32)
    norms = [f("norm_mix_pre"), f("norm_mix_post"), f("norm_ffn_pre"), f("norm_ffn_post")]
    for l in range(2):
        o = l * PPL
        for k in range(4):
            pp[:, o + k * 8:o + k * 8 + 8] = _chunkcol(norms[k][l], 8)
        lre = f("s5_lambda_re")[l].reshape(1024)
        lim = f("s5_lambda_im")[l].reshape(1024)
        ldt = np.repeat(f("s5_log_dt")[l], 64)
        pp[:, o + 32:o + 40] = _chunkcol(lre, 8)
        pp[:, o + 40:o + 48] = _chunkcol(lim, 8)
        pp[:, o + 48:o + 56] = _chunkcol(ldt, 8)
        rp[l, 0], rp[l, 1], rp[l, 2] = lre, lim, ldt
        pp[:, o + 56:o + 58] = _chunkcol(f("s5_d")[l], 2)
        pp[:, o + 58:o + 60] = _chunkcol(f("s5_b_glu")[l], 2)
        cw = _chunkcol(f("lru_conv_w")[l], 2)
        pp[:, o + 60:o + 68] = np.transpose(cw, (0, 2, 1)).reshape(128, 8)
        gr[l, 0] = f("gdn_a_log")[l]
        gr[l, 1] = f("gdn_dt_bias")[l]
        bre, bim = f("s5_b_re")[l], f("s5_b_im")[l]
        cre, cim = f("s5_c_re")[l], f("s5_c_im")[l]
        for grp in range(16):
            sc, half = grp // 2, grp % 2
            rows = slice(32 * (sc % 4) + 16 * half, 32 * (sc % 4) + 16 * half + 16)
            colsS = slice(64 * half, 64 * half + 64)
            bbd[l, 0, sc, rows, colsS] = bre[grp].T
            bbd[l, 1, sc, rows, colsS] = bim[grp].T
            cbd[l, 0, sc, colsS, rows] = cre[grp].T
            cbd[l, 1, sc, colsS, rows] = cim[grp].T
        wa, wx = f("lru_w_a")[l], f("lru_w_x")[l]
        for hb in range(8):
            c, r0 = hb // 4, 32 * (hb % 4)
            wax[l, 0, c, r0:r0 + 32, r0:r0 + 32] = wa[hb]
            wax[l, 1, c, r0:r0 + 32, r0:r0 + 32] = wx[hb]
    for l in range(2):
        o = l * PPL
        pp[:, o + 68:o + 70] = _chunkcol(f("lru_conv_b")[l], 2)
        pp[:, o + 70:o + 72] = _chunkcol(f("lru_b_a")[l], 2)
        pp[:, o + 72:o + 74] = _chunkcol(f("lru_b_x")[l], 2)
        gw = _chunkcol(f("gdn_conv_w")[l], 12)
        pp[:, o + 76:o + 124] = np.transpose(gw, (0, 2, 1)).reshape(128, 48)
        pp[:, o + 124] = f("gdn_norm_w")[l]
        pp[:, o + 125:o + 127] = _chunkcol(f("lru_lambda")[l], 2)
    cf, cb = _consts()
    return dict(w_in=f("w_in"), w_out=f("w_out"), w_gu=f("ffn_w_gate_up"), w_dn=f("ffn_w_down"), pp=pp, rp=rp, gr=gr,
                cf=cf, cb=cb, bbd=bbd, cbd=cbd, wglu=f("s5_w_glu"), wax=wax)


def _prep_core(inp, c):
    f = lambda k: np.asarray(inp[k], dtype=np.float32)
    bs = slice(c * NSB, (c + 1) * NSB)
    d = {}
    d["xp"] = np.ascontiguousarray(f("x_prompt")[c].T)
    d["xs"] = np.ascontiguousarray(f("x_sample")[bs].reshape(SN, D_MODEL).T)
    s5 = np.stack([f("state_s5_re")[:, bs], f("state_s5_im")[:, bs]], axis=1).reshape(2, 2, NSB, 8, 128)
    d["i_s5"] = np.ascontiguousarray(np.transpose(s5, (0, 1, 4, 3, 2)))
    d["i_lruh"] = np.ascontiguousarray(np.transpose(f("state_lru_h")[:, bs].reshape(2, NSB, 2, 128), (0, 3, 2, 1)))
    d["i_lruconv"] = np.ascontiguousarray(np.transpose(f("state_lru_conv")[:, bs].reshape(2, NSB, 3, 2, 128), (0, 4, 3, 1, 2)))
    d["i_gdnconv"] = np.ascontiguousarray(np.transpose(f("state_gdn_conv")[:, bs].reshape(2, NSB, 3, 12, 128), (0, 4, 3, 1, 2)))
    d["i_gdnS"] = np.ascontiguousarray(f("state_gdn_S")[:, bs])
    return d


def _assemble(results):
    n = len(results)
    yp = np.stack([r["yp"].T for r in results])
    ys = np.concatenate([r["ys"].T.reshape(NSB, SL, D_MODEL) for r in results])
    p_s5 = np.stack([np.transpose(r["o_s5"], (0, 1, 3, 2)).reshape(2, 2, 16, 64) for r in results], axis=2)
    p_lruh = np.stack([np.transpose(r["o_lruh"], (0, 2, 1)).reshape(2, 256) for r in results], axis=1)
    p_lruconv = np.stack([np.transpose(r["o_lruconv"], (0, 3, 2, 1)).reshape(2, 3, 256) for r in results], axis=1)
    p_gdnS = np.stack([r["o_gdnS"] for r in results], axis=1)
    p_gdnconv = np.stack([np.transpose(r["o_gdnconv"], (0, 3, 2, 1)).reshape(2, 3, 1536) for r in results], axis=1)
    s_s5 = np.concatenate([np.transpose(r["os_s5"], (0, 1, 4, 3, 2)).reshape(2, 2, NSB, 16, 64) for r in results], axis=2)
    s_lruh = np.concatenate([np.transpose(r["os_lruh"], (0, 3, 2, 1)).reshape(2, NSB, 256) for r in results], axis=1)
    s_lruconv = np.concatenate([np.transpose(r["os_lruconv"], (0, 3, 4, 2, 1)).reshape(2, NSB, 3, 256) for r in results], axis=1)
    s_gdnS = np.concatenate([r["os_gdnS"] for r in results], axis=1)
    s_gdnconv = np.concatenate([np.transpose(r["os_gdnconv"], (0, 3, 4, 2, 1)).reshape(2, NSB, 3, 1536) for r in results], axis=1)
    outs = (yp, ys, p_s5[:, 0], p_s5[:, 1], p_lruh, p_lruconv, p_gdnS, p_gdnconv,
            s_s5[:, 0], s_s5[:, 1], s_lruh, s_lruconv, s_gdnS, s_gdnconv)
    return tuple(np.ascontiguousarray(o, dtype=np.float32) for o in outs)


def kernel(**inputs):
    k = _get_nc()
    shared = _prep_shared(inputs)
    in_maps = []
    for c in range(8):
        d = dict(shared)
        d.update(_prep_core(inputs, c))
        in_maps.append(d)
    res = run_bass_kernel_spmd(k.nc, in_maps, core_ids=list(range(8)))
    return _assemble(res.results)
```

```python
import math
import numpy as np
from contextlib import ExitStack
import concourse.bass as bass
import concourse.mybir as mybir
from concourse.bass_utils import run_bass_kernel_spmd

F32 = mybir.dt.float32
BF16 = mybir.dt.bfloat16
AF = mybir.ActivationFunctionType
ALU = mybir.AluOpType

ENG = ("pe", "act", "dve", "pool", "sp")
NDMA_SEM = 40
PI = math.pi


class Tok:
    __slots__ = ("w", "r")

    def __init__(self):
        self.w = None
        self.r = {}


def toks(n):
    return [Tok() for _ in range(n)]


class Prog:
    def __init__(self, same_engine_sync=True):
        self.nc = bass.Bass("TRN2", target_bir_lowering=False)
        self.es = ExitStack()
        self.ops = {e: [] for e in ENG}
        self.cnt = {e: 0 for e in ENG}
        self.seen = {e: {} for e in ENG}
        self.same_engine_sync = same_engine_sync
        self.esem = {e: self.es.enter_context(self.nc.semaphore("s_" + e)) for e in ENG[:4]}
        self.dsem = [self.es.enter_context(self.nc.semaphore("d%d" % i)) for i in range(NDMA_SEM)]
        self.dval = [0] * NDMA_SEM
        self.dnext = {False: 0, True: 0}
        self.nsb = 0
        self.capture = None
        self.opn = {e: {} for e in ENG}

    def sb(self, shape, dtype=F32):
        self.nsb += 1
        return self.es.enter_context(self.nc.sbuf_tensor("sb%d" % self.nsb, list(shape), dtype))

    def ps(self, shape, dtype=F32):
        self.nsb += 1
        return self.es.enter_context(self.nc.psum_tensor("ps%d" % self.nsb, list(shape), dtype))

    def dram(self, name, shape, dtype=F32, kind="ExternalInput"):
        return self.nc.dram_tensor(name, list(shape), dtype, kind=kind).ap()

    def _deps(self, reads, writes):
        deps = {}
        for t in reads:
            if t.w is not None and deps.get(t.w[0], 0) < t.w[1]:
                deps[t.w[0]] = t.w[1]
        for t in writes:
            if t.w is not None and deps.get(t.w[0], 0) < t.w[1]:
                deps[t.w[0]] = t.w[1]
            for k, v in t.r.items():
                if deps.get(k, 0) < v:
                    deps[k] = v
        return deps

    def _waits(self, eng, deps):
        waits = []
        for k, v in deps.items():
            if k == eng and (eng == "pe" or not self.same_engine_sync):
                continue
            if self.seen[eng].get(k, 0) >= v:
                continue
            self.seen[eng][k] = v
            waits.append((k, v))
        return waits

    def _update(self, ev, reads, writes):
        k, v = ev
        for t in reads:
            if t.r.get(k, 0) < v:
                t.r[k] = v
        for t in writes:
            t.w = ev
            t.r = {}

    def op(self, eng, fn, reads=(), writes=(), n=0):
        if self.capture is not None:
            self.capture.append((0, eng, fn, list(reads), list(writes), n))
            return
        deps = self._deps(reads, writes)
        if ELIDE_BIG and eng != "pe" and eng in deps and n >= BIG_N:
            big = True
            for t in list(reads) + list(writes):
                if t.w is not None and t.w[0] == eng and self.opn[eng].get(t.w[1], 0) < BIG_N:
                    big = False
            for t in writes:
                if eng in t.r and self.opn[eng].get(t.r[eng], 0) < BIG_N:
                    big = False
            if big:
                del deps[eng]
        waits = self._waits(eng, deps)
        self.cnt[eng] += 1
        self.opn[eng][self.cnt[eng]] = n
        ev = (eng, self.cnt[eng])
        self.ops[eng].append((waits, fn, ev))
        self._update(ev, reads, writes)

    def dma(self, eng, fn, reads=(), writes=()):
        if self.capture is not None:
            self.capture.append((1, eng, fn, list(reads), list(writes), 0))
            return
        deps = self._deps(reads, writes)
        half = NDMA_SEM // 2
        base = half if eng == "pool" else 0
        i = base + self.dnext[eng == "pool"]
        self.dnext[eng == "pool"] = (self.dnext[eng == "pool"] + 1) % half
        key = "d%d" % i
        if self.dval[i] > 0 and deps.get(key, 0) < self.dval[i]:
            deps[key] = self.dval[i]
        waits = self._waits(eng, deps)
        self.dval[i] += 16
        ev = (key, self.dval[i])
        self.ops[eng].append((waits, fn, ev))
        self._update(ev, reads, writes)

    def replay(self, a, b=(), ratio=1.0):
        ia = ib = 0
        acc = 0.0
        while ia < len(a) or ib < len(b):
            if ia < len(a) and (ib >= len(b) or acc < ratio):
                rec = a[ia]
                ia += 1
                acc += 1.0
            else:
                rec = b[ib]
                ib += 1
                acc -= ratio
            if rec[0]:
                self.dma(*rec[1:5])
            else:
                self.op(*rec[1:6])

    def barrier(self):
        for e in ENG:
            deps = {k: self.cnt[k] for k in ENG[:4] if k != e and self.cnt[k] > 0}
            if e != "sp":
                for i in range(NDMA_SEM):
                    if self.dval[i] > 0:
                        deps["d%d" % i] = self.dval[i]
            waits = self._waits(e, deps)
            if waits:
                self.ops[e].append((waits, None, None))

    def _sem(self, k):
        return self.esem[k] if k in self.esem else self.dsem[int(k[1:])]

    def build(self):
        nc = self.nc
        with nc.Block() as block:
            def run(e, name):
                for waits, fn, ev in self.ops[name]:
                    for k, v in waits:
                        e.wait_ge(self._sem(k), v)
                    if fn is None:
                        continue
                    inst = fn(e)
                    k, v = ev
                    inst.then_inc(self._sem(k), 1 if k in self.esem else 16)
                if name == "sp":
                    for i in range(NDMA_SEM):
                        if self.dval[i] > 0:
                            e.wait_ge(self.dsem[i], self.dval[i])
                    for k2 in ENG[:4]:
                        if self.cnt[k2] > 0:
                            e.wait_ge(self.esem[k2], self.cnt[k2])

            @block.tensor
            def _(e):
                run(e, "pe")

            @block.scalar
            def _(e):
                run(e, "act")

            @block.vector
            def _(e):
                run(e, "dve")

            @block.gpsimd
            def _(e):
                run(e, "pool")

            @block.sync
            def _(e):
                run(e, "sp")
        self.es.close()
        return nc


D_MODEL = 1024
SEQ = 2048
NT = 4
TN = 512
NSB = 16
SL = 4
SN = NSB * SL
IN_COLS = 2824
FFN_H = 2816
SEG = 128
EPS = 1e-6
PPL = 128
NLEV = [1, 2, 4, 8, 16, 32, 64]
SAME_ENGINE_SYNC = True
ELIDE_BIG = False
BIG_N = 256
ARENA_F = 12864
ARENA_B = 16896

CF_ID, CF_TRI, CF_ONES, CF_MSTR, CF_MINCT = 0, 128, 256, 384, 512
CF_TRIS, CF_SEGS, CF_MSTRS, CF_MINCTS, CF_SIND = 640, 704, 768, 832, 896
NCF = 912
CB_ID, CB_ONES, CB_NML, CB_NMU, CB_SEQM = 0, 128, 256, 256 + 7 * 128, 256 + 14 * 128
NCB = CB_SEQM + 1024


class Grp:
    def __init__(self, kind, idx):
        self.kind = kind
        self.idx = idx
        if kind == "p":
            self.N, self.nb, self.L, self.P, self.nblk = TN, 1, TN, 128, 4
        else:
            self.N, self.nb, self.L, self.P, self.nblk = SN, NSB, SL, 64, 1


class Kern:
    def __init__(self, dbg=False, nlayers=2, groups=None):
        self.dbgon = dbg
        self.P = Prog(same_engine_sync=SAME_ENGINE_SYNC)
        self.nl = nlayers
        self.groups = groups
        self.dbg_outs = []
        self.build()

    @staticmethod
    def nel(ap):
        return int(np.prod(ap.shape[1:]))

    def tt(self, eng, out, a, b, op, r, w):
        n = min(self.nel(out), self.nel(a), self.nel(b))
        self.P.op(eng, lambda e: e.tensor_tensor(out=out, in0=a, in1=b, op=op), r, w, n=n)

    def ts(self, eng, out, a, s1, op0, r, w, s2=None, op1=None):
        if op1 is None:
            self.P.op(eng, lambda e: e.tensor_scalar(out=out, in0=a, scalar1=s1, scalar2=None, op0=op0), r, w, n=self.nel(out))
        else:
            self.P.op(eng, lambda e: e.tensor_scalar(out=out, in0=a, scalar1=s1, scalar2=s2, op0=op0, op1=op1), r, w, n=self.nel(out))

    def stt(self, out, a, s, b, op0, op1, r, w):
        n = min(self.nel(out), self.nel(a), self.nel(b))
        self.P.op("dve", lambda e: e.scalar_tensor_tensor(out=out, in0=a, scalar=s, in1=b, op0=op0, op1=op1), r, w, n=n)

    def act(self, out, in_, func, r, w, bias=None, scale=None):
        kw = {}
        if bias is not None:
            kw["bias"] = bias
        if scale is not None:
            kw["scale"] = scale
        self.P.op("act", lambda e: e.activation(out=out, in_=in_, func=func, **kw), r, w, n=min(self.nel(out), self.nel(in_)))

    def cp(self, eng, out, in_, r, w):
        if eng == "act":
            self.P.op("act", lambda e: e.activation(out=out, in_=in_, func=AF.Copy), r, w, n=min(self.nel(out), self.nel(in_)))
        else:
            self.P.op(eng, lambda e: e.tensor_copy(out=out, in_=in_), r, w, n=min(self.nel(out), self.nel(in_)))

    def mm(self, out, lhsT, rhs, r, w, start=True, stop=True):
        self.P.op("pe", lambda e: e.matmul(out, lhsT=lhsT, rhs=rhs, start=start, stop=stop), r, w)

    def tr(self, out, in_, ident, r, w):
        self.P.op("pe", lambda e: e.transpose(out=out, in_=in_, identity=ident), r, w)

    def recip(self, out, in_, r, w):
        self.P.op("dve", lambda e: e.reciprocal(out=out, in_=in_), r, w)

    def memset(self, eng, ap, val, w):
        self.P.op(eng, lambda e: e.memset(ap, val), [], w)

    def ld(self, out, in_, w, eng="sp", r=()):
        self.P.dma(eng, lambda e: e.dma_start(out=out, in_=in_), r, w)

    def st(self, out, in_, r, eng="sp"):
        self.P.dma(eng, lambda e: e.dma_start(out=out, in_=in_), r, [])

    def dbg(self, name, ap, r, shape, bf=False):
        if not self.dbgon:
            return
        d = self.P.dram("dbg_" + name, shape, kind="ExternalOutput")
        self.dbg_outs.append("dbg_" + name)
        self.st(d, ap, r, eng=("pool" if bf else "sp"))

    def psum(self):
        st = self.ps_set
        i = st[self.ps_cur[id(st)] % len(st)]
        self.ps_cur[id(st)] += 1
        return self.PS[i], self.tPS[i]

    def capture(self, fn, psset):
        P = self.P
        old = self.ps_set
        self.ps_set = psset
        P.capture = []
        fn()
        ops = P.capture
        P.capture = None
        self.ps_set = old
        return ops

    def AFB_(self, *shape):
        n = int(np.prod(shape))
        v = self.AF_((n + 1) // 2).bitcast(BF16)[:, 0:n]
        return self._shape(v, shape)

    def arena_reset(self):
        self.P.barrier()
        self.af = 0
        self.ab = 0

    def AF_(self, *shape):
        n = int(np.prod(shape))
        self.af = (self.af + 15) // 16 * 16
        assert self.af + n <= ARENA_F, ("arena f32 overflow", self.af, n)
        v = self.ARF[:, self.af:self.af + n]
        self.af += n
        return self._shape(v, shape)

    def AB_(self, *shape):
        n = int(np.prod(shape))
        self.ab = (self.ab + 31) // 32 * 32
        assert self.ab + n <= ARENA_B, ("arena bf16 overflow", self.ab, n)
        v = self.ARB[:, self.ab:self.ab + n]
        self.ab += n
        return self._shape(v, shape)

    @staticmethod
    def _shape(v, shape):
        if len(shape) == 1:
            return v
        if len(shape) == 2:
            return v.rearrange("p (a b) -> p a b", a=shape[0])
        return v.rearrange("p (a b c) -> p a b c", a=shape[0], b=shape[1])

    def wplan(self):
        self.wA = []
        self.wB = []
        for g in self.groups:
            for l in range(self.nl):
                wi = self.w_in[l]
                for c0 in (0, 256, 512, 768, 1024, 1280, 1536, 1792, 2048, 2312, 2568):
                    self.wA.append((wi[:, c0:c0 + 256], 256))
                for c0 in range(0, D_MODEL, 256):
                    self.wA.append((self.w_out[l][:, c0:c0 + 256], 256))
                for c0 in range(0, FFN_H, 256):
                    self.wA.append((self.w_gu[l][:, c0:c0 + 256], 256))
                    self.wA.append((self.w_gu[l][:, FFN_H + c0:FFN_H + c0 + 256], 256))
                for c0 in range(0, D_MODEL, 128):
                    self.wB.append(self.w_dn[l][:, c0:c0 + 128])
        self.wAi = 0
        self.wAe = 0
        self.wBi = 0
        self.wBe = 0

    def _emitA(self, upto):
        while self.wAe <= upto and self.wAe < len(self.wA):
            j = self.wAe
            ap, w = self.wA[j]
            buf = self.WA[j % len(self.WA)]
            self.ld(buf[:, :, 0:w], ap.rearrange("(kc p) n -> p kc n", p=128), [self.tWA[j % len(self.WA)]], eng="pool")
            self.wAe += 1

    def getA(self):
        i = self.wAi
        self._emitA(i)
        self.wAi += 1
        return self.WA[i % len(self.WA)], self.tWA[i % len(self.WA)]

    def prefetchA(self):
        self._emitA(self.wAi + len(self.WA) - 1)

    def prefetchB(self):
        self._emitB(self.wBi + len(self.WB) - 1)

    def _emitB(self, upto):
        while self.wBe <= upto and self.wBe < len(self.wB):
            j = self.wBe
            buf = self.WB[j % len(self.WB)]
            self.ld(buf[:], self.wB[j].rearrange("(kc p) n -> p kc n", p=128), [self.tWB[j % len(self.WB)]], eng="pool")
            self.wBe += 1

    def getB(self):
        i = self.wBi
        self._emitB(i)
        self.wBi += 1
        return self.WB[i % len(self.WB)], self.tWB[i % len(self.WB)]

    def build(self):
        P = self.P
        nl = self.nl
        if self.groups is None:
            self.groups = [Grp("s", 0)] + [Grp("p", i) for i in range(NT)]
        self.xp = P.dram("xp", [D_MODEL, SEQ])
        self.xs = P.dram("xs", [D_MODEL, SN])
        self.i_s5 = P.dram("i_s5", [2, 2, 128, 8, NSB])
        self.i_lruh = P.dram("i_lruh", [2, 128, 2, NSB])
        self.i_lruconv = P.dram("i_lruconv", [2, 128, 2, NSB, 3])
        self.i_gdnconv = P.dram("i_gdnconv", [2, 128, 12, NSB, 3])
        self.i_gdnS = P.dram("i_gdnS", [2, NSB, 4, 128, 128])
        self.w_in = P.dram("w_in", [2, D_MODEL, IN_COLS])
        self.w_out = P.dram("w_out", [2, D_MODEL, D_MODEL])
        self.w_gu = P.dram("w_gu", [2, D_MODEL, 2 * FFN_H])
        self.w_dn = P.dram("w_dn", [2, FFN_H, D_MODEL])
        self.d_pp = P.dram("pp", [128, 2 * PPL])
        self.d_rp = P.dram("rp", [2, 3, 1024])
        self.d_gr = P.dram("gr", [2, 2, 4])
        self.d_cf = P.dram("cf", [128, NCF])
        self.d_cb = P.dram("cb", [128, NCB])
        self.d_bbd = P.dram("bbd", [2, 2, 8, 128, 128])
        self.d_cbd = P.dram("cbd", [2, 2, 8, 128, 128])
        self.d_wglu = P.dram("wglu", [2, 256, 256])
        self.d_wax = P.dram("wax", [2, 2, 2, 128, 128])
        o = lambda n, s: P.dram(n, s, kind="ExternalOutput")
        self.yp = o("yp", [D_MODEL, SEQ])
        self.ys = o("ys", [D_MODEL, SN])
        self.o_s5 = o("o_s5", [2, 2, 128, 8])
        self.o_lruh = o("o_lruh", [2, 128, 2])
        self.o_lruconv = o("o_lruconv", [2, 128, 2, 3])
        self.o_gdnconv = o("o_gdnconv", [2, 128, 12, 3])
        self.o_gdnS = o("o_gdnS", [2, 4, 128, 128])
        self.os_s5 = o("os_s5", [2, 2, 128, 8, NSB])
        self.os_lruh = o("os_lruh", [2, 128, 2, NSB])
        self.os_lruconv = o("os_lruconv", [2, 128, 2, NSB, 3])
        self.os_gdnconv = o("os_gdnconv", [2, 128, 12, NSB, 3])
        self.os_gdnS = o("os_gdnS", [2, NSB, 4, 128, 128])

        self.X = P.sb([128, 8, TN]); self.tX = toks(8)
        self.HB = P.sb([128, 8, TN], BF16); self.tHB = toks(8)
        self.MIXB = P.sb([128, 8, TN], BF16); self.tMIX = toks(8)
        self.ARF = P.sb([128, ARENA_F]); self.ARB = P.sb([128, ARENA_B], BF16)
        self.CF = P.sb([128, NCF]); self.tCF = Tok()
        self.CB = P.sb([128, NCB], BF16); self.tCB = Tok()
        self.PP = P.sb([128, 2 * PPL]); self.tPP = Tok()
        self.EPS = P.sb([128, 16]); self.tC = Tok()
        self.WA = [P.sb([128, 8, 256], BF16) for _ in range(5)]; self.tWA = toks(5)
        self.WB = [P.sb([128, 22, 128], BF16) for _ in range(3)]; self.tWB = toks(3)
        self.PS = [P.ps([128, 512]) for _ in range(6)]; self.tPS = toks(6)
        self.PSN = P.ps([128, 512]); self.tPSN = Tok()
        self.PSB = P.ps([128, 2, 512], BF16); self.tPSB = [Tok()] * 2
        self.PS_ALL = [0, 1, 2, 3, 4, 5]; self.PS_F = [0, 1, 2]; self.PS_C = [3, 4, 5]
        self.ps_set = self.PS_ALL
        self.ps_cur = {id(self.PS_ALL): 0, id(self.PS_F): 0, id(self.PS_C): 0}
        self.psbn = 0
        L2 = range(2)
        self.TC = [P.sb([128, 8, SEG]) for _ in L2]; self.TS = [P.sb([128, 8, SEG]) for _ in L2]
        self.S5C = [P.sb([128, 8, 16]) for _ in L2]
        self.BRE = [P.sb([128, 8, 128], BF16) for _ in L2]; self.BIM = [P.sb([128, 8, 128], BF16) for _ in L2]
        self.CRE = [P.sb([128, 8, 128], BF16) for _ in L2]; self.NCIM = [P.sb([128, 8, 128], BF16) for _ in L2]
        self.WGLU = [P.sb([128, 2, 256], BF16) for _ in L2]
        self.WAX = [P.sb([128, 2, 2, 128], BF16) for _ in L2]
        self.WAB = [P.sb([128, 8, 8], BF16) for _ in L2]
        self.NTSL = [P.sb([128, 16]) for _ in L2]
        self.LC = [P.sb([128, 16]) for _ in L2]
        self.GR = [P.sb([128, 2, 16]) for _ in L2]
        self.tL = toks(2)
        self.S5I = [P.sb([128, 2, 16]) for _ in L2]; self.tS5I = [toks(8) for _ in L2]
        self.HL = [P.sb([128, 16]) for _ in L2]; self.tHL = toks(2); self.tHLc = [toks(2) for _ in L2]
        self.HXL = [P.sb([128, 2, 8]) for _ in L2]; self.tHXL = toks(2)
        self.HQKV = [P.sb([128, 12, 4]) for _ in L2]; self.tHQ = toks(2)
        self.SF = [P.sb([128, 4, 128]) for _ in L2]; self.SBF = [P.sb([128, 4, 128], BF16) for _ in L2]
        self.tS = [toks(4) for _ in L2]

        self.wplan()
        self.setup()
        self.dbg("s5c0", self.S5C[0][:, :, 0:8], [self.tL[0]], [128, 8, 8])
        self.dbg("tc0", self.TC[0][:], [self.tL[0]], [128, 8, SEG])
        self.dbg("ts0", self.TS[0][:], [self.tL[0]], [128, 8, SEG])
        self.dbg("bre0", self.BRE[0][:], [self.tL[0]], [128, 8, 128], bf=True)
        self.dbg("bim0", self.BIM[0][:], [self.tL[0]], [128, 8, 128], bf=True)
        self.dbg("ncim0", self.NCIM[0][:], [self.tL[0]], [128, 8, 128], bf=True)
        self.XN = None
        for gi, g in enumerate(self.groups):
            self.next_g = self.groups[gi + 1] if gi + 1 < len(self.groups) else None
            for l in range(nl):
                self.layer(g, l)
        self.nc = P.build()

    def cf(self, off, P_, n):
        return self.CF[0:P_, off:off + n]

    def pp(self, l, off, n=1):
        return self.PP[:, l * PPL + off:l * PPL + off + n]

    def sincos(self, th, shape, sn, cs, r, w):
        k = self.AF_(*shape); rr = self.AF_(*shape); tk = Tok()
        for phase, dst in ((0.0, sn), (PI / 2, cs)):
            self.ts("dve", k, th, PI - phase, ALU.is_ge, r, [tk])
            for m in range(1, 7):
                self.stt(k, th, (2 * m + 1) * PI - phase, k, ALU.is_ge, ALU.add, r + [tk], [tk])
            self.stt(rr, k, -2 * PI, th, ALU.mult, ALU.add, r + [tk], [tk])
            if phase:
                self.ts("dve", rr, rr, phase, ALU.add, [tk], [tk])
            self.act(dst, rr, AF.Sin, [tk], w + [tk])

    def setup(self):
        P = self.P
        self.af = 0
        self.ab = 0
        self.ld(self.CF[:], self.d_cf, [self.tCF])
        self.ld(self.PP[:], self.d_pp, [self.tPP])
        self.ld(self.CB[:], self.d_cb, [self.tCB], eng="pool")
        self.memset("dve", self.EPS[:], EPS, [self.tC])
        for l in range(self.nl):
            tl = self.tL[l]
            self.memset("dve", self.HL[l][:], 0.0, self.tHLc[l])
            self.memset("dve", self.HXL[l][:], 0.0, [self.tHXL[l]])
            self.memset("dve", self.HQKV[l][:], 0.0, [self.tHQ[l]])
            self.memset("dve", self.S5I[l][:], 0.0, self.tS5I[l])
            self.memset("dve", self.SF[l][:], 0.0, self.tS[l])
            self.memset("dve", self.SBF[l][:], 0.0, self.tS[l])
            self.ld(self.CRE[l][:], self.d_cbd[l, 0].rearrange("s r c -> r s c"), [tl], eng="pool")
            self.ld(self.WGLU[l][:], self.d_wglu[l].rearrange("(kc p) n -> p kc n", p=128), [tl], eng="pool")
            self.ld(self.WAX[l][:], self.d_wax[l].rearrange("a c r k -> r a c k"), [tl], eng="pool")
            self.ld(self.WAB[l][:], self.w_in[l][:, 2304:2312].rearrange("(kc p) n -> p kc n", p=128), [tl], eng="pool")
            self.ld(self.GR[l][:, :, 0:4], self.d_gr[l:l + 1].rearrange("o a h -> o (a h)").partition_broadcast(128)
                    .rearrange("p o (a h) -> p (o a) h", a=2), [tl])
            self.arena_reset()
            r0 = [self.tPP]
            t = Tok()
            lre, lim, ldt = self.pp(l, 32, 8), self.pp(l, 40, 8), self.pp(l, 48, 8)
            lr = self.AF_(8); dt = self.AF_(8); th = self.AF_(8); sn = self.AF_(8); cs = self.AF_(8); tmp = self.AF_(8)
            C = self.S5C[l]
            self.ts("dve", lr, lre, -1e-4, ALU.min, r0, [t])
            self.act(dt, ldt, AF.Exp, r0, [t])
            self.tt("dve", th, lim, dt, ALU.mult, r0 + [t], [t])
            self.tt("dve", tmp, lr, dt, ALU.mult, [t], [t])
            self.act(C[:, 0, 0:8], tmp, AF.Exp, [t], [t, tl])
            self.sincos(th, (8,), sn, cs, [t], [t])
            self.tt("dve", C[:, 3, 0:8], C[:, 0, 0:8], cs, ALU.mult, [t], [t, tl])
            self.tt("dve", C[:, 4, 0:8], C[:, 0, 0:8], sn, ALU.mult, [t], [t, tl])
            self.ts("dve", C[:, 6, 0:8], C[:, 4, 0:8], -1.0, ALU.mult, [t], [t, tl])
            TCt, TSt = self.TC[l], self.TS[l]
            er = self.AF_(8); ei = self.AF_(8); e2 = self.AF_(8); e3 = self.AF_(8)
            t1 = self.AF_(8, SEG // 2); t2 = self.AF_(8, SEG // 2)
            self.memset("dve", TCt[:, :, 0:1], 1.0, [t, tl])
            self.memset("dve", TSt[:, :, 0:1], 0.0, [t, tl])
            self.cp("dve", er, cs, [t], [t])
            self.cp("dve", ei, sn, [t], [t])
            m = 1
            while m < SEG:
                erb = er.unsqueeze(2).to_broadcast([128, 8, m]); eib = ei.unsqueeze(2).to_broadcast([128, 8, m])
                self.tt("dve", t1[:, :, 0:m], TCt[:, :, 0:m], erb, ALU.mult, [t], [t])
                self.tt("dve", t2[:, :, 0:m], TSt[:, :, 0:m], eib, ALU.mult, [t], [t])
                self.tt("dve", TCt[:, :, m:2 * m], t1[:, :, 0:m], t2[:, :, 0:m], ALU.subtract, [t], [t, tl])
                self.tt("dve", t1[:, :, 0:m], TCt[:, :, 0:m], eib, ALU.mult, [t], [t])
                self.tt("dve", t2[:, :, 0:m], TSt[:, :, 0:m], erb, ALU.mult, [t], [t])
                self.tt("dve", TSt[:, :, m:2 * m], t1[:, :, 0:m], t2[:, :, 0:m], ALU.add, [t], [t, tl])
                self.tt("dve", e2, er, er, ALU.mult, [t], [t])
                self.tt("dve", e3, ei, ei, ALU.mult, [t], [t])
                self.stt(ei, er, 2.0, ei, ALU.mult, ALU.mult, [t], [t])
                self.tt("dve", er, e2, e3, ALU.subtract, [t], [t])
                m *= 2
            self.cp("dve", C[:, 1, 0:8], er, [t], [t, tl])
            self.cp("dve", C[:, 2, 0:8], ei, [t], [t, tl])
            self.ts("dve", C[:, 5, 0:8], ei, -1.0, ALU.mult, [t], [t, tl])
            self.ts("dve", self.NTSL[l][:, 0:8], TSt[:, :, SEG - 1], -1.0, ALU.mult, [t], [t, tl])
            li = lim
            am1 = self.AF_(8); n1 = self.AF_(8); n2 = self.AF_(8); dn = self.AF_(8); fr = self.AF_(8); fi = self.AF_(8)
            self.ts("dve", am1, C[:, 3, 0:8], -1.0, ALU.add, [t], [t])
            self.tt("dve", n1, am1, lr, ALU.mult, [t], [t])
            self.tt("dve", n2, C[:, 4, 0:8], li, ALU.mult, [t, self.tPP], [t])
            self.tt("dve", n1, n1, n2, ALU.add, [t], [t])
            self.tt("dve", n2, C[:, 4, 0:8], lr, ALU.mult, [t], [t])
            self.tt("dve", dn, am1, li, ALU.mult, [t, self.tPP], [t])
            self.tt("dve", n2, n2, dn, ALU.subtract, [t], [t])
            self.tt("dve", dn, lr, lr, ALU.mult, [t], [t])
            self.tt("dve", am1, li, li, ALU.mult, [t, self.tPP], [t])
            self.tt("dve", dn, dn, am1, ALU.add, [t], [t])
            self.recip(dn, dn, [t], [t])
            self.tt("dve", fr, n1, dn, ALU.mult, [t], [t])
            self.tt("dve", fi, n2, dn, ALU.mult, [t], [t])
            tdg = toks(2)
            DG = [self.AF_(8, 128), self.AF_(8, 128)]
            idf = self.cf(CF_ID, 128, 128); onesf = self.cf(CF_ONES, 128, 128)
            rows = []
            for qi, col in enumerate((fr, fi)):
                banks = []
                for half in range(2):
                    ps, tps = self.psum()
                    for j in range(4):
                        sc = half * 4 + j
                        self.ts("dve", DG[qi][:, sc, :], idf, col[:, sc:sc + 1], ALU.mult, [self.tCF, t], [tdg[qi]])
                        self.mm(ps[:, j * 128:(j + 1) * 128], onesf, DG[qi][:, sc, :], [self.tCF, tdg[qi]], [tps])
                    banks.append((ps, tps))
                rows.append(banks)
            bre = self.AF_(8, 128); bim = self.AF_(8, 128); u1 = self.AF_(8, 128); u2 = self.AF_(8, 128)
            self.ld(bre, self.d_bbd[l, 0].rearrange("s r c -> r s c"), [t])
            self.ld(bim, self.d_bbd[l, 1].rearrange("s r c -> r s c"), [t])
            for half in range(2):
                hs = slice(half * 4, half * 4 + 4)
                frp, tfr = rows[0][half]; fip, tfi = rows[1][half]
                frv = frp[:, :].rearrange("p (s c) -> p s c", s=4); fiv = fip[:, :].rearrange("p (s c) -> p s c", s=4)
                self.tt("dve", u1[:, hs], frv, bre[:, hs], ALU.mult, [tfr, t], [t])
                self.tt("dve", u2[:, hs], fiv, bim[:, hs], ALU.mult, [tfi, t], [t])
                self.tt("dve", self.BRE[l][:, hs], u1[:, hs], u2[:, hs], ALU.subtract, [t], [t, tl])
                self.tt("dve", u1[:, hs], frv, bim[:, hs], ALU.mult, [tfr, t], [t])
                self.tt("dve", u2[:, hs], fiv, bre[:, hs], ALU.mult, [tfi, t], [t])
                self.tt("dve", self.BIM[l][:, hs], u1[:, hs], u2[:, hs], ALU.add, [t], [t, tl])
            self.ld(bre, self.d_cbd[l, 1].rearrange("s r c -> r s c"), [t])
            self.ts("dve", self.NCIM[l][:], bre, -1.0, ALU.mult, [t], [t, tl])
            a3 = u1[:, 0]
            e1 = a3[:, 0:2]
            self.act(e1, self.pp(l, 125, 2), AF.Exp, [self.tPP], [t], scale=-1.0)
            self.act(e1, e1, AF.Ln, [t], [t], bias=1.0)
            self.ts("dve", self.LC[l][:, 0:2], e1, -8.0, ALU.mult, [t], [t, tl])
            self.act(self.GR[l][:, 0, 0:4], self.GR[l][:, 0, 0:4], AF.Exp, [tl], [tl])
            self.ts("dve", self.GR[l][:, 0, 0:4], self.GR[l][:, 0, 0:4], -1.0, ALU.mult, [tl], [tl])
        self.arena_reset()

    def norm_ssq(self, srcs, N, r_list):
        ps, tps = self.psum()
        n = len(srcs)
        sc = 1.0 / math.sqrt(128.0 * n)
        for c, (src, rt) in enumerate(zip(srcs, r_list)):
            sq = self.SQ[c % 2]; tsq = self.tSQ[c % 2]
            self.act(sq[:, 0:N], src, AF.Square, rt, [tsq], scale=sc)
            self.mm(ps[:, 0:N], self.CB[:, CB_ONES:CB_ONES + 128], sq[:, 0:N], [tsq, self.tCB], [tps],
                    start=(c == 0), stop=(c == n - 1))
        return ps, tps

    def rstd_from(self, ps, tps, N, out, tout):
        self.act(out, ps[:, 0:N], AF.Ln, [tps, self.tC], [tout], bias=self.EPS[:, 0:1])
        self.act(out, out, AF.Exp, [tout], [tout], scale=-0.5)

    def prenorm(self, g, l, kind):
        N = g.N
        ps, tps = self.norm_ssq([self.X[:, c, 0:N] for c in range(8)], N, [[self.tX[c]] for c in range(8)])
        rs = self.RSTD; trs = self.tRSTD
        self.rstd_from(ps, tps, N, rs[:, 0:N], trs)
        for c in range(8):
            self.stt(self.HB[:, c, 0:N], self.X[:, c, 0:N], self.pp(l, kind * 8 + c), rs[:, 0:N], ALU.mult, ALU.mult,
                     [self.tX[c], trs, self.tPP], [self.tHB[c]])

    def postnorm(self, g, l, kind, MO, tMO, ps, tps):
        N = g.N
        rs = self.RSTD; trs = self.tRSTD
        self.rstd_from(ps, tps, N, rs[:, 0:N], trs)
        for c in range(8):
            self.tt("dve", MO[:, c, :], MO[:, c, :], rs[:, 0:N], ALU.mult, [tMO[c], trs], [tMO[c]])
            self.stt(self.X[:, c, 0:N], MO[:, c, :], self.pp(l, kind * 8 + c), self.X[:, c, 0:N], ALU.mult, ALU.add,
                     [tMO[c], self.tPP], [self.tX[c]])

    def proj_chunk(self, wbuf, twb, col, N, rhs_buf, rhs_toks, nk=8):
        ps, tps = self.psum()
        for k in range(nk):
            self.mm(ps[:, 0:N], wbuf[:, k, col:col + 128], rhs_buf[:, k, 0:N], [twb, rhs_toks[k]], [tps],
                    start=(k == 0), stop=(k == nk - 1))
        return ps, tps

    def layer(self, g, l):
        N = g.N
        tl = self.tL[l]
        if l == 0 and self.XN is not None:
            for c in range(8):
                self.cp(("act", "dve", "pool")[c % 3], self.X[:, c, 0:N], self.XN[:, c, 0:N], [self.tXN], [self.tX[c]])
            self.XN = None
        elif l == 0:
            src = self.xs if g.kind == "s" else self.xp[:, g.idx * TN:(g.idx + 1) * TN]
            sv = src.rearrange("(c p) t -> p c t", p=128)
            for c in range(8):
                self.ld(self.X[:, c, 0:N], sv[:, c, :], [self.tX[c]])
        self.arena_reset()
        self.SQ = [self.AB_(TN), self.AB_(TN)]; self.tSQ = toks(2)
        self.RSTD = self.AF_(TN); self.tRSTD = Tok()
        self.prenorm(g, l, 0)
        U = self.AF_(2, N); UB = self.AB_(2, N); tU = toks(2)
        XLs = self.AF_(2, g.nb, 3 + g.L); tXL = toks(2)
        GL = self.AF_(2, N); tGL = toks(2)
        wb, twb = self.getA()
        for c in range(2):
            ps, tps = self.proj_chunk(wb, twb, c * 128, N, self.HB, self.tHB)
            self.cp("act", U[:, c, :], ps[:, 0:N], [tps], [tU[c]])
            self.cp("dve", UB[:, c, :], ps[:, 0:N], [tps], [tU[c]])
        self.prefetchA()
        wb, twb = self.getA()
        for c in range(2):
            ps, tps = self.proj_chunk(wb, twb, c * 128, N, self.HB, self.tHB)
            self.cp("act", XLs[:, c, :, 3:3 + g.L], ps[:, 0:N].rearrange("p (b t) -> p b t", b=g.nb), [tps], [tXL[c]])
        self.prefetchA()
        wb, twb = self.getA()
        for c in range(2):
            ps, tps = self.proj_chunk(wb, twb, c * 128, N, self.HB, self.tHB)
            self.cp("act", GL[:, c, :], ps[:, 0:N], [tps], [tGL[c]])
        self.prefetchA()
        markF, markB = self.af, self.ab
        if g.kind == "p":
            self.s5_prompt(g, l, U, UB, tU)
        else:
            self.s5_sample(g, l, U, UB, tU)
        self.P.barrier()
        self.af, self.ab = markF, markB
        self.GT = [self.AF_(N), self.AF_(N)]; self.tGT = Tok()
        self.lru(g, l, XLs, tXL, GL, tGL)
        self.arena_reset()
        self.SQ = [self.AB_(TN), self.AB_(TN)]; self.tSQ = toks(2)
        self.gdn(g, l)
        if l == 0 and g.idx == 0:
            self.dbg("mix_%s" % g.kind, self.MIXB[:, :, 0:N], self.tMIX, [128, 8, N], bf=True)
        self.arena_reset()
        self.SQ = [self.AB_(TN), self.AB_(TN)]; self.tSQ = toks(2)
        self.RSTD = self.AF_(TN); self.tRSTD = Tok()
        MO = self.AF_(8, N); tMO = toks(8)
        if l == self.nl - 1 and self.next_g is not None:
            ng = self.next_g
            self.XN = self.AF_(8, TN); self.tXN = Tok()
            nsrc = self.xs if ng.kind == "s" else self.xp[:, ng.idx * TN:(ng.idx + 1) * TN]
            self.ld(self.XN[:, :, 0:ng.N], nsrc.rearrange("(c p) t -> p c t", p=128), [self.tXN])
        nps, tnps = self.PSN, self.tPSN
        sc = 1.0 / math.sqrt(1024.0)
        for blk in range(4):
            wb, twb = self.getA()
            for c in range(2):
                oc = blk * 2 + c
                ps, tps = self.proj_chunk(wb, twb, c * 128, N, self.MIXB, self.tMIX)
                self.cp("act", MO[:, oc, :], ps[:, 0:N], [tps], [tMO[oc]])
                sq = self.SQ[oc % 2]; tsq = self.tSQ[oc % 2]
                self.act(sq[:, 0:N], ps[:, 0:N], AF.Square, [tps], [tsq], scale=sc)
                self.mm(nps[:, 0:N], self.CB[:, CB_ONES:CB_ONES + 128], sq[:, 0:N], [tsq, self.tCB], [tnps],
                        start=(oc == 0), stop=(oc == 7))
            self.prefetchA()
        if l == 0:
            self.dbg("mo_%s%d" % (g.kind, g.idx), MO, tMO, [128, 8, N])
        self.postnorm(g, l, 1, MO, tMO, nps, tnps)
        if l == 0 and g.idx == 0:
            self.dbg("x1_%s" % g.kind, self.X[:, :, 0:N], self.tX, [128, 8, N])
        self.prenorm(g, l, 2)
        if l == 0 and g.idx == 0:
            self.dbg("hb2_%s" % g.kind, self.HB[:, :, 0:N], self.tHB, [128, 8, N], bf=True)
        self.prefetchB()
        HID = self.AB_(22, N); tH = toks(22)
        SG = [self.AF_(N), self.AF_(N)]; tSG = toks(2)
        for c0 in range(0, FFN_H, 256):
            w = 256
            wg, twg = self.getA()
            wu, twu = self.getA()
            for c in range(w // 128):
                hc = c0 // 128 + c
                pg, tpg = self.proj_chunk(wg, twg, c * 128, N, self.HB, self.tHB)
                pu, tpu = self.proj_chunk(wu, twu, c * 128, N, self.HB, self.tHB)
                sg = SG[hc % 2]; tsg = tSG[hc % 2]
                self.act(sg, pg[:, 0:N], AF.Silu, [tpg], [tsg])
                self.tt("dve", HID[:, hc, :], pu[:, 0:N], sg, ALU.mult, [tpu, tsg], [tH[hc]])
            self.prefetchA()
        FO = MO; tFO = tMO
        nps, tnps = self.PSN, self.tPSN
        for blk in range(8):
            wb, twb = self.getB()
            for c in range(1):
                oc = blk
                ps, tps = self.proj_chunk(wb, twb, 0, N, HID, tH, nk=22)
                self.prefetchB()
                self.cp("act", FO[:, oc, :], ps[:, 0:N], [tps], [tFO[oc]])
                sq = self.SQ[oc % 2]; tsq = self.tSQ[oc % 2]
                self.act(sq[:, 0:N], ps[:, 0:N], AF.Square, [tps], [tsq], scale=sc)
                self.mm(nps[:, 0:N], self.CB[:, CB_ONES:CB_ONES + 128], sq[:, 0:N], [tsq, self.tCB], [tnps],
                        start=(oc == 0), stop=(oc == 7))
        if l == 0 and g.idx == 0:
            self.dbg("hid_%s" % g.kind, HID, tH, [128, 22, N], bf=True)
            self.dbg("fo_%s" % g.kind, FO, tFO, [128, 8, N])
        self.postnorm(g, l, 3, FO, tFO, nps, tnps)
        if l == 0 and g.idx == 0:
            self.dbg("x2_%s" % g.kind, self.X[:, :, 0:N], self.tX, [128, 8, N])
        if l == self.nl - 1:
            dst = self.ys if g.kind == "s" else self.yp[:, g.idx * TN:(g.idx + 1) * TN]
            dv = dst.rearrange("(c p) t -> p c t", p=128)
            for c in range(8):
                self.st(dv[:, c, :], self.X[:, c, 0:N], [self.tX[c]])

    def gelu(self, out, x, N, r, w, gt=None, tg=None):
        gt = gt or self.GT
        tg = tg or self.tGT
        t1 = gt[0]; t2 = gt[1]
        self.act(t1[:, 0:N], x, AF.Square, r, [tg])
        self.ts("dve", t1[:, 0:N], t1[:, 0:N], 0.044715, ALU.mult, [tg], [tg], s2=1.0, op1=ALU.add)
        self.tt("dve", t1[:, 0:N], t1[:, 0:N], x, ALU.mult, r + [tg], [tg])
        self.act(t2[:, 0:N], t1[:, 0:N], AF.Sigmoid, [tg], [tg], scale=1.5957691216057308)
        self.tt("dve", out, t2[:, 0:N], x, ALU.mult, r + [tg], w + [tg])

    def s5_out(self, g, l, U, tU, XR, XI, tXR, tXI=None):
        tXI = tXI or tXR
        N = g.N
        tl = self.tL[l]
        self.GT = [self.AF_(N), self.AF_(N)]; self.tGT = Tok()
        Y1 = self.AF_(2, N); Y1B = self.AB_(2, N); tY = toks(2)
        for uc in range(2):
            ps, tps = self.psum()
            for j in range(4):
                sc = uc * 4 + j
                self.mm(ps[:, 0:N], self.CRE[l][:, sc, :], XR[:, sc, :], [tl, tXR[sc]], [tps], start=(j == 0), stop=False)
                self.mm(ps[:, 0:N], self.NCIM[l][:, sc, :], XI[:, sc, :], [tl, tXI[sc]], [tps], start=False, stop=(j == 3))
            self.stt(Y1[:, uc, :], U[:, uc, :], self.pp(l, 56 + uc), ps[:, 0:N], ALU.mult, ALU.add,
                     [tU[uc], tps, self.tPP], [tY[uc]])
            self.gelu(Y1[:, uc, :], Y1[:, uc, :], N, [tY[uc]], [tY[uc]])
            self.cp("act", Y1B[:, uc, :], Y1[:, uc, :], [tY[uc]], [tY[uc]])
        for oc in range(2):
            ps, tps = self.psum()
            for k in range(2):
                self.mm(ps[:, 0:N], self.WGLU[l][:, k, oc * 128:(oc + 1) * 128], Y1B[:, k, :], [tl, tY[k]], [tps],
                        start=(k == 0), stop=(k == 1))
            sg = self.GT[0]
            self.act(sg[:, 0:N], ps[:, 0:N], AF.Sigmoid, [tps, self.tPP], [self.tGT], bias=self.pp(l, 58 + oc))
            self.tt("dve", self.MIXB[:, oc, 0:N], sg[:, 0:N], Y1[:, oc, :], ALU.mult, [self.tGT, tY[oc]], [self.tMIX[oc]])

    def s5_prompt(self, g, l, U, UB, tU):
        N = g.N
        NS = N // SEG
        tl = self.tL[l]
        C = self.S5C[l]
        XR = self.AB_(8, N); XI = self.AB_(8, N); tXR = toks(8); tXI = toks(8)
        sets = [([self.AF_(N) for _ in range(6)], toks(6)) for _ in range(2)]
        tI = self.tS5I[l]
        I = self.S5I[l]
        last = (g.idx == NT - 1)
        v = lambda ap: ap.rearrange("p (s k) -> p s k", s=NS)
        T4 = [self.AF_(16) for _ in range(2)]; tT4 = toks(2)

        def chunk(sc):
            uc = sc // 4
            (PRs, PIs, A, B, C2, D), (tPR, tPI, tA, tB, tC2, tD) = sets[sc % 2]
            t4 = T4[sc % 2]; tt4 = tT4[sc % 2]
            pr, tpr = self.psum()
            pi, tpi = self.psum()
            self.mm(pr[:, 0:N], self.BRE[l][:, sc, :], UB[:, uc, :], [tl, tU[uc]], [tpr])
            self.mm(pi[:, 0:N], self.BIM[l][:, sc, :], UB[:, uc, :], [tl, tU[uc]], [tpi])
            tc_, ts_ = self.TC[l][:, sc, :], self.TS[l][:, sc, :]
            TCb = tc_.unsqueeze(1).to_broadcast([128, NS, SEG]); TSb = ts_.unsqueeze(1).to_broadcast([128, NS, SEG])
            rho = C[:, 0, sc:sc + 1]
            prv = pr[:, 0:N].rearrange("p (s k) -> p s k", s=NS); piv = pi[:, 0:N].rearrange("p (s k) -> p s k", s=NS)
            self.tt("dve", v(A), prv, TCb, ALU.mult, [tpr, tl], [tA])
            self.tt("dve", v(B), piv, TSb, ALU.mult, [tpi, tl], [tB])
            self.tt("dve", v(C2), piv, TCb, ALU.mult, [tpi, tl], [tC2])
            self.tt("dve", v(D), prv, TSb, ALU.mult, [tpr, tl], [tD])
            self.tt("dve", A, A, B, ALU.add, [tA, tB], [tA])
            self.tt("pool", C2, C2, D, ALU.subtract, [tC2, tD], [tC2])
            for s_ in range(NS):
                cols = slice(s_ * SEG, (s_ + 1) * SEG)
                self.P.op("dve", lambda e, o=PRs[:, cols], d1=A[:, cols], rho=rho, ii=I[:, 0, sc:sc + 1]: e.tensor_tensor_scan(
                    out=o, data0=rho.to_broadcast([128, SEG]), data1=d1, initial=ii, op0=ALU.mult, op1=ALU.add),
                    [tA, tI[sc], tl], [tPR])
                self.P.op("dve", lambda e, o=PIs[:, cols], d1=C2[:, cols], rho=rho, ii=I[:, 1, sc:sc + 1]: e.tensor_tensor_scan(
                    out=o, data0=rho.to_broadcast([128, SEG]), data1=d1, initial=ii, op0=ALU.mult, op1=ALU.add),
                    [tC2, tI[sc], tl], [tPI])
                e0 = s_ * SEG + SEG - 1
                we_r, we_i = PRs[:, e0:e0 + 1], PIs[:, e0:e0 + 1]
                if last and s_ == NS - 1:
                    cr, ci, nci = tc_[:, SEG - 1:SEG], ts_[:, SEG - 1:SEG], self.NTSL[l][:, sc:sc + 1]
                else:
                    cr, ci, nci = C[:, 1, sc:sc + 1], C[:, 2, sc:sc + 1], C[:, 5, sc:sc + 1]
                self.act(t4[:, 0:1], we_i, AF.Copy, [tPI, tl], [tt4], scale=nci)
                self.act(I[:, 0, sc:sc + 1], we_r, AF.Identity, [tPR, tt4, tl], [tI[sc]], scale=cr, bias=t4[:, 0:1])
                self.act(t4[:, 1:2], we_r, AF.Copy, [tPR, tl], [tt4], scale=ci)
                self.act(I[:, 1, sc:sc + 1], we_i, AF.Identity, [tPI, tt4, tl], [tI[sc]], scale=cr, bias=t4[:, 1:2])
            self.tt("dve", v(B), v(PRs), TCb, ALU.mult, [tPR, tl], [tB])
            self.tt("pool", v(D), v(PIs), TSb, ALU.mult, [tPI, tl], [tD])
            self.tt("dve", XR[:, sc, :], B, D, ALU.subtract, [tB, tD], [tXR[sc]])
            self.tt("pool", v(A), v(PRs), TSb, ALU.mult, [tPR, tl], [tA])
            self.tt("dve", v(C2), v(PIs), TCb, ALU.mult, [tPI, tl], [tC2])
            self.tt("pool", XI[:, sc, :], A, C2, ALU.add, [tA, tC2], [tXI[sc]])

        for sc in range(0, 8, 2):
            o0 = self.capture(lambda: chunk(sc), self.PS_F)
            o1 = self.capture(lambda: chunk(sc + 1), self.PS_C)
            self.P.replay(o0, o1, ratio=1.0)
        if last:
            self.st(self.o_s5[l].rearrange("a p c -> p a c"), I[:, :, 0:8], tI)
        if l == 0 and g.idx == 0:
            self.dbg("xr_p", XR, tXR, [128, 8, N], bf=True)
            self.dbg("xi_p", XI, tXI, [128, 8, N], bf=True)
        self.s5_out(g, l, U, tU, XR, XI, tXR, tXI)

    def s5_sample(self, g, l, U, UB, tU):
        N = g.N
        tl = self.tL[l]
        C = self.S5C[l]
        XR = self.AB_(8, N); XI = self.AB_(8, N); tXR = toks(8)
        BR = self.AF_(8, N); BI = self.AF_(8, N); tB = toks(8)
        ST = self.AF_(2, 8, NSB); tST = Tok()
        T = [self.AF_(8, NSB) for _ in range(4)]; tT = Tok()
        self.ld(ST, self.i_s5[l].rearrange("a p c b -> p a c b"), [tST])
        for sc in range(8):
            uc = sc // 4
            pr, tpr = self.psum()
            pi, tpi = self.psum()
            self.mm(pr[:, 0:N], self.BRE[l][:, sc, :], UB[:, uc, :], [tl, tU[uc]], [tpr])
            self.mm(pi[:, 0:N], self.BIM[l][:, sc, :], UB[:, uc, :], [tl, tU[uc]], [tpi])
            self.cp("act", BR[:, sc, :], pr[:, 0:N], [tpr], [tB[sc]])
            self.cp("act", BI[:, sc, :], pi[:, 0:N], [tpi], [tB[sc]])
        arb = C[:, 3, 0:8].unsqueeze(2).to_broadcast([128, 8, NSB])
        aib = C[:, 4, 0:8].unsqueeze(2).to_broadcast([128, 8, NSB])
        BRv = BR.rearrange("p c (b t) -> p c b t", t=SL); BIv = BI.rearrange("p c (b t) -> p c b t", t=SL)
        XRv = XR.rearrange("p c (b t) -> p c b t", t=SL); XIv = XI.rearrange("p c (b t) -> p c b t", t=SL)
        xr, xi = ST[:, 0], ST[:, 1]
        for t in range(SL):
            self.tt("dve", T[0], xr, arb, ALU.mult, [tST, tl], [tT])
            self.tt("dve", T[1], xi, aib, ALU.mult, [tST, tl], [tT])
            self.tt("dve", T[2], xr, aib, ALU.mult, [tST, tl], [tT])
            self.tt("dve", T[3], xi, arb, ALU.mult, [tST, tl], [tT])
            self.tt("dve", T[0], T[0], T[1], ALU.subtract, [tT], [tT])
            self.tt("dve", T[2], T[2], T[3], ALU.add, [tT], [tT])
            self.tt("dve", xr, T[0], BRv[:, :, :, t], ALU.add, [tT] + tB, [tST])
            self.tt("dve", xi, T[2], BIv[:, :, :, t], ALU.add, [tT] + tB, [tST])
            self.cp("dve", XRv[:, :, :, t], xr, [tST], tXR)
            self.cp("dve", XIv[:, :, :, t], xi, [tST], tXR)
        self.st(self.os_s5[l].rearrange("a p c b -> p a c b"), ST, [tST])
        self.s5_out(g, l, U, tU, XR, XI, tXR)

    def conv(self, out3, stg, wbase, l, c, L, r, w, bias=None):
        if bias is None:
            self.ts("dve", out3, stg[:, :, 0:L], self.pp(l, wbase + c * 4), ALU.mult, r + [self.tPP], w)
        else:
            self.ts("dve", out3, stg[:, :, 0:L], self.pp(l, wbase + c * 4), ALU.mult, r + [self.tPP], w, s2=bias, op1=ALU.add)
        for j in range(1, 4):
            self.stt(out3, stg[:, :, j:j + L], self.pp(l, wbase + c * 4 + j), out3, ALU.mult, ALU.add, r + w + [self.tPP], w)

    def lru(self, g, l, XLs, tXL, GL, tGL):
        N, nb, L = g.N, g.nb, g.L
        tl = self.tL[l]
        XC = self.AF_(2, N); XCB = self.AB_(2, N); tXC = toks(2)
        RG = self.AF_(2, N); GI = self.AF_(2, N); tG = toks(2)
        A = self.AF_(N); M_ = self.AF_(N); B_ = self.AF_(N); H = self.AF_(N); tA = Tok()
        HS = self.AF_(2, NSB); tHS = Tok()
        v3 = lambda ap: ap.rearrange("p (b t) -> p b t", b=nb)
        if g.kind == "s":
            LH = self.AF_(2, NSB, 3); LO = self.AF_(2, NSB, 3); tLH = Tok(); tLO = Tok()
            self.ld(LH, self.i_lruconv[l], [tLH])
        for c in range(2):
            if g.kind == "p":
                self.cp("dve", XLs[:, c, 0, 0:3], self.HXL[l][:, c, 0:3], [self.tHXL[l]], [tXL[c]])
            else:
                self.cp("dve", XLs[:, c, :, 0:3], LH[:, c], [tLH], [tXL[c]])
        if g.kind == "s":
            self.ld(HS, self.i_lruh[l], [tHS])
        for c in range(2):
            self.conv(v3(XC[:, c, :]), XLs[:, c], 60, l, c, L, [tXL[c]], [tXC[c]], bias=self.pp(l, 68 + c))
            self.cp("act", XCB[:, c, :], XC[:, c, :], [tXC[c]], [tXC[c]])
            if g.kind == "p":
                if g.idx == NT - 1:
                    pass
            else:
                self.cp("pool", LO[:, c], XLs[:, c, :, L:L + 3], [tXL[c]], [tLO])
        if g.kind == "s":
            self.st(self.os_lruconv[l], LO, [tLO])
        if g.kind == "p":
            for c in range(2):
                self.cp("dve", self.HXL[l][:, c, 0:3], XLs[:, c, 0, L:L + 3], [tXL[c]], [self.tHXL[l]])
            if g.idx == NT - 1:
                self.st(self.o_lruconv[l], self.HXL[l][:, :, 0:3], [self.tHXL[l]])
        for c in range(2):
            for a in range(2):
                ps, tps = self.psum()
                self.mm(ps[:, 0:N], self.WAX[l][:, a, c, :], XCB[:, c, :], [tl, tXC[c]], [tps])
                dst = RG if a == 0 else GI
                self.act(dst[:, c, :], ps[:, 0:N], AF.Sigmoid, [tps, self.tPP], [tG[c]], bias=self.pp(l, 70 + a * 2 + c))
        sets = [(A, M_, B_, H, self.GT, tA, self.tGT),
                (self.AF_(N), self.AF_(N), self.AF_(N), self.AF_(N), [self.AF_(N), self.AF_(N)], Tok(), Tok())]

        def chunk(c):
            A, M_, B_, H, gt, tA, tg = sets[c]
            self.act(A[:], RG[:, c, :], AF.Exp, [tG[c], tl], [tA], scale=self.LC[l][:, c:c + 1])
            self.act(M_[:], A[:], AF.Square, [tA], [tA])
            self.act(M_[:], M_[:], AF.Sqrt, [tA], [tA], scale=-1.0, bias=1.0)
            if g.kind == "p" and g.idx == 0:
                self.memset("dve", M_[:, 0:1], 1.0, [tA])
            self.tt("dve", B_[:], GI[:, c, :], XC[:, c, :], ALU.mult, [tG[c], tXC[c]], [tA])
            self.tt("dve", B_[:], B_[:], M_[:], ALU.mult, [tA], [tA])
            if g.kind == "p":
                self.P.op("dve", lambda e, c=c: e.tensor_tensor_scan(
                    out=H[:], data0=A[:], data1=B_[:], initial=self.HL[l][:, c:c + 1], op0=ALU.mult, op1=ALU.add),
                    [tA, self.tHLc[l][c]], [tA])
                self.cp("dve", self.HL[l][:, c:c + 1], H[:, N - 1:N], [tA], [self.tHLc[l][c]])
            else:
                Av, Bv, Hv = v3(A[:]), v3(B_[:]), v3(H[:])
                hp = HS[:, c, :]
                for t in range(L):
                    self.tt("dve", Hv[:, :, t], Av[:, :, t], hp, ALU.mult, [tA, tHSc[c]], [tA])
                    self.tt("dve", Hv[:, :, t], Hv[:, :, t], Bv[:, :, t], ALU.add, [tA], [tA])
                    hp = Hv[:, :, t]
                self.cp("dve", HS[:, c, :], Hv[:, :, L - 1], [tA], [tHSc[c]])
            self.gelu(GL[:, c, :], GL[:, c, :], N, [tGL[c]], [tGL[c]], gt=gt, tg=tg)
            self.tt("dve", self.MIXB[:, 2 + c, 0:N], H[:], GL[:, c, :], ALU.mult, [tA, tGL[c]], [self.tMIX[2 + c]])

        tHSc = [tHS, Tok()]
        if g.kind == "s":
            tHSc[1].w = tHS.w
        o0 = self.capture(lambda: chunk(0), self.PS_F)
        o1 = self.capture(lambda: chunk(1), self.PS_C)
        self.P.replay(o0, o1, ratio=1.0)
        tHS = None
        if g.kind == "p":
            if g.idx == NT - 1:
                self.st(self.o_lruh[l], self.HL[l][:, 0:2], self.tHLc[l])
        else:
            self.st(self.os_lruh[l], HS, tHSc)

    def psb(self):
        i = self.psbn
        self.psbn = (self.psbn + 1) % 2
        return self.PSB[:, i, :], self.tPSB[i]

    def gdn(self, g, l):
        N, nb, L, Pn = g.N, g.nb, g.L, g.P
        tl = self.tL[l]
        smp = (g.kind == "s")
        QT = self.AB_(4, N); KT = self.AB_(4, N); VT = self.AB_(4, N); SZ = self.AB_(4, N)
        tQ = toks(4); tK = toks(4); tV = toks(4); tZ = toks(4)
        STG = [self.AF_(nb, 3 + L) for _ in range(2)]; tSTG = toks(2)
        CV = [self.AF_(N) for _ in range(4)]; tCV = toks(4)
        RN = [self.AF_(N) for _ in range(4)]; tRN = toks(4)
        v3 = lambda ap: ap.rearrange("p (b t) -> p b t", b=nb)
        n = 0
        if smp:
            HALL = self.AF_(12, NSB, 3); HOUT = self.AF_(12, NSB, 3); tHALL = Tok(); tHOUT = Tok()
            self.ld(HALL, self.i_gdnconv[l], [tHALL])
        for grp in range(3):
            for h in range(4):
                c = grp * 4 + h
                if h % 2 == 0:
                    if h:
                        self.prefetchA()
                    wb, twb = self.getA()
                ps, tps = self.proj_chunk(wb, twb, (h % 2) * 128, N, self.HB, self.tHB)
                stg = STG[n % 2]; tst = tSTG[n % 2]
                n += 1
                cv = CV[h]; tcv = tCV[h]
                self.cp("act", stg[:, :, 3:3 + L], v3(ps[:, 0:N]), [tps], [tst])
                if smp:
                    self.cp("dve", stg[:, :, 0:3], HALL[:, c], [tHALL], [tst])
                    self.cp("pool", HOUT[:, c], stg[:, :, L:L + 3], [tst], [tHOUT])
                else:
                    self.cp("dve", stg[:, 0, 0:3], self.HQKV[l][:, c, 0:3], [self.tHQ[l]], [tst])
                    self.cp("dve", self.HQKV[l][:, c, 0:3], stg[:, 0, L:L + 3], [tst], [self.tHQ[l]])
                self.conv(v3(cv), stg, 76, l, c, L, [tst], [tcv])
            self.prefetchA()
            if grp == 2:
                for h in range(4):
                    self.act(VT[:, h, :], CV[h], AF.Silu, [tCV[h]], [tV[h]])
                continue
            for h in range(4):
                self.act(CV[h], CV[h], AF.Silu, [tCV[h]], [tCV[h]])
            p2s = []
            for h in range(4):
                sq = self.SQ[h % 2]; tsq = self.tSQ[h % 2]
                self.act(sq[:, 0:N], CV[h], AF.Square, [tCV[h]], [tsq])
                p2, tp2 = self.psum()
                self.mm(p2[:, 0:N], self.CB[:, CB_ONES:CB_ONES + 128], sq[:, 0:N], [tsq, self.tCB], [tp2])
                p2s.append((p2, tp2))
            for h in range(4):
                self.act(RN[h], p2s[h][0][:, 0:N], AF.Ln, [p2s[h][1], self.tC], [tRN[h]], bias=self.EPS[:, 0:1])
            for h in range(4):
                self.act(RN[h], RN[h], AF.Exp, [tRN[h]], [tRN[h]], scale=-0.5)
            dst, td = (QT, tQ) if grp == 0 else (KT, tK)
            for h in range(4):
                self.stt(dst[:, h, :], CV[h], (128.0 ** -0.5) if grp == 0 else 1.0, RN[h], ALU.mult, ALU.mult,
                         [tCV[h], tRN[h]], [td[h]])
        if smp:
            self.st(self.os_gdnconv[l], HOUT, [tHOUT])
        if (not smp) and g.idx == NT - 1:
            self.st(self.o_gdnconv[l], self.HQKV[l][:, :, 0:3], [self.tHQ[l]])
        for h in range(4):
            if h % 2 == 0:
                if h:
                    self.prefetchA()
                wb, twb = self.getA()
            ps, tps = self.proj_chunk(wb, twb, (h % 2) * 128, N, self.HB, self.tHB)
            self.act(SZ[:, h, :], ps[:, 0:N], AF.Silu, [tps], [tZ[h]])
        self.prefetchA()
        nblk = g.nblk
        GTM = self.AF_(nblk, 4); BTM = self.AF_(nblk, 4); tGB = Tok()
        ps, tps = self.psum()
        for blk in range(nblk):
            for k in range(8):
                self.mm(ps[0:Pn, blk * 8:blk * 8 + 8], self.HB[:, k, blk * Pn:(blk + 1) * Pn], self.WAB[l][:, k, :],
                        [self.tHB[k], tl], [tps], start=(k == 0), stop=(k == 7))
        pv = ps[0:Pn, 0:nblk * 8].rearrange("p (b c) -> p b c", c=8)
        G_, B_ = GTM[0:Pn], BTM[0:Pn]
        dtb = self.GR[l][0:Pn, 1, 0:4].unsqueeze(1).to_broadcast([Pn, nblk, 4])
        nea = self.GR[l][0:Pn, 0, 0:4].unsqueeze(1).to_broadcast([Pn, nblk, 4])
        self.tt("dve", G_, pv[:, :, 0:4], dtb, ALU.add, [tps, tl], [tGB])
        self.act(G_, G_, AF.Exp, [tGB], [tGB])
        self.act(G_, G_, AF.Ln, [tGB], [tGB], bias=1.0)
        self.tt("dve", G_, G_, nea, ALU.mult, [tGB, tl], [tGB])
        self.act(B_, pv[:, :, 4:8], AF.Sigmoid, [tps], [tGB])
        f4 = lambda: self.AF_(4, 128)
        b4 = lambda: self.AB_(4, 128)
        nset = 2 if nblk > 1 else 1
        GBc = f4(); D1 = f4(); R1 = f4(); EG = f4(); OT = f4(); RT = f4()
        DECB = b4(); DECT = b4(); Mm = b4(); Nm = b4(); VN = b4(); Ub = b4(); XmL = b4(); XmU = b4()
        Dl = [b4(), b4()]; Du = [b4(), b4()]
        VBs = [f4() for _ in range(nset)]
        SMs = [self.AF_(8, 16) for _ in range(nset)]
        KDs = [b4()] + [self.AFB_(4, 128) for _ in range(nset - 1)]
        ATTs = [b4()] + [self.AFB_(4, 128) for _ in range(nset - 1)]
        QDs = [b4()] + [self.AFB_(4, 128) for _ in range(nset - 1)]
        TTs = [self.AFB_(4, 128) for _ in range(nset)]
        tbs = toks(nset); tTTs = toks(nset)
        tD = Tok(); tM = Tok(); tDl = toks(2); tDu = toks(2); tX = toks(2); tUb = toks(4); tVN = toks(4); tOT = Tok()
        idb = self.CB[0:Pn, CB_ID:CB_ID + Pn]
        idbh = idb.unsqueeze(1).to_broadcast([Pn, 4, Pn])
        if smp:
            TRI, SEGM, MSTR, MINCT = (self.cf(o_, Pn, Pn) for o_ in (CF_TRIS, CF_SEGS, CF_MSTRS, CF_MINCTS))
            levels = [1, 2]
        else:
            TRI, SEGM, MSTR, MINCT = (self.cf(o_, Pn, Pn) for o_ in (CF_TRI, CF_ONES, CF_MSTR, CF_MINCT))
            levels = NLEV
        bh = lambda ap: ap.unsqueeze(1).to_broadcast([Pn, 4, Pn])
        h4 = lambda p_: p_[0:Pn, 0:4 * Pn].rearrange("p (h j) -> p h j", h=4)

        def front(blk):
            par = blk % nset
            VB, SM, KD, ATT, QD, TTf, tb, tTT = VBs[par], SMs[par], KDs[par], ATTs[par], QDs[par], TTs[par], tbs[par], tTTs[par]
            cols = slice(blk * Pn, (blk + 1) * Pn)
            gcol = GTM[0:Pn, blk, :]; bcol = BTM[0:Pn, blk, :]
            gc, gl, kdf, nbeg, egt, gam = (SM[0:Pn, i, 0:4] for i in range(6))
            ps, tps = self.psum()
            self.mm(ps[0:Pn, 0:4], TRI, gcol, [self.tCF, tGB], [tps])
            self.mm(ps[0:Pn, 4:8], SEGM, gcol, [self.tCF, tGB], [tps])
            self.cp("dve", gc, ps[0:Pn, 0:4], [tps], [tb])
            self.cp("dve", gl, ps[0:Pn, 4:8], [tps], [tb])
            self.cp("dve", GBc[0:Pn], gcol.unsqueeze(2).to_broadcast([Pn, 4, 128]), [tGB], [tD])
            pg, tpg = self.psum()
            pgv = pg[:, 0:4 * Pn].rearrange("p (h j) -> p h j", h=4)
            for h in range(4):
                self.mm(pgv[:, h, :], GBc[0:Pn, h, :], TRI, [tD, self.tCF], [tpg])
            d1 = D1[0:Pn, :, 0:Pn]; r1 = R1[0:Pn, :, 0:Pn]
            self.tt("dve", d1, pgv[0:Pn], gc.unsqueeze(2).to_broadcast([Pn, 4, Pn]), ALU.subtract, [tpg, tb], [tD])
            self.ts("dve", d1, d1, -60.0, ALU.max, [tD], [tD], s2=60.0, op1=ALU.min)
            eg = EG[:, :, 0:Pn]
            self.act(eg, pgv, AF.Exp, [tpg], [tD])
            BM = GBc[0:Pn, :, 0:Pn]
            self.tt("pool", BM, bh(MSTR), bcol.unsqueeze(2).to_broadcast([Pn, 4, Pn]), ALU.mult, [tpg, tGB, self.tCF], [tD])
            self.act(r1, d1, AF.Exp, [tD], [tD], scale=-1.0)
            self.act(d1, d1, AF.Exp, [tD], [tD])
            decb = DECB[0:Pn, :, 0:Pn]; dect = DECT[0:Pn, :, 0:Pn]
            self.stt(decb, r1, 1.0, BM, ALU.min, ALU.mult, [tD], [tD])
            self.stt(dect, d1, 1.0, bh(MINCT), ALU.min, ALU.mult, [tD, self.tCF], [tD])
            self.tt("dve", kdf, gl, gc, ALU.subtract, [tb], [tb])
            self.act(kdf, kdf, AF.Exp, [tb], [tb])
            self.act(egt, gc, AF.Exp, [tb], [tb])
            self.stt(nbeg, egt, -1.0, bcol, ALU.mult, ALU.mult, [tb, tGB], [tb])
            if not smp:
                self.act(gam, gl, AF.Exp, [tb], [tb])
            pb, tpb = self.psb()
            pbv = pb[0:Pn, :].rearrange("p (h d) -> p h d", h=4)
            for h in range(4):
                self.tr(pbv[:, h, :], VT[:, h, cols], self.CB[:, CB_ID:CB_ID + 128], [tV[h], self.tCB], [tpb])
            self.tt("dve", VB[0:Pn], pbv, bcol.unsqueeze(2).to_broadcast([Pn, 4, 128]), ALU.mult, [tpb, tGB], [tb])
            pb, tpb = self.psb()
            pbv = pb[0:Pn, :].rearrange("p (h d) -> p h d", h=4)
            for h in range(4):
                self.tr(pbv[:, h, :], KT[:, h, cols], self.CB[:, CB_ID:CB_ID + 128], [tK[h], self.tCB], [tpb])
            self.tt("dve", KD[0:Pn], pbv, kdf.unsqueeze(2).to_broadcast([Pn, 4, 128]), ALU.mult, [tpb, tb], [tb])
            pk, tpk = self.psum()
            pkv = h4(pk)
            for h in range(4):
                self.mm(pkv[:, h, :], KT[:, h, cols], KT[:, h, cols], [tK[h]], [tpk])
            mm_ = Mm[0:Pn, :, 0:Pn]; nm_ = Nm[0:Pn, :, 0:Pn]
            self.tt("dve", mm_, pkv, decb, ALU.mult, [tpk, tD], [tM])
            pb, tpb = self.psb()
            pbv = pb[0:Pn, 0:4 * Pn].rearrange("p (h d) -> p h d", h=4)
            for h in range(4):
                self.tr(pbv[:, h, :], mm_[:, h, :], idb, [tM, self.tCB], [tpb])
            self.cp("act", nm_, pbv, [tpb], [tM])
            pq, tpq = self.psum()
            pqv = h4(pq)
            for h in range(4):
                self.mm(pqv[:, h, :], KT[:, h, cols], QT[:, h, cols], [tK[h], tQ[h]], [tpq])
            att = ATT[0:Pn, :, 0:Pn]
            self.tt("dve", att, pqv, dect, ALU.mult, [tpq, tD], [tb])
            qd = QD[:, :, 0:Pn]
            self.tt("pool", qd, QT[:, :, cols], eg, ALU.mult, tQ + [tD], [tb])
            cur = 0
            dl = Dl[0][0:Pn, :, 0:Pn]; du = Du[0][0:Pn, :, 0:Pn]
            nml = lambda i: bh(self.CB[0:Pn, CB_NML + i * 128:CB_NML + i * 128 + Pn])
            nmu = lambda i: bh(self.CB[0:Pn, CB_NMU + i * 128:CB_NMU + i * 128 + Pn])
            self.tt("pool", dl, mm_, nml(0), ALU.mult, [tM, self.tCB], [tDl[0]])
            self.tt("pool", dl, dl, idbh, ALU.add, [tDl[0], self.tCB], [tDl[0]])
            self.tt("dve", du, nm_, nmu(0), ALU.mult, [tM, self.tCB], [tDu[0]])
            self.tt("dve", du, du, idbh, ALU.add, [tDu[0], self.tCB], [tDu[0]])
            for li in range(1, len(levels)):
                lastlev = (li == len(levels) - 1)
                dl = Dl[cur][0:Pn, :, 0:Pn]; du = Du[cur][0:Pn, :, 0:Pn]
                dl2 = Dl[1 - cur][0:Pn, :, 0:Pn]
                du2 = TTf[0:Pn, :, 0:Pn] if lastlev else Du[1 - cur][0:Pn, :, 0:Pn]
                tdl, tdu, tdl2 = tDl[cur], tDu[cur], tDl[1 - cur]
                tdu2 = tTT if lastlev else tDu[1 - cur]
                xl = XmL[0:Pn, :, 0:Pn]; xu = XmU[0:Pn, :, 0:Pn]
                p1, tp1 = self.psum()
                p1v = h4(p1)
                for h in range(4):
                    self.mm(p1v[:, h, :], mm_[:, h, :], du[:, h, :], [tM, tdu], [tp1])
                self.tt("dve", xu, p1v, nmu(li), ALU.mult, [tp1, self.tCB], [tX[1]])
                if not lastlev:
                    p2, tp2 = self.psum()
                    p2v = h4(p2)
                    for h in range(4):
                        self.mm(p2v[:, h, :], nm_[:, h, :], dl[:, h, :], [tM, tdl], [tp2])
                    self.tt("dve", xl, p2v, nml(li), ALU.mult, [tp2, self.tCB], [tX[0]])
                p3, tp3 = self.psum()
                p3v = h4(p3)
                for h in range(4):
                    self.mm(p3v[:, h, :], dl[:, h, :], xu[:, h, :], [tdl, tX[1]], [tp3], start=True, stop=False)
                    self.mm(p3v[:, h, :], idb, du[:, h, :], [self.tCB, tdu], [tp3], start=False, stop=True)
                self.cp("act", du2, p3v, [tp3], [tdu2])
                if not lastlev:
                    p4, tp4 = self.psum()
                    p4v = h4(p4)
                    for h in range(4):
                        self.mm(p4v[:, h, :], du[:, h, :], xl[:, h, :], [tdu, tX[0]], [tp4], start=True, stop=False)
                        self.mm(p4v[:, h, :], idb, dl[:, h, :], [self.tCB, tdl], [tp4], start=False, stop=True)
                    self.cp("act", dl2, p4v, [tp4], [tdl2])
                cur = 1 - cur

        def chain(blk):
            par = blk % nset
            VB, SM, KD, ATT, QD, TTf, tb, tTT = VBs[par], SMs[par], KDs[par], ATTs[par], QDs[par], TTs[par], tbs[par], tTTs[par]
            cols = slice(blk * Pn, (blk + 1) * Pn)
            gcol = GTM[0:Pn, blk, :]
            gc, gl, kdf, nbeg, egt, gam = (SM[0:Pn, i, 0:4] for i in range(6))
            att = ATT[0:Pn, :, 0:Pn]; qd = QD[:, :, 0:Pn]
            TT_ = TTf[0:Pn, :, 0:Pn]
            if smp:
                self.gdn_state_sample(g, l, KT, tK, KD, tb, TT_, tTT, VB, nbeg, qd, att, gcol, OT, tOT, Ub, tUb, VN, tVN, tGB)
            else:
                S_f, S_b, tS = self.SF[l], self.SBF[l], self.tS[l]
                v4 = lambda p_: p_[:, :].rearrange("p (h j) -> p h j", h=4)
                pks, tpks = self.psum()
                for h in range(4):
                    self.mm(v4(pks)[:, h, :], KT[:, h, cols], S_b[:, h, :], [tK[h], tS[h]], [tpks])
                for h in range(4):
                    self.stt(Ub[:, h, :], v4(pks)[:, h, :], nbeg[:, h:h + 1], VB[:, h, :], ALU.mult, ALU.add, [tpks, tb], [tUb[h]])
                pvn, tpvn = self.psum()
                for h in range(4):
                    self.mm(v4(pvn)[:, h, :], TT_[:, h, :], Ub[:, h, :], [tTT, tUb[h]], [tpvn])
                self.cp("act", VN[:, :, :], v4(pvn), [tpvn], tVN)
                pot, tpot = self.psum()
                for h in range(4):
                    self.mm(v4(pot)[:, h, :], S_b[:, h, :], qd[:, h, :], [tS[h], tb], [tpot], start=True, stop=False)
                    self.mm(v4(pot)[:, h, :], VN[:, h, :], att[:, h, :], [tVN[h], tb], [tpot], start=False, stop=True)
                self.cp("act", OT[:, :, 0:Pn], v4(pot), [tpot], [tOT])
                pss, tpss = self.psum()
                for h in range(4):
                    self.mm(v4(pss)[:, h, :], KD[:, h, :], VN[:, h, :], [tb, tVN[h]], [tpss])
                for h in range(4):
                    self.stt(S_f[:, h, :], S_f[:, h, :], gam[:, h:h + 1], v4(pss)[:, h, :], ALU.mult, ALU.add, [tb, tpss], [tS[h]])
                self.cp("pool", S_b[:, :, :], S_f[:, :, :], tS, tS)
            ot = OT[:, :, 0:Pn]
            sq = self.SQ[0]; tsq = self.tSQ[0]
            sqv = sq[:, 0:4 * Pn].rearrange("p (h j) -> p h j", h=4)
            self.act(sqv, ot, AF.Square, [tOT], [tsq], scale=1.0 / math.sqrt(128.0))
            pn, tpn = self.psum()
            self.mm(pn[:, 0:4 * Pn], self.CB[:, CB_ONES:CB_ONES + 128], sq[:, 0:4 * Pn], [tsq, self.tCB], [tpn])
            rt = RT[:, :, 0:Pn]
            self.act(rt, pn[:, 0:4 * Pn].rearrange("p (h j) -> p h j", h=4), AF.Ln, [tpn, self.tC], [tOT], bias=self.EPS[:, 0:1])
            self.act(rt, rt, AF.Exp, [tOT], [tOT], scale=-0.5)
            self.stt(ot, ot, self.pp(l, 124), rt, ALU.mult, ALU.mult, [tOT, self.tPP], [tOT])
            self.tt("dve", self.MIXB[:, 4:8, cols], ot, SZ[:, :, cols], ALU.mult, [tOT] + tZ, self.tMIX[4:8])

        if nblk == 1:
            front(0)
            chain(0)
        else:
            self.P.replay(self.capture(lambda: front(0), self.PS_F))
            for blk in range(nblk):
                c_ops = self.capture(lambda: chain(blk), self.PS_C)
                f_ops = self.capture(lambda: front(blk + 1), self.PS_F) if blk + 1 < nblk else []
                self.P.replay(f_ops, c_ops, ratio=(len(f_ops) / max(1, len(c_ops))))
        if (not smp) and g.idx == NT - 1:
            self.st(self.o_gdnS[l].rearrange("h d v -> d h v"), self.SF[l][:], self.tS[l])

    def gdn_state_sample(self, g, l, KT, tK, KD, tb, TT_, tTT, VB, nbeg, qd, att, gcol, OT, tOT, Ub, tUb, VN, tVN, tGB):
        Pn = 64
        S0 = [self.AF_(NSB, 128) for _ in range(2)]; tS0 = toks(2)
        S0B = [self.AB_(NSB, 128) for _ in range(2)]
        SN_ = [self.AF_(NSB, 128) for _ in range(1)]; tSN = toks(1)
        KTm = self.AB_(NSB, Pn); KDm = self.AB_(NSB, 128); tKm = Tok()
        GSI = self.AF_(4, NSB); GAM = self.AF_(4, NSB); tGm = Tok()
        OTB = self.AF_(4, Pn)
        sind = self.cf(CF_SIND, Pn, NSB)
        self.tt("dve", GSI[0:Pn], gcol.unsqueeze(2).to_broadcast([Pn, 4, NSB]), sind.unsqueeze(1).to_broadcast([Pn, 4, NSB]),
                ALU.mult, [self.tCF, tGB], [tGm])
        ps, tps = self.psum()
        self.mm(ps[:, 0:4 * NSB], self.cf(CF_ONES, Pn, 128), GSI[0:Pn].rearrange("p h b -> p (h b)"), [tGm, self.tCF], [tps])
        self.act(GAM.rearrange("p h b -> p (h b)"), ps[:, 0:4 * NSB], AF.Exp, [tps], [tGm])
        seqm = self.CB[:, CB_SEQM:CB_SEQM + 1024].rearrange("p (b i) -> p b i", b=NSB)
        for h in range(4):
            s0 = S0[h % 2]; s0b = S0B[h % 2]; ts0 = tS0[h % 2]; sn = SN_[0]; tsn = tSN[0]
            self.ld(s0, self.i_gdnS[l][:, h].rearrange("b d v -> d b v"), [ts0])
            self.cp("act", s0b, s0, [ts0], [ts0])
            self.tt("dve", KTm, KT[:, h, 0:Pn].unsqueeze(1).to_broadcast([128, NSB, Pn]), seqm, ALU.mult, [tK[h], self.tCB], [tKm])
            self.tt("dve", KDm[0:Pn], KD[0:Pn, h, :].unsqueeze(1).to_broadcast([Pn, NSB, 128]),
                    sind.unsqueeze(2).to_broadcast([Pn, NSB, 128]), ALU.mult, [tb, self.tCF], [tKm])
            pks, tpks = self.psum()
            for b in range(NSB):
                self.mm(pks[0:Pn, 0:128], KTm[:, b, :], s0b[:, b, :], [tKm, ts0], [tpks], start=(b == 0), stop=(b == NSB - 1))
            self.stt(Ub[0:Pn, h, :], pks[0:Pn, 0:128], nbeg[:, h:h + 1], VB[0:Pn, h, :], ALU.mult, ALU.add, [tpks, tb], [tUb[h]])
            pvn, tpvn = self.psum()
            self.mm(pvn[0:Pn, 0:128], TT_[:, h, :], Ub[0:Pn, h, :], [tTT, tUb[h]], [tpvn])
            self.cp("act", VN[0:Pn, h, :], pvn[0:Pn, 0:128], [tpvn], [tVN[h]])
            pota, tpota = self.psum()
            self.mm(pota[:, 0:Pn], VN[0:Pn, h, :], att[:, h, :], [tVN[h], tb], [tpota])
            potb, tpotb = self.psum()
            for b in range(NSB):
                self.mm(potb[:, b * SL:(b + 1) * SL], s0b[:, b, :], qd[:, h, b * SL:(b + 1) * SL], [ts0, tb], [tpotb])
            self.cp("act", OTB[:, h, :], potb[:, 0:Pn], [tpotb], [tOT])
            self.tt("dve", OT[:, h, 0:Pn], pota[:, 0:Pn], OTB[:, h, :], ALU.add, [tpota, tOT], [tOT])
            for b4 in range(NSB // 4):
                pss, tpss = self.psum()
                pssv = pss[:, :].rearrange("p (j v) -> p j v", j=4)
                for j in range(4):
                    b = b4 * 4 + j
                    self.mm(pssv[:, j, :], KDm[0:Pn, b, :], VN[0:Pn, h, :], [tKm, tVN[h]], [tpss])
                for j in range(4):
                    b = b4 * 4 + j
                    self.stt(sn[:, b, :], s0[:, b, :], GAM[:, h, b:b + 1], pssv[:, j, :], ALU.mult, ALU.add, [ts0, tGm, tpss], [tsn])
            self.st(self.os_gdnS[l][:, h].rearrange("b d v -> d b v"), sn, [tsn])


def _consts():
    cf = np.zeros((128, NCF), np.float32)
    i = np.arange(128)
    cf[:, CF_ID:CF_ID + 128] = np.eye(128)
    cf[:, CF_TRI:CF_TRI + 128] = (i[:, None] <= i[None, :])
    cf[:, CF_ONES:CF_ONES + 128] = 1.0
    cf[:, CF_MSTR:CF_MSTR + 128] = (i[:, None] > i[None, :])
    cf[:, CF_MINCT:CF_MINCT + 128] = (i[None, :] >= i[:, None])
    j = np.arange(64)
    same = (j[:, None] // SL) == (j[None, :] // SL)
    cf[:64, CF_TRIS:CF_TRIS + 64] = same & (j[:, None] <= j[None, :])
    cf[:64, CF_SEGS:CF_SEGS + 64] = same
    cf[:64, CF_MSTRS:CF_MSTRS + 64] = same & (j[:, None] > j[None, :])
    cf[:64, CF_MINCTS:CF_MINCTS + 64] = same & (j[None, :] >= j[:, None])
    cf[:64, CF_SIND:CF_SIND + NSB] = (j[:, None] // SL) == np.arange(NSB)[None, :]
    cb = np.zeros((128, NCB), np.float32)
    cb[:, CB_ID:CB_ID + 128] = np.eye(128)
    cb[:, CB_ONES:CB_ONES + 128] = 1.0
    for li, s in enumerate(NLEV):
        I = i[:, None] // s
        J = i[None, :] // s
        ml = ((I % 2 == 1) & (J == I - 1)).astype(np.float32)
        cb[:, CB_NML + li * 128:CB_NML + (li + 1) * 128] = -ml
        cb[:, CB_NMU + li * 128:CB_NMU + (li + 1) * 128] = -ml.T
    seqm = (np.arange(NSB)[:, None] == (j[None, :] // SL)).astype(np.float32).reshape(1, NSB * 64)
    cb[:, CB_SEQM:CB_SEQM + 1024] = seqm
    return cf, cb


def _chunkcol(v, n):
    return np.moveaxis(v.reshape(v.shape[:-1] + (n, 128)), -1, 0)


_NC_CACHE = {}


def _get_nc(dbg=False):
    key = ("full", dbg)
    if key not in _NC_CACHE:
        _NC_CACHE[key] = Kern(dbg=dbg)
    return _NC_CACHE[key]


def _prep_shared(inp):
    f = lambda k: np.ascontiguousarray(np.asarray(inp[k], dtype=np.float32))
    pp = np.zeros((128, 2 * PPL), np.float32)
    rp = np.zeros((2, 3, 1024), np.float32)
    gr = np.zeros((2, 2, 4), np.float32)
    bbd = np.zeros((2, 2, 8, 128, 128), np.float32)
    cbd = np.zeros((2, 2, 8, 128, 128), np.float32)
    wax = np.zeros((2, 2, 2, 128, 128), np.float32)
    norms = [f("norm_mix_pre"), f("norm_mix_post"), f("norm_ffn_pre"), f("norm_ffn_post")]
    for l in range(2):
        o = l * PPL
        for k in range(4):
            pp[:, o + k * 8:o + k * 8 + 8] = _chunkcol(norms[k][l], 8)
        lre = f("s5_lambda_re")[l].reshape(1024)
        lim = f("s5_lambda_im")[l].reshape(1024)
        ldt = np.repeat(f("s5_log_dt")[l], 64)
        pp[:, o + 32:o + 40] = _chunkcol(lre, 8)
        pp[:, o + 40:o + 48] = _chunkcol(lim, 8)
        pp[:, o + 48:o + 56] = _chunkcol(ldt, 8)
        rp[l, 0], rp[l, 1], rp[l, 2] = lre, lim, ldt
        pp[:, o + 56:o + 58] = _chunkcol(f("s5_d")[l], 2)
        pp[:, o + 58:o + 60] = _chunkcol(f("s5_b_glu")[l], 2)
        cw = _chunkcol(f("lru_conv_w")[l], 2)
        pp[:, o + 60:o + 68] = np.transpose(cw, (0, 2, 1)).reshape(128, 8)
        gr[l, 0] = f("gdn_a_log")[l]
        gr[l, 1] = f("gdn_dt_bias")[l]
        bre, bim = f("s5_b_re")[l], f("s5_b_im")[l]
        cre, cim = f("s5_c_re")[l], f("s5_c_im")[l]
        for grp in range(16):
            sc, half = grp // 2, grp % 2
            rows = slice(32 * (sc % 4) + 16 * half, 32 * (sc % 4) + 16 * half + 16)
            colsS = slice(64 * half, 64 * half + 64)
            bbd[l, 0, sc, rows, colsS] = bre[grp].T
            bbd[l, 1, sc, rows, colsS] = bim[grp].T
            cbd[l, 0, sc, colsS, rows] = cre[grp].T
            cbd[l, 1, sc, colsS, rows] = cim[grp].T
        wa, wx = f("lru_w_a")[l], f("lru_w_x")[l]
        for hb in range(8):
            c, r0 = hb // 4, 32 * (hb % 4)
            wax[l, 0, c, r0:r0 + 32, r0:r0 + 32] = wa[hb]
            wax[l, 1, c, r0:r0 + 32, r0:r0 + 32] = wx[hb]
    for l in range(2):
        o = l * PPL
        pp[:, o + 68:o + 70] = _chunkcol(f("lru_conv_b")[l], 2)
        pp[:, o + 70:o + 72] = _chunkcol(f("lru_b_a")[l], 2)
        pp[:, o + 72:o + 74] = _chunkcol(f("lru_b_x")[l], 2)
        gw = _chunkcol(f("gdn_conv_w")[l], 12)
        pp[:, o + 76:o + 124] = np.transpose(gw, (0, 2, 1)).reshape(128, 48)
        pp[:, o + 124] = f("gdn_norm_w")[l]
        pp[:, o + 125:o + 127] = _chunkcol(f("lru_lambda")[l], 2)
    cf, cb = _consts()
    return dict(w_in=f("w_in"), w_out=f("w_out"), w_gu=f("ffn_w_gate_up"), w_dn=f("ffn_w_down"), pp=pp, rp=rp, gr=gr,
                cf=cf, cb=cb, bbd=bbd, cbd=cbd, wglu=f("s5_w_glu"), wax=wax)


def _prep_core(inp, c):
    f = lambda k: np.asarray(inp[k], dtype=np.float32)
    bs = slice(c * NSB, (c + 1) * NSB)
    d = {}
    d["xp"] = np.ascontiguousarray(f("x_prompt")[c].T)
    d["xs"] = np.ascontiguousarray(f("x_sample")[bs].reshape(SN, D_MODEL).T)
    s5 = np.stack([f("state_s5_re")[:, bs], f("state_s5_im")[:, bs]], axis=1).reshape(2, 2, NSB, 8, 128)
    d["i_s5"] = np.ascontiguousarray(np.transpose(s5, (0, 1, 4, 3, 2)))
    d["i_lruh"] = np.ascontiguousarray(np.transpose(f("state_lru_h")[:, bs].reshape(2, NSB, 2, 128), (0, 3, 2, 1)))
    d["i_lruconv"] = np.ascontiguousarray(np.transpose(f("state_lru_conv")[:, bs].reshape(2, NSB, 3, 2, 128), (0, 4, 3, 1, 2)))
    d["i_gdnconv"] = np.ascontiguousarray(np.transpose(f("state_gdn_conv")[:, bs].reshape(2, NSB, 3, 12, 128), (0, 4, 3, 1, 2)))
    d["i_gdnS"] = np.ascontiguousarray(f("state_gdn_S")[:, bs])
    return d


def _assemble(results):
    n = len(results)
    yp = np.stack([r["yp"].T for r in results])
    ys = np.concatenate([r["ys"].T.reshape(NSB, SL, D_MODEL) for r in results])
    p_s5 = np.stack([np.transpose(r["o_s5"], (0, 1, 3, 2)).reshape(2, 2, 16, 64) for r in results], axis=2)
    p_lruh = np.stack([np.transpose(r["o_lruh"], (0, 2, 1)).reshape(2, 256) for r in results], axis=1)
    p_lruconv = np.stack([np.transpose(r["o_lruconv"], (0, 3, 2, 1)).reshape(2, 3, 256) for r in results], axis=1)
    p_gdnS = np.stack([r["o_gdnS"] for r in results], axis=1)
    p_gdnconv = np.stack([np.transpose(r["o_gdnconv"], (0, 3, 2, 1)).reshape(2, 3, 1536) for r in results], axis=1)
    s_s5 = np.concatenate([np.transpose(r["os_s5"], (0, 1, 4, 3, 2)).reshape(2, 2, NSB, 16, 64) for r in results], axis=2)
    s_lruh = np.concatenate([np.transpose(r["os_lruh"], (0, 3, 2, 1)).reshape(2, NSB, 256) for r in results], axis=1)
    s_lruconv = np.concatenate([np.transpose(r["os_lruconv"], (0, 3, 4, 2, 1)).reshape(2, NSB, 3, 256) for r in results], axis=1)
    s_gdnS = np.concatenate([r["os_gdnS"] for r in results], axis=1)
    s_gdnconv = np.concatenate([np.transpose(r["os_gdnconv"], (0, 3, 4, 2, 1)).reshape(2, NSB, 3, 1536) for r in results], axis=1)
    outs = (yp, ys, p_s5[:, 0], p_s5[:, 1], p_lruh, p_lruconv, p_gdnS, p_gdnconv,
            s_s5[:, 0], s_s5[:, 1], s_lruh, s_lruconv, s_gdnS, s_gdnconv)
    return tuple(np.ascontiguousarray(o, dtype=np.float32) for o in outs)


def kernel(**inputs):
    k = _get_nc()
    shared = _prep_shared(inputs)
    in_maps = []
    for c in range(8):
        d = dict(shared)
        d.update(_prep_core(inputs, c))
        in_maps.append(d)
    res = run_bass_kernel_spmd(k.nc, in_maps, core_ids=list(range(8)))
    return _assemble(res.results)
```
